# Optimizing a Trainium2 kernel written in Bass

```python
import jax, jax.numpy as jnp
from jax import lax
import numpy as np

D_MODEL = 2048
BATCH = 16
SEQ = 2048
DEPTH = 4

CTX_LEN = 256
GRID_W = 64
HEAD_DIM = 128
D_FOURIER = D_MODEL // 4
FOURIER_GROUP_DIM = 128
N_FOURIER_GROUPS = D_FOURIER // FOURIER_GROUP_DIM
N_Q_HEADS = (D_MODEL - D_FOURIER) // HEAD_DIM
N_KV_HEADS = N_Q_HEADS // 3
GQA_GROUP = N_Q_HEADS // N_KV_HEADS
D_Q = N_Q_HEADS * HEAD_DIM
D_KV = N_KV_HEADS * HEAD_DIM
D_IN_PROJ = D_FOURIER + D_Q + 2 * D_KV
D_MIX = D_FOURIER + D_Q
WINDOW = 128
BLOCK = 128
KEY_SPAN = BLOCK + 2 * WINDOW
D_FF = ((8 * D_MODEL // 3 + 255) // 256) * 256
CONV_WIDTH = 3
ROPE_BASE = 10000.0
ROPE_AXIS_DIM = HEAD_DIM // 2
N_MOD = 6
LN_EPS = 1e-5
DEEPNORM_ALPHA = (2 * DEPTH) ** 0.25
DEEPNORM_BETA = (8 * DEPTH) ** -0.25
NEG_INF = -1e30
ATTN_SCALE = HEAD_DIM ** -0.5

kernel_name = "hybrid_fourier_window_gqa_convffn_deepnorm"


def _layernorm(x, g, b):
    xf = x.astype(jnp.float32)
    mu = jnp.mean(xf, axis=-1, keepdims=True)
    var = jnp.mean(jnp.square(xf - mu), axis=-1, keepdims=True)
    y = (xf - mu) * lax.rsqrt(var + LN_EPS) * g.astype(jnp.float32) + b.astype(jnp.float32)
    return y.astype(x.dtype)


def _axial_rope_angles(n_tokens):
    rows = n_tokens // GRID_W
    row = jnp.repeat(jnp.arange(rows, dtype=jnp.float32), GRID_W)
    col = jnp.tile(jnp.arange(GRID_W, dtype=jnp.float32), rows)
    inv = ROPE_BASE ** (-jnp.arange(0, ROPE_AXIS_DIM, 2, dtype=jnp.float32) / ROPE_AXIS_DIM)
    return row[:, None] * inv, col[:, None] * inv


def _rotate(x, ang):
    cos = jnp.cos(ang)[:, None, :].astype(x.dtype)
    sin = jnp.sin(ang)[:, None, :].astype(x.dtype)
    x1, x2 = jnp.split(x, 2, axis=-1)
    return jnp.concatenate([x1 * cos - x2 * sin, x2 * cos + x1 * sin], axis=-1)


def _apply_axial_rope(x, ang_row, ang_col):
    xr, xc = jnp.split(x, 2, axis=-1)
    return jnp.concatenate([_rotate(xr, ang_row), _rotate(xc, ang_col)], axis=-1)


def _fourier_mix(f, w_four):
    bsz, t = f.shape[0], f.shape[1]
    z = f.astype(jnp.float32).reshape(bsz, t, N_FOURIER_GROUPS, FOURIER_GROUP_DIM)
    y = jnp.fft.fft2(z, axes=(1, 3), norm="ortho").real
    return y.reshape(bsz, t, D_FOURIER).astype(f.dtype) @ w_four


def _sink_column(sink, lead_shape):
    s = sink.astype(jnp.float32).reshape(N_KV_HEADS, GQA_GROUP)
    return jnp.broadcast_to(s[None, :, :, None, None], lead_shape + (1,))


def _latent_window_attention(q, k, v, kc, vc, sink):
    bsz, t = q.shape[0], q.shape[1]
    nb = t // BLOCK
    kp = jnp.pad(k, ((0, 0), (WINDOW, WINDOW), (0, 0), (0, 0)))
    vp = jnp.pad(v, ((0, 0), (WINDOW, WINDOW), (0, 0), (0, 0)))
    qb = jnp.moveaxis(q.reshape(bsz, nb, BLOCK, N_KV_HEADS, GQA_GROUP, HEAD_DIM), 1, 0)
    qi = jnp.arange(BLOCK)[:, None]
    kj = jnp.arange(KEY_SPAN)[None, :]
    band = jnp.abs(kj - WINDOW - qi) <= WINDOW

    def one_block(args):
        qblk, b = args
        kblk = lax.dynamic_slice_in_dim(kp, b * BLOCK, KEY_SPAN, axis=1)
        vblk = lax.dynamic_slice_in_dim(vp, b * BLOCK, KEY_SPAN, axis=1)
        kpos = b * BLOCK + kj - WINDOW
        valid = band & (kpos >= 0) & (kpos < t)
        s_loc = jnp.einsum('bqhgd,bkhd->bhgqk', qblk, kblk).astype(jnp.float32) * ATTN_SCALE
        s_loc = jnp.where(valid, s_loc, NEG_INF)
        s_ctx = jnp.einsum('bqhgd,bchd->bhgqc', qblk, kc).astype(jnp.float32) * ATTN_SCALE
        s = jnp.concatenate([s_loc, s_ctx, _sink_column(sink, s_loc.shape[:-1])], axis=-1)
        p = jax.nn.softmax(s, axis=-1)
        p_loc = p[..., :KEY_SPAN].astype(v.dtype)
        p_ctx = p[..., KEY_SPAN:KEY_SPAN + kc.shape[1]].astype(v.dtype)
        return (jnp.einsum('bhgqk,bkhd->bqhgd', p_loc, vblk)
                + jnp.einsum('bhgqc,bchd->bqhgd', p_ctx, vc))

    out = lax.map(one_block, (qb, jnp.arange(nb)))
    return jnp.moveaxis(out, 0, 1).reshape(bsz, t, D_Q)


def _context_attention(qc, kc, vc, sink):
    bsz, n = qc.shape[0], qc.shape[1]
    s = jnp.einsum('bqhgd,bkhd->bhgqk', qc, kc).astype(jnp.float32) * ATTN_SCALE
    s = jnp.concatenate([s, _sink_column(sink, s.shape[:-1])], axis=-1)
    p = jax.nn.softmax(s, axis=-1)[..., :n].astype(vc.dtype)
    return jnp.einsum('bhgqk,bkhd->bqhgd', p, vc).reshape(bsz, n, D_Q)


def _split_proj(h, w_in):
    bsz, t = h.shape[0], h.shape[1]
    p = h @ w_in
    f, q, k, v = jnp.split(p, [D_FOURIER, D_FOURIER + D_Q, D_FOURIER + D_Q + D_KV], axis=-1)
    return (f, q.reshape(bsz, t, N_Q_HEADS, HEAD_DIM),
            k.reshape(bsz, t, N_KV_HEADS, HEAD_DIM), v.reshape(bsz, t, N_KV_HEADS, HEAD_DIM))


def _group_q(q):
    return q.reshape(q.shape[0], q.shape[1], N_KV_HEADS, GQA_GROUP, HEAD_DIM)


def _conv_ffn(h, w_up, conv_w, conv_b, w_down):
    u = h @ w_up
    u = lax.conv_general_dilated(
        u, conv_w[:, None, :].astype(u.dtype), window_strides=(1,), padding='SAME',
        dimension_numbers=('NWC', 'WIO', 'NWC'), feature_group_count=2 * D_FF) + conv_b
    a, g = jnp.split(u, 2, axis=-1)
    return (jax.nn.silu(g) * a) @ w_down


def setup_inputs(seed: int = 0) -> dict:
    key = jax.random.key(seed)
    ks = jax.random.split(key, 16)
    f32 = jnp.float32
    nrm = lambda k, shape, s: jax.random.normal(k, shape, f32) * s
    return {
        "x": nrm(ks[0], (BATCH, SEQ, D_MODEL), 1.0),
        "c": nrm(ks[1], (BATCH, D_MODEL), 1.0),
        "ctx": nrm(ks[2], (BATCH, CTX_LEN, D_MODEL), 1.0),
        "c_ctx": nrm(ks[3], (D_MODEL,), 1.0),
        "w_ada": nrm(ks[4], (DEPTH, D_MODEL, N_MOD * D_MODEL), 0.5 * D_MODEL ** -0.5),
        "b_ada": nrm(ks[5], (DEPTH, N_MOD * D_MODEL), 0.02),
        "w_in": nrm(ks[6], (DEPTH, D_MODEL, D_IN_PROJ), D_MODEL ** -0.5),
        "sink": nrm(ks[7], (DEPTH, N_Q_HEADS), 0.5),
        "w_four": nrm(ks[8], (DEPTH, D_FOURIER, D_FOURIER), D_FOURIER ** -0.5),
        "w_out": nrm(ks[9], (DEPTH, D_MIX, D_MODEL), DEEPNORM_BETA * D_MIX ** -0.5),
        "ln_g": 1.0 + nrm(ks[10], (DEPTH, 2, D_MODEL), 0.02),
        "ln_b": nrm(ks[11], (DEPTH, 2, D_MODEL), 0.02),
        "w_up": nrm(ks[12], (DEPTH, D_MODEL, 2 * D_FF), D_MODEL ** -0.5),
        "conv_w": nrm(ks[13], (DEPTH, CONV_WIDTH, 2 * D_FF), CONV_WIDTH ** -0.5),
        "conv_b": nrm(ks[14], (DEPTH, 2 * D_FF), 0.02),
        "w_down": nrm(ks[15], (DEPTH, D_FF, D_MODEL), DEEPNORM_BETA * D_FF ** -0.5),
    }


def reference(x, c, ctx, c_ctx, w_ada, b_ada, w_in, sink, w_four, w_out, ln_g, ln_b,
              w_up, conv_w, conv_b, w_down):
    ang_row, ang_col = _axial_rope_angles(x.shape[1])
    silu_c = jax.nn.silu(c)
    silu_cc = jax.nn.silu(c_ctx)
    for l in range(DEPTH):
        last = l == DEPTH - 1
        m = (silu_c @ w_ada[l] + b_ada[l])[:, None, :]
        sh_a, sc_a, g_a, sh_f, sc_f, g_f = jnp.split(m, N_MOD, axis=-1)
        mc = silu_cc @ w_ada[l] + b_ada[l]
        shc_a, scc_a, gc_a, shc_f, scc_f, gc_f = jnp.split(mc, N_MOD, axis=-1)

        h = x * (1.0 + sc_a) + sh_a
        hc = ctx * (1.0 + scc_a) + shc_a
        f, q, k, v = _split_proj(h, w_in[l])
        q = _group_q(_apply_axial_rope(q, ang_row, ang_col))
        k = _apply_axial_rope(k, ang_row, ang_col)
        if last:
            kvc = hc @ w_in[l][:, D_FOURIER + D_Q:]
            kc, vc = jnp.split(kvc, 2, axis=-1)
            kc = kc.reshape(hc.shape[0], hc.shape[1], N_KV_HEADS, HEAD_DIM)
            vc = vc.reshape(hc.shape[0], hc.shape[1], N_KV_HEADS, HEAD_DIM)
        else:
            fc, qc, kc, vc = _split_proj(hc, w_in[l])
        y = jnp.concatenate([_fourier_mix(f, w_four[l]),
                             _latent_window_attention(q, k, v, kc, vc, sink[l])], axis=-1) @ w_out[l]
        x = _layernorm(DEEPNORM_ALPHA * x + g_a * y, ln_g[l, 0], ln_b[l, 0])
        if not last:
            yc = jnp.concatenate([_fourier_mix(fc, w_four[l]),
                                  _context_attention(_group_q(qc), kc, vc, sink[l])], axis=-1) @ w_out[l]
            ctx = _layernorm(DEEPNORM_ALPHA * ctx + gc_a * yc, ln_g[l, 0], ln_b[l, 0])

        h = x * (1.0 + sc_f) + sh_f
        x = _layernorm(DEEPNORM_ALPHA * x + g_f * _conv_ffn(h, w_up[l], conv_w[l], conv_b[l], w_down[l]),
                       ln_g[l, 1], ln_b[l, 1])
        if not last:
            hc = ctx * (1.0 + scc_f) + shc_f
            ctx = _layernorm(DEEPNORM_ALPHA * ctx + gc_f * _conv_ffn(hc, w_up[l], conv_w[l], conv_b[l], w_down[l]),
                             ln_g[l, 1], ln_b[l, 1])
    return x
```

```python
import contextlib
import os
import numpy as np
import ml_dtypes
import concourse.bass as bass
import concourse.mybir as mybir
from concourse.bass_utils import run_bass_kernel_spmd

F32 = mybir.dt.float32
BF16 = mybir.dt.bfloat16
AF = mybir.ActivationFunctionType
ALU = mybir.AluOpType

D = 2048
KC = 16
T = 2048
CL = 256
NT = T + CL
NTT = NT // 128
DFF = 5632
NJ = DFF // 128
L_ALL = 4
NQ = 12
NKV = 4
ALPHA = (2 * L_ALL) ** 0.25
LN_EPS = 1e-5
EPSP = LN_EPS / (ALPHA * ALPHA)
ATTN_SCALE = 128 ** -0.5
NEG = -30000.0
TGS = [(0, 512), (512, 512), (1024, 512), (1536, 512), (2048, 256)]
SLOT = 8192
NSLOT = 3
ARENA_BYTES = 136 * 1024
SAME_ENGINE_SYNC = True


class DmaSem:
    __slots__ = ("sem", "count", "id")

    def __init__(self, sem, i):
        self.sem = sem
        self.count = 0
        self.id = i


class Buf:
    __slots__ = ("name", "w", "r", "ds", "psum")

    def __init__(self, name, ds=None, psum=False):
        self.psum = psum
        self.name = name
        self.w = []
        self.r = []
        self.ds = ds


class Op:
    __slots__ = ("eng", "ins", "cdeps", "ddeps", "ms", "need", "ds", "dval", "idx")


COMPUTE = ("pe", "act", "dve", "pool")


class Prog:
    def __init__(self):
        self.ops = []
        self.last = {}
        self.lastc = {}
        self.phase_dma = []
        self.pending_barrier = {}

    def op(self, eng, ins, R=(), W=(), dma=None):
        o = Op()
        o.eng = eng
        o.ins = ins
        o.ms = 0
        o.need = False
        o.idx = len(self.ops)
        o.ds = None
        o.dval = 0
        cd = {}
        dd = {}

        def add(p):
            if p.ds is not None:
                k = p.ds.id
                if k not in dd or dd[k][1] < p.dval:
                    dd[k] = (p.ds, p.dval)
            else:
                q = cd.get(p.eng)
                if q is None or q.idx < p.idx:
                    cd[p.eng] = p

        pb = self.pending_barrier.pop(eng, None)
        if pb is not None:
            for p in pb:
                add(p)
        for b in R:
            for p in b.w:
                add(p)
            if b.psum:
                for p in b.r:
                    if p.eng != eng:
                        add(p)
        for b in W:
            if b.r or (b in R):
                for p in b.r:
                    add(p)
                for p in b.w:
                    add(p)
                b.r = []
                b.w = [o]
            else:
                for p in b.w:
                    if p.eng != eng:
                        add(p)
                b.w.append(o)
        for b in R:
            if b not in W:
                b.r.append(o)
        if dma is not None:
            ds = dma.ds
            ds.count += 16
            o.ds = ds
            o.dval = ds.count
            if eng != "pool":
                self.phase_dma.append(o)
        if eng in cd:
            if eng == "pe" or not SAME_ENGINE_SYNC or eng in ("sp", "pool"):
                del cd[eng]
        o.cdeps = list(cd.values())
        o.ddeps = list(dd.values())
        self.ops.append(o)
        self.last[eng] = o
        if dma is None and ins[0] != "nop":
            self.lastc[eng] = o
        return o

    def barrier(self):
        prev = [self.lastc[e] for e in COMPUTE if e in self.lastc] + self.phase_dma
        self.phase_dma = []
        for e in ("pe", "act", "dve", "sp"):
            cur = self.pending_barrier.get(e, [])
            self.pending_barrier[e] = cur + prev

    def finalize(self, nc, engsem):
        for o in self.ops:
            for p in o.cdeps:
                p.need = True
        cnt = {}
        for o in self.ops:
            if o.need:
                cnt[o.eng] = cnt.get(o.eng, 0) + 1
                o.ms = cnt[o.eng]

    def emit(self, eng, e, engsem):
        waited = {}
        n = 0
        for o in self.ops:
            if o.eng != eng:
                continue
            for p in o.cdeps:
                key = ("c", p.eng)
                if waited.get(key, 0) < p.ms:
                    e.wait_ge(engsem[p.eng], p.ms)
                    waited[key] = p.ms
            for (ds, val) in o.ddeps:
                key = ("d", ds.id)
                if waited.get(key, 0) < val:
                    e.wait_ge(ds.sem, val)
                    waited[key] = val
            ins = o.ins
            k = ins[0]
            r = None
            if k == "mm":
                r = e.matmul(ins[1], ins[2], ins[3], start=ins[4], stop=ins[5])
            elif k == "act":
                r = e.activation(out=ins[1], in_=ins[2], func=ins[3], bias=ins[4], scale=ins[5])
            elif k == "tt":
                r = e.tensor_tensor(out=ins[1], in0=ins[2], in1=ins[3], op=ins[4])
            elif k == "ts":
                r = e.tensor_scalar(ins[1], ins[2], ins[3], ins[4], ins[5], ins[6])
            elif k == "stt":
                r = e.scalar_tensor_tensor(ins[1], ins[2], ins[3], ins[4], ins[5], ins[6])
            elif k == "copy":
                r = e.tensor_copy(out=ins[1], in_=ins[2])
            elif k == "recip":
                r = e.reciprocal(ins[1], ins[2])
            elif k == "memset":
                r = e.memset(ins[1], ins[2])
            elif k == "dma":
                r = e.dma_start(out=ins[1], in_=ins[2])
            elif k == "nop":
                r = None
            else:
                raise ValueError(k)
            if r is not None:
                if o.ds is not None:
                    r.then_inc(o.ds.sem, 16)
                elif o.need:
                    r.then_inc(engsem[eng], 1)
            n += 1
        return n


class Deferred:
    def __init__(self):
        self.q = []

    def add(self, delay, fn):
        self.q.append([delay, fn])

    def tick(self):
        for it in self.q:
            it[0] -= 1
        while self.q and self.q[0][0] <= 0:
            self.q.pop(0)[1]()

    def flush(self):
        while self.q:
            self.q.pop(0)[1]()


class Arena:
    def __init__(self, ap_f32, nbytes):
        self.ap = ap_f32
        self.nbytes = nbytes
        self.off = 0

    def reset(self):
        self.off = 0

    def alloc(self, free_shape, dtype):
        esz = 4 if dtype == F32 else 2
        n = int(np.prod(free_shape))
        nb = (n * esz + 63) // 64 * 64
        assert self.off + nb <= self.nbytes, ("arena overflow", self.off, nb)
        v = self.ap[:, self.off // 4:(self.off + nb) // 4]
        self.off += nb
        if dtype != F32:
            v = v.bitcast(dtype)
        v = v[:, 0:n]
        if len(free_shape) == 2:
            v = v.rearrange("p (a b) -> p a b", b=free_shape[1])
        elif len(free_shape) == 3:
            v = v.rearrange("p (a b c) -> p a b c", b=free_shape[1], c=free_shape[2])
        elif len(free_shape) == 4:
            v = v.rearrange("p (a b c d) -> p a b c d", b=free_shape[1], c=free_shape[2], d=free_shape[3])
        return v


def build_nc(nlayers=L_ALL, nb=2, dump=(), stop_after=None):
    nc = bass.Bass("TRN2", target_bir_lowering=False)
    P = Prog()
    last_layer_idx = L_ALL - 1

    def din(name, shape, dt=F32):
        return nc.dram_tensor(name, list(shape), dt, kind="ExternalInput").ap()

    def dscr(name, shape, dt):
        kind = "ExternalOutput" if name in dump else "Internal"
        return nc.dram_tensor(name, list(shape), dt, kind=kind).ap()

    x0T = din("x0T", [nb, KC, 128, NT])
    cT = din("cT", [128, KC, 3])
    w_ada = din("w_ada_t", [nlayers, 24, 128, SLOT])
    b_ada = din("b_ada_t", [128, nlayers, 96])
    w_in = din("w_in_t", [nlayers, 6, 128, SLOT])
    w_four = din("w_four_t", [nlayers, 128, 2048])
    w_out = din("w_out_t", [nlayers, 4, 128, SLOT])
    w_up = din("w_up_t", [nlayers, 22, 128, SLOT])
    w_down = din("w_down_t", [nlayers, 16, 128, NJ * 128])
    conv_in = din("conv_t", [128, nlayers, 4, 88])
    ln_in = din("ln_t", [128, nlayers, 2, 2, KC])
    sink_in = din("sink_t", [128, nlayers * NQ])
    rope_in = din("rope_t", [128, 2, T])
    cbf_in = din("cbf_t", [128, 2176], BF16)
    dft_in = din("dft_t", [8, 128, SLOT], BF16)
    dftc_in = din("dftc_t", [128, 1024], BF16)
    outT = nc.dram_tensor("outT", [nb, KC, 128, T], F32, kind="ExternalOutput").ap()
    XA = dscr("XA", [KC, 128, NT], F32)
    XB = dscr("XB", [KC, 128, NT], F32)
    H1 = dscr("H1", [KC, 128, NT], BF16)
    H2 = dscr("H2", [KC, 128, NT], BF16)
    H1B = dscr("H1B", [KC, 128, NT], BF16)
    QT = dscr("QT", [NQ, 128, NT], BF16)
    KT = dscr("KT", [NKV, 128, NT], BF16)
    VV = dscr("VV", [NTT, 128, 512], BF16)
    FT = dscr("FT", [4, 128, NT], BF16)
    MIX = dscr("MIX", [KC, 128, NT], BF16)
    RR = dscr("RR", [NJ, 128, NT], BF16)
    MODD = dscr("MODD", [128, nlayers * 96 * 3], F32)
    WD16 = dscr("WD16", [nlayers, 2, 8, 128, 5632], BF16)
    WO16 = dscr("WO16", [nlayers, 4, 128, SLOT], BF16)

    def sb(name, shape, dt):
        return nc.alloc_sbuf_tensor(name, list(shape), dt)

    arena_t = sb("arena", [128, ARENA_BYTES // 4], F32)
    ring_t = [sb(f"ring{i}", [128, SLOT], BF16) for i in range(NSLOT)]
    cbf = sb("cbf", [128, 2176], BF16)
    MOD = sb("mod", [128, nlayers, 96, 3], F32)
    FS = sb("fs", [128, nlayers, 2, KC, 3], F32)
    FB = sb("fb", [128, nlayers, 2, KC, 3], F32)
    LNP = sb("lnp", [128, nlayers, 2, 2, KC], F32)
    CONV = sb("conv", [128, nlayers, 4, 88], F32)
    BADA = sb("bada", [128, nlayers, 96], F32)
    ES = sb("es", [128, nlayers * NQ], F32)
    CIN = sb("cin", [128, KC, 3], F32)
    SCB = sb("scb", [128, KC, 3], BF16)
    EPS = sb("eps", [128, 1], F32)
    psum = [nc.alloc_psum_tensor(f"ps{i}", [128, 512], F32) for i in range(8)]

    IDENT = cbf[:, 0:128]
    ONES = cbf[:, 128:256]
    ONESD = cbf[:, 256:384]
    PSWAP = cbf[:, 384:512]
    MASKLO = cbf[:, 512:896]
    MASKUP = cbf[:, 896:1280]
    C128S = cbf[:, 1280:1536]

    A = Arena(arena_t[:], ARENA_BYTES)

    sem_handles = []

    sem_stack = contextlib.ExitStack()

    def new_sem(name):
        h = sem_stack.enter_context(nc.semaphore(name))
        sem_handles.append(h)
        return h

    engsem = {e: new_sem("s_" + e) for e in ("pe", "act", "dve", "pool")}
    dsem_pool = [DmaSem(new_sem(f"d{i}"), i) for i in range(60)]
    dsem_free = list(dsem_pool)
    phase_sems = []

    def dbuf(name, persistent=False):
        ds = dsem_free.pop()
        if not persistent:
            phase_sems.append(ds)
        return Buf(name, ds)

    def end_phase():
        P.barrier()
        dsem_free.extend(phase_sems)
        phase_sems.clear()
        A.reset()

    PS = [Buf(f"ps{i}", psum=True) for i in range(8)]
    RING = [dbuf(f"ring{i}", persistent=True) for i in range(NSLOT)]
    ring_ctr = [0]
    CONSTB = dbuf("consts", persistent=True)
    GLOB = Buf("glob")

    PRE = [dbuf(f"pre{i}", persistent=True) for i in range(nlayers)]

    def precast_steps(l):
        steps = []
        for G in range(4):
            steps.append(lambda G=G: P.op("pool", ("dma", WO16[l, G].rearrange("p (a b) -> p a b", b=2048),
                                                   w_out[l, G].rearrange("p (a b) -> p a b", b=2048)),
                                          R=(), W=(PRE[l],), dma=PRE[l]))
        for m in range(KC):
            def f(m=m):
                src = w_down[l, m].rearrange("p (h a b) -> p h a b", h=2, b=1408)
                dst = WD16[l, :, m // 2, :, (m % 2) * 2816:(m % 2 + 1) * 2816].rearrange("h p (a b) -> p h a b", b=1408)
                P.op("pool", ("dma", dst, src), R=(), W=(PRE[l],), dma=PRE[l])
            steps.append(f)
        return steps

    pre_pending = {}

    def wload(src_ap, nelem, extra_R=()):
        i = ring_ctr[0] % NSLOT
        ring_ctr[0] += 1
        dst_ap = ring_t[i][:, 0:nelem]
        if nelem > 2048:
            bb = 2048 if nelem % 2048 == 0 else 1408
            dst_ap = dst_ap.rearrange("p (a b) -> p a b", b=bb)
            src_ap = src_ap.rearrange("p (a b) -> p a b", b=bb)
        P.op("pool", ("dma", dst_ap, src_ap), R=extra_R, W=(RING[i],), dma=RING[i])
        return ring_t[i], RING[i]

    def mm(out, lhsT, rhs, start, stop, R, W):
        P.op("pe", ("mm", out, lhsT, rhs, start, stop), R=R, W=W)

    def act(out, in_, func, R, W, bias=0.0, scale=1.0):
        P.op("act", ("act", out, in_, func, bias, scale), R=R, W=W)

    def load(dst, src, buf):
        P.op("sp", ("dma", dst, src), R=(), W=(buf,), dma=buf)

    def store(dst, src, buf):
        P.op("sp", ("dma", dst, src), R=(buf,), W=(), dma=buf)

    load(cbf[:], cbf_in[:], CONSTB)
    load(CIN[:], cT[:], CONSTB)
    load(BADA[:], b_ada[:], CONSTB)
    load(LNP[:], ln_in[:], CONSTB)
    load(CONV[:], conv_in[:], CONSTB)
    load(ES[:], sink_in[:], CONSTB)
    P.op("dve", ("memset", EPS[:], EPSP), R=(), W=(GLOB,))
    act(ES[:], ES[:], AF.Exp, R=(CONSTB,), W=(CONSTB, GLOB))
    act(SCB[:], CIN[:], AF.Silu, R=(CONSTB,), W=(GLOB,))
    def ada_layer(l):
        pb = PS[l % 2]
        pt = psum[l % 2]
        for G in range(24):
            wt, wb = wload(w_ada[l, G], SLOT)
            wv = wt[:].rearrange("p (m k c) -> p m k c", m=4, k=KC)
            for mi in range(4):
                m = G * 4 + mi
                for k in range(KC):
                    mm(pt[:, m * 3:m * 3 + 3], wv[:, mi, k, :], SCB[:, k, :], k == 0, k == KC - 1,
                       R=(wb, GLOB), W=(pb,))
        for r in range(3):
            P.op("dve", ("tt", MOD[:, l, :, r], pt[:, 0:288].rearrange("p (m r) -> p m r", r=3)[:, :, r], BADA[:, l, :], ALU.add),
                 R=(pb, CONSTB), W=(GLOB,))
        for (lo, mul, add_) in ((16, 1.0, 1.0), (64, 1.0, 1.0), (32, 1.0 / ALPHA, 0.0), (80, 1.0 / ALPHA, 0.0)):
            P.op("dve", ("ts", MOD[:, l, lo:lo + 16, :], MOD[:, l, lo:lo + 16, :], mul, add_, ALU.mult, ALU.add),
                 R=(GLOB,), W=(GLOB,))
    def mod_ap(l, idx, k, r):
        return MOD[:, l, idx * 16 + k, r:r + 1]

    def mod0(b):
        dst = H1 if b == 0 else H1B
        for k in range(KC):
            i = k % 6
            load(XI[i], x0T[b][k], XIb[i])
            if k % 2 == 0:
                act(HOo[i][:, 0:T], XI[i][:, 0:T], AF.Identity, R=(XIb[i], GLOB), W=(HOb[i],),
                    bias=mod_ap(0, 0, k, b), scale=mod_ap(0, 1, k, b))
                act(HOo[i][:, T:NT], XI[i][:, T:NT], AF.Identity, R=(XIb[i], GLOB), W=(HOb[i],),
                    bias=mod_ap(0, 0, k, 2), scale=mod_ap(0, 1, k, 2))
            else:
                P.op("dve", ("ts", HOo[i][:, 0:T], XI[i][:, 0:T], mod_ap(0, 1, k, b), mod_ap(0, 0, k, b),
                             ALU.mult, ALU.add), R=(XIb[i], GLOB), W=(HOb[i],))
                P.op("dve", ("ts", HOo[i][:, T:NT], XI[i][:, T:NT], mod_ap(0, 1, k, 2), mod_ap(0, 0, k, 2),
                             ALU.mult, ALU.add), R=(XIb[i], GLOB), W=(HOb[i],))
            store(dst[k], HOo[i], HOb[i])

    ada_layer(0)
    XI = [A.alloc([NT], F32) for _ in range(6)]
    HOo = [A.alloc([NT], BF16) for _ in range(6)]
    XIb = [dbuf(f"xi{i}") for i in range(6)]
    HOb = [dbuf(f"ho{i}") for i in range(6)]
    for b_ in range(nb):
        mod0(b_)
    for l_ in range(1, nlayers):
        ada_layer(l_)
    for l in range(nlayers):
        for which in range(2):
            if which == 0:
                s_lo, b_lo, ls = 64, 48, l
            else:
                if l + 1 >= nlayers:
                    continue
                s_lo, b_lo, ls = 16, 0, l + 1
            for r in range(3):
                g_ap = LNP[:, l, which, 0, :]
                be_ap = LNP[:, l, which, 1, :]
                P.op("dve", ("tt", FS[:, l, which, :, r], MOD[:, ls, s_lo:s_lo + 16, r], g_ap, ALU.mult),
                     R=(GLOB, CONSTB), W=(GLOB,))
                P.op("dve", ("tt", FB[:, l, which, :, r], MOD[:, ls, s_lo:s_lo + 16, r], be_ap, ALU.mult),
                     R=(GLOB, CONSTB), W=(GLOB,))
                P.op("dve", ("tt", FB[:, l, which, :, r], FB[:, l, which, :, r], MOD[:, ls, b_lo:b_lo + 16, r], ALU.add),
                     R=(GLOB,), W=(GLOB,))
    if "MODD" in dump:
        MODb = dbuf("modd")
        P.op("sp", ("dma", MODD[:], MOD[:].rearrange("p l m r -> p (l m r)")), R=(GLOB, MODb), W=(), dma=MODb)
    end_phase()

    stopped = [False]

    def check_stop(l, b, ph):
        if stop_after is not None and (l, b, ph) == tuple(stop_after):
            stopped[0] = True
        return stopped[0]

    def ln_sched(DQ, Z, ZSQ, HO, STAT, w, l, which, r, zbuf, hbuf, sqbuf, stbuf, want_h, finish, delays, zk):
        pm, pq = psum[6], psum[7]
        MEAN = STAT[:, 0, 0:w]
        RSTD = STAT[:, 1, 0:w]
        two_pass = ZSQ is None

        def stepA():
            for hh in range(2):
                ks = slice(hh * 8, hh * 8 + 8)
                act(HO[:, ks, 0:w], Z[:, ks, 0:w], AF.Copy, R=[zbuf] + zk[ks], W=(hbuf,))
                if not two_pass:
                    act(ZSQ[:, ks, 0:w], Z[:, ks, 0:w], AF.Square, R=[zbuf] + zk[ks], W=(sqbuf,))

        def stepB():
            for k in range(KC):
                mm(pm[:, 0:w], ONESD, HO[:, k, 0:w], k == 0, k == KC - 1, R=(hbuf, CONSTB), W=(PS[6],))
            if not two_pass:
                for k in range(KC):
                    mm(pq[:, 0:w], ONESD, ZSQ[:, k, 0:w], k == 0, k == KC - 1, R=(sqbuf, CONSTB), W=(PS[7],))
            act(MEAN, pm[:, 0:w], AF.Copy, R=(PS[6],), W=(stbuf,))
            if two_pass:
                for hh in range(2):
                    ks = slice(hh * 8, hh * 8 + 8)
                    act(HO[:, ks, 0:w], Z[:, ks, 0:w], AF.Square, R=[zbuf] + zk[ks], W=(hbuf,))

        def stepC():
            if two_pass:
                for k in range(KC):
                    mm(pq[:, 0:w], ONESD, HO[:, k, 0:w], k == 0, k == KC - 1, R=(hbuf, CONSTB), W=(PS[7],))

        def stepD0():
            P.op("dve", ("tt", RSTD, MEAN, MEAN, ALU.mult), R=(stbuf,), W=(stbuf,))
            P.op("dve", ("tt", pq[:, 0:w], pq[:, 0:w], RSTD, ALU.subtract), R=(stbuf, PS[7]), W=(PS[7],))
            act(pq[:, 0:w], pq[:, 0:w], AF.Sqrt, R=(PS[7], GLOB), W=(PS[7],), bias=EPS[:, 0:1])
            P.op("dve", ("recip", pq[:, 0:w], pq[:, 0:w]), R=(PS[7],), W=(PS[7],))

        def stepDk(k0, k1):
            for k in range(k0, k1):
                P.op("dve", ("tt", Z[:, k, 0:w], Z[:, k, 0:w], pm[:, 0:w], ALU.subtract), R=(zk[k], PS[6]), W=(zk[k],))
                P.op("dve", ("tt", Z[:, k, 0:w], Z[:, k, 0:w], pq[:, 0:w], ALU.mult), R=(zk[k], PS[7]), W=(zk[k],))
                if want_h:
                    act(HO[:, k, 0:w], Z[:, k, 0:w], AF.Identity, R=(zk[k], GLOB), W=(hbuf,),
                        bias=FB[:, l, which, k, r:r + 1], scale=FS[:, l, which, k, r:r + 1])
                act(Z[:, k, 0:w], Z[:, k, 0:w], AF.Identity, R=(zk[k], CONSTB), W=(zk[k],),
                    bias=LNP[:, l, which, 1, k:k + 1], scale=LNP[:, l, which, 0, k:k + 1])
            if k1 == KC:
                finish()

        DQ.add(delays[0], stepA)
        DQ.add(delays[1], stepB)
        DQ.add(delays[2], stepC)
        DQ.add(delays[3], stepD0)
        kpt = delays[4]
        for i_, k0 in enumerate(range(0, KC, kpt)):
            DQ.add(delays[3] + 1 + i_, lambda k0=k0: stepDk(k0, k0 + kpt))

    if stop_after is not None and stop_after[2] == "ADA":
        stopped[0] = True
    for b in range(nb):
        if stopped[0]:
            break
        xsrc = x0T[b]
        for l in range(nlayers):
            if stopped[0]:
                break
            last = (l == last_layer_idx)
            tgs_q = TGS[:4] if last else TGS

            if b == 0:
                for f_ in pre_pending.pop(l, precast_steps(l) if l == 0 else []):
                    f_()
            HT = A.alloc([KC, NT], BF16)
            ROPE = A.alloc([2, T], F32)
            OUTC = [A.alloc([NT], BF16) for _ in range(2)]
            VST = [A.alloc([512], BF16) for _ in range(2)]
            XBb_t = [A.alloc([512], BF16) for _ in range(2)]
            T1 = [A.alloc([512], F32) for _ in range(2)]
            T2 = [A.alloc([512], F32) for _ in range(2)]
            HTb = [dbuf(f"ht{i}") for i in range(5)]
            ROPEb = dbuf("rope")
            OUTCb = [dbuf(f"outc{i}") for i in range(2)]
            VSTb = [dbuf(f"vst{i}") for i in range(2)]
            XBb = [Buf(f"xb{i}") for i in range(2)]
            T1b = [Buf(f"t1{i}") for i in range(2)]
            T2b = [Buf(f"t2{i}") for i in range(2)]
            for n_, (t0_, w_) in enumerate(TGS):
                for kh in range(2):
                    h1src = H1B if (l == 0 and b == 1) else H1
                    load(HT[:, kh * 8:kh * 8 + 8, t0_:t0_ + w_],
                         h1src[kh * 8:kh * 8 + 8, :, t0_:t0_ + w_].rearrange("k p t -> p k t"), HTb[n_])
                if n_ == 0:
                    load(ROPE[:], rope_in[:], ROPEb)
            wt, wb = wload(w_in[l, 0], SLOT)
            wv = wt[:].rearrange("p (k c) -> p k c", k=KC)
            for tt in range(NTT):
                bk = tt % 4
                for k in range(KC):
                    mm(psum[bk][:], HT[:, k, tt * 128:(tt + 1) * 128], wv[:, k, :], k == 0, k == KC - 1,
                       R=(wb, HTb[tt // 4]), W=(PS[bk],))
                i = tt % 2
                act(VST[i][:], psum[bk][:], AF.Copy, R=(PS[bk],), W=(VSTb[i],))
                store(VV[tt], VST[i], VSTb[i])
            tile_ctr = 0
            oc_ctr = 0
            pending = None

            def rope_tail(pd):
                (bk_, sw_, i_, oc_, t0_, w_, ocb_) = pd
                mm(psum[sw_][:, 0:w_], PSWAP, XBb_t[i_][:, 0:w_], True, True, R=(XBb[i_], CONSTB), W=(PS[sw_],))
                P.op("dve", ("tt", psum[bk_][:, 0:w_], psum[bk_][:, 0:w_], ROPE[:, 0, t0_:t0_ + w_], ALU.mult),
                     R=(PS[bk_], ROPEb, XBb[i_]), W=(PS[bk_],))
                P.op("dve", ("tt", T2[i_][:, 0:w_], psum[sw_][:, 0:w_], ROPE[:, 1, t0_:t0_ + w_], ALU.mult),
                     R=(PS[sw_], ROPEb), W=(T2b[i_],))
                P.op("dve", ("tt", oc_[:, t0_:t0_ + w_], psum[bk_][:, 0:w_], T2[i_][:, 0:w_], ALU.add),
                     R=(PS[bk_], T2b[i_]), W=(ocb_,))

            for G in range(1, 6):
                wt, wb = wload(w_in[l, G], SLOT)
                wv = wt[:].rearrange("p (k c) -> p k c", k=KC)
                for mi in range(4):
                    is_f = (G == 1)
                    is_k = (not is_f) and mi == 0
                    tgl = TGS if (is_k or not last) else TGS[:4]
                    oc = OUTC[oc_ctr % 2]
                    ocb = OUTCb[oc_ctr % 2]
                    oc_ctr += 1
                    for (t0, w) in tgl:
                        bk = tile_ctr % 4
                        for k in range(KC):
                            mm(psum[bk][:, 0:w], wv[:, k, mi * 128:(mi + 1) * 128], HT[:, k, t0:t0 + w],
                               k == 0, k == KC - 1, R=(wb, HTb[t0 // 512]), W=(PS[bk],))
                        if pending is not None:
                            rope_tail(pending)
                            pending = None
                        if is_f or t0 >= T:
                            act(oc[:, t0:t0 + w], psum[bk][:, 0:w], AF.Copy, R=(PS[bk],), W=(ocb,))
                        else:
                            i = tile_ctr % 2
                            act(XBb_t[i][:, 0:w], psum[bk][:, 0:w], AF.Copy, R=(PS[bk],), W=(XBb[i],))
                            pending = (bk, 4 + i, i, oc, t0, w, ocb)
                        tile_ctr += 1
                    if pending is not None:
                        rope_tail(pending)
                        pending = None
                    wcols = tgl[-1][0] + tgl[-1][1]
                    if is_f:
                        dst = FT[mi]
                    elif is_k:
                        dst = KT[G - 2]
                    else:
                        dst = QT[3 * (G - 2) + mi - 1]
                    store(dst[:, 0:wcols], oc[:, 0:wcols], ocb)
            end_phase()
            if check_stop(l, b, "P2"):
                break

            KTs = [A.alloc([NT], BF16) for _ in range(2)]
            QTs = [A.alloc([3, NT], BF16) for _ in range(2)]
            VH = [A.alloc([NTT, 128], BF16) for _ in range(2)]
            OT = [A.alloc([3, NT], BF16) for _ in range(2)]
            PT = [A.alloc([384], BF16) for _ in range(4)]
            REC = [A.alloc([384], F32) for _ in range(2)]
            KQVb = [dbuf(f"kqv{i}") for i in range(2)]
            OTb = [dbuf(f"ot{i}") for i in range(2)]
            PTb = [Buf(f"pt{i}") for i in range(4)]
            RECb = [Buf(f"rec{i}") for i in range(2)]
            RECb2 = [Buf(f"recb{i}") for i in range(2)]
            nqb = 16 if last else 18
            pt_ctr = 0
            s_ctr = 0
            pend_norm = []
            LA = 2
            wq = T if last else NT

            def p3_load(h_):
                j2 = h_ % 2
                load(KTs[j2], KT[h_], KQVb[j2])
                load(QTs[j2][:, :, 0:wq], QT[3 * h_:3 * h_ + 3, :, 0:wq].rearrange("g p t -> p g t"), KQVb[j2])
                load(VH[j2], VV[:, :, h_ * 128:(h_ + 1) * 128].rearrange("t p c -> p t c"), KQVb[j2])

            p3_load(0)
            items = []
            for h in range(NKV):
                for qb in range(nqb):
                    if qb < 16:
                        keys = []
                        if qb > 0:
                            keys.append((qb - 1, MASKLO))
                        keys.append((qb, None))
                        if qb < 15:
                            keys.append((qb + 1, MASKUP))
                        keys += [(16, None), (17, None)]
                    else:
                        keys = [(16, None), (17, None)]
                    for ii, (kb, mask) in enumerate(keys):
                        items.append((h, qb, ii, kb, mask, len(keys)))
            ptis = {}

            def norm(h, qb):
                i2 = h % 2
                ob = qb % 2
                po, psm = psum[3 + ob], psum[5 + ob]
                pob, psb = PS[3 + ob], PS[5 + ob]
                rc = REC[ob]
                for g in range(3):
                    hq = 3 * h + g
                    P.op("dve", ("ts", rc[:, g * 128:(g + 1) * 128], psm[:, g * 128:(g + 1) * 128],
                                 ES[:, l * NQ + hq:l * NQ + hq + 1], None, ALU.add, ALU.bypass),
                         R=(psb, GLOB), W=(RECb[ob], RECb2[ob]))
                act(rc[:, 192:384], rc[:, 192:384], AF.Ln, R=(RECb2[ob],), W=(RECb2[ob],))
                act(rc[:, 192:384], rc[:, 192:384], AF.Exp, R=(RECb2[ob],), W=(RECb2[ob],), scale=-1.0)
                P.op("dve", ("recip", rc[:, 0:192], rc[:, 0:192]), R=(RECb[ob],), W=(RECb[ob],))
                P.op("dve", ("tt", OT[i2][:, :, qb * 128:(qb + 1) * 128],
                             po[:, 0:384].rearrange("p (g q) -> p g q", g=3),
                             rc[:].rearrange("p (g q) -> p g q", g=3), ALU.mult),
                     R=(pob, RECb[ob], RECb2[ob]), W=(OTb[i2],))
                if qb == nqb - 1:
                    store(MIX[4 + 3 * h:4 + 3 * h + 3, :, 0:wq].rearrange("g p t -> p g t"), OT[i2][:, :, 0:wq], OTb[i2])

            pend_n = []
            for n in range(len(items) + LA):
                if n < len(items):
                    (h, qb, ii, kb, mask, nk) = items[n]
                    i2 = h % 2
                    sb_ = s_ctr % 3
                    s_ctr += 1
                    qsl = QTs[i2][:, :, qb * 128:(qb + 1) * 128]
                    mm(psum[sb_][:, 0:384], KTs[i2][:, kb * 128:(kb + 1) * 128], qsl, True, mask is None,
                       R=(KQVb[i2],), W=(PS[sb_],))
                    if mask is not None:
                        mm(psum[sb_][:, 0:384], IDENT, mask, False, True, R=(CONSTB,), W=(PS[sb_],))
                    pti = pt_ctr % 4
                    pt_ctr += 1
                    ptis[n] = pti
                    act(PT[pti][:], psum[sb_][:, 0:384], AF.Exp, R=(PS[sb_],), W=(PTb[pti],), scale=ATTN_SCALE)
                m_ = n - LA
                if m_ >= 0:
                    (h, qb, ii, kb, mask, nk) = items[m_]
                    i2 = h % 2
                    ob = qb % 2
                    if qb == 0 and ii == 0 and h + 1 < NKV:
                        p3_load(h + 1)
                    pti = ptis.pop(m_)
                    mm(psum[3 + ob][:, 0:384], VH[i2][:, kb, :], PT[pti][:], ii == 0, ii == nk - 1,
                       R=(KQVb[i2], PTb[pti]), W=(PS[3 + ob],))
                    mm(psum[5 + ob][:, 0:384], ONES, PT[pti][:], ii == 0, ii == nk - 1,
                       R=(CONSTB, PTb[pti]), W=(PS[5 + ob],))
                    if ii == 1 and pend_n:
                        norm(*pend_n.pop(0))
                    if ii == nk - 1:
                        pend_n.append((h, qb))
            while pend_n:
                norm(*pend_n.pop(0))
            end_phase()
            if check_stop(l, b, "P3"):
                break

            FTs = A.alloc([4, NT], BF16)
            ZCS = A.alloc([NTT, 4, 256], BF16)
            YT = A.alloc([4, NT], BF16)
            OUTC = [A.alloc([NT], BF16) for _ in range(2)]
            DFC = A.alloc([2, 2, 256], BF16)
            FTb = dbuf("fts")
            DFCb = dbuf("dfc")
            ZCSb = Buf("zcs")
            YTb = Buf("yt")
            OUTCb = [dbuf(f"outc{i}") for i in range(2)]
            ntt_f = 16 if last else NTT
            wf = T if last else NT
            load(FTs[:, :, 0:wf], FT[:, :, 0:wf].rearrange("g p t -> p g t"), FTb)
            load(DFC[:], dftc_in[:], DFCb)
            for tt in range(ntt_f):
                for gp in range(2):
                    bk = (tt * 2 + gp) % 4
                    for gi in range(2):
                        g = gp * 2 + gi
                        mm(psum[bk][:, gi * 256:(gi + 1) * 256], FTs[:, g, tt * 128:(tt + 1) * 128], C128S, True, True,
                           R=(FTb, CONSTB), W=(PS[bk],))
                    act(ZCS[:, tt, gp * 2:gp * 2 + 2, :], psum[bk][:].rearrange("p (g c) -> p g c", g=2), AF.Copy,
                        R=(PS[bk],), W=(ZCSb,))
            tile_ctr = 0
            for Gt in range(8):
                wt, wb = wload(dft_in[Gt], SLOT)
                wv = wt[:].rearrange("p (k s c) -> p k s c", k=KC, s=2)
                for g in range(4):
                    bk = tile_ctr % 4
                    tile_ctr += 1
                    for k in range(KC):
                        mm(psum[bk][:, 0:256], ZCS[:, k, g, 0:128], wv[:, k, 0, :], k == 0, False,
                           R=(ZCSb, wb), W=(PS[bk],))
                        mm(psum[bk][:, 0:256], ZCS[:, k, g, 128:256], wv[:, k, 1, :], False, k == KC - 1,
                           R=(ZCSb, wb), W=(PS[bk],))
                    act(YT[:, g, Gt * 256:(Gt + 1) * 256], psum[bk][:, 0:256], AF.Copy, R=(PS[bk],), W=(YTb,))
            if not last:
                for g in range(4):
                    bk = tile_ctr % 4
                    tile_ctr += 1
                    for k in range(2):
                        mm(psum[bk][:, 0:256], ZCS[:, 16 + k, g, 0:128], DFC[:, k, 0, :], k == 0, False,
                           R=(ZCSb, DFCb), W=(PS[bk],))
                        mm(psum[bk][:, 0:256], ZCS[:, 16 + k, g, 128:256], DFC[:, k, 1, :], False, k == 1,
                           R=(ZCSb, DFCb), W=(PS[bk],))
                    act(YT[:, g, T:NT], psum[bk][:, 0:256], AF.Copy, R=(PS[bk],), W=(YTb,))
            wt, wb = wload(w_four[l], 2048)
            wv = wt[:, 0:2048].rearrange("p (k c) -> p k c", k=4)
            for mi in range(4):
                oc = OUTC[mi % 2]
                ocb = OUTCb[mi % 2]
                for (t0, w) in tgs_q:
                    bk = tile_ctr % 4
                    tile_ctr += 1
                    for k in range(4):
                        mm(psum[bk][:, 0:w], wv[:, k, mi * 128:(mi + 1) * 128], YT[:, k, t0:t0 + w], k == 0, k == 3,
                           R=(wb, YTb), W=(PS[bk],))
                    act(oc[:, t0:t0 + w], psum[bk][:, 0:w], AF.Copy, R=(PS[bk],), W=(ocb,))
                store(MIX[mi][:, 0:wf], oc[:, 0:wf], ocb)
            end_phase()
            if check_stop(l, b, "P4"):
                break

            xdst = XB
            MX = [A.alloc([KC, 512], BF16) for _ in range(2)]
            Zt = [A.alloc([KC, 512], F32) for _ in range(2)]
            ZSQ = A.alloc([KC, 512], BF16)
            HO = A.alloc([KC, 512], BF16)
            STAT = A.alloc([2, 512], F32)
            MXb = [dbuf(f"mx{i}") for i in range(2)]
            Zb = [dbuf(f"z{i}") for i in range(2)]
            ZK = [[Buf(f"zk{i}_{k}") for k in range(KC)] for i in range(2)]
            HOb = dbuf("ho")
            SQb = Buf("zsq")
            STb = Buf("stat")
            tile_ctr = 0
            DQ = Deferred()
            ntg = len(tgs_q)

            def p5_load_mx(ti_):
                t0_, w_ = tgs_q[ti_]
                load(MX[ti_ % 2][:, :, 0:w_], MIX[:, :, t0_:t0_ + w_].rearrange("k p t -> p k t"), MXb[ti_ % 2])

            def p5_load_z(ti_, src_):
                t0_, w_ = tgs_q[ti_]
                load(Zt[ti_ % 2][:, :, 0:w_], src_[:, :, t0_:t0_ + w_].rearrange("k p t -> p k t"), Zb[ti_ % 2])

            p5_load_mx(0)
            p5_load_z(0, xsrc)
            if ntg > 1:
                p5_load_z(1, xsrc)
            for ti, (t0, w) in enumerate(tgs_q):
                i2 = ti % 2
                r = b if t0 < T else 2
                if ti + 1 < ntg:
                    p5_load_mx(ti + 1)
                for G in range(4):
                    wt, wb = wload(WO16[l, G], SLOT, extra_R=(PRE[l],))
                    wv = wt[:].rearrange("p (k c) -> p k c", k=KC)
                    for mi in range(4):
                        m = G * 4 + mi
                        bk = tile_ctr % 6
                        tile_ctr += 1
                        for k in range(KC):
                            mm(psum[bk][:, 0:w], wv[:, k, mi * 128:(mi + 1) * 128], MX[i2][:, k, 0:w],
                               k == 0, k == KC - 1, R=(wb, MXb[i2]), W=(PS[bk],))
                        P.op("dve", ("stt", Zt[i2][:, m, 0:w], psum[bk][:, 0:w], mod_ap(l, 2, m, r),
                                     Zt[i2][:, m, 0:w], ALU.mult, ALU.add),
                             R=(PS[bk], Zb[i2], ZK[i2][m], GLOB), W=(ZK[i2][m],))
                        DQ.tick()

                def fin(ti=ti, t0=t0, w=w, i2=i2, src_=xsrc):
                    P.op("sp", ("dma", xdst[:, :, t0:t0 + w].rearrange("k p t -> p k t"), Zt[i2][:, :, 0:w]),
                         R=[Zb[i2]] + ZK[i2], W=(), dma=Zb[i2])
                    store(H2[:, :, t0:t0 + w].rearrange("k p t -> p k t"), HO[:, :, 0:w], HOb)
                    if ti + 2 < ntg:
                        p5_load_z(ti + 2, src_)

                ln_sched(DQ, Zt[i2], ZSQ, HO, STAT, w, l, 0, r, Zb[i2], HOb, SQb, STb, True, fin, (0, 6, 6, 8, 2), ZK[i2])
            DQ.flush()
            end_phase()
            xsrc = xdst
            if check_stop(l, b, "P5"):
                break

            CW = 2308
            LAT0, CTX0 = 1, 2051
            HT = A.alloc([KC, NT], BF16)
            CAt = [A.alloc([CW], F32) for _ in range(2)]
            CGt = [A.alloc([CW], F32) for _ in range(2)]
            RO = [A.alloc([NT], BF16) for _ in range(2)]
            HTb = [dbuf(f"ht{i}") for i in range(5)]
            ROb = [dbuf(f"ro{i}") for i in range(2)]
            CAb = [[Buf(f"ca{i}_{n}") for n in range(5)] for i in range(2)]
            CGb = [[Buf(f"cg{i}_{n}") for n in range(5)] for i in range(2)]
            wq = T if last else NT
            for n_, (t0_, w_) in enumerate(tgs_q):
                for kh in range(2):
                    load(HT[:, kh * 8:kh * 8 + 8, t0_:t0_ + w_],
                         H2[kh * 8:kh * 8 + 8, :, t0_:t0_ + w_].rearrange("k p t -> p k t"), HTb[n_])

            def ccol(t0):
                return (LAT0 + t0) if t0 < T else (CTX0 + t0 - T)

            tile_ctr = 0
            if b == 0 and l + 1 < nlayers:
                pre_pending[l + 1] = precast_steps(l + 1)
            for G in range(22):
                if b == 0 and pre_pending.get(l + 1):
                    pre_pending[l + 1].pop(0)()
                wt, wb = wload(w_up[l, G], SLOT)
                wv = wt[:].rearrange("p (j a k c) -> p j a k c", j=2, a=2, k=KC)
                for jj in range(2):
                    j = 2 * G + jj
                    i2 = j % 2
                    streams = ((0, CAt[i2], CAb[i2], j, 0), (1, CGt[i2], CGb[i2], NJ + j, 4))
                    pend = None

                    def taps(pd):
                        (n_, t0_, w_, bka_) = pd
                        for (ag, Ct, Cb, ch, bo) in streams:
                            c0 = ccol(t0_)
                            nb_ = [Cb[n_]]
                            if t0_ < T:
                                if n_ > 0:
                                    nb_.append(Cb[n_ - 1])
                                if n_ < 3:
                                    nb_.append(Cb[n_ + 1])
                            pt_ = psum[bo + bka_]
                            P.op("dve", ("stt", Ct[:, c0 + 1:c0 + 1 + w_], pt_[:, 0:w_], CONV[:, l, 0, ch:ch + 1],
                                         Ct[:, c0 + 1:c0 + 1 + w_], ALU.mult, ALU.add),
                                 R=[PS[bo + bka_], CONSTB] + nb_, W=nb_)
                            P.op("dve", ("stt", Ct[:, c0 - 1:c0 - 1 + w_], pt_[:, 0:w_], CONV[:, l, 2, ch:ch + 1],
                                         Ct[:, c0 - 1:c0 - 1 + w_], ALU.mult, ALU.add),
                                 R=[PS[bo + bka_], CONSTB] + nb_, W=nb_)

                    for n, (t0, w) in enumerate(tgs_q):
                        bka = tile_ctr % 4
                        tile_ctr += 1
                        for (ag, Ct, Cb, ch, bo) in streams:
                            for k in range(KC):
                                mm(psum[bo + bka][:, 0:w], wv[:, jj, ag, k, :], HT[:, k, t0:t0 + w], k == 0, k == KC - 1,
                                   R=(wb, HTb[n]), W=(PS[bo + bka],))
                        for (ag, Ct, Cb, ch, bo) in streams:
                            c0 = ccol(t0)
                            act(Ct[:, c0:c0 + w], psum[bo + bka][:, 0:w], AF.Identity, R=(PS[bo + bka], CONSTB), W=(Cb[n],),
                                bias=CONV[:, l, 3, ch:ch + 1], scale=CONV[:, l, 1, ch:ch + 1])
                        if pend is not None and t0 < T:
                            taps(pend)
                            pend = None
                        if pend is not None:
                            taps(pend)
                            pend = None
                        pend = (n, t0, w, bka)
                    taps(pend)
                    pend = None
                    allb_a = CAb[i2][:len(tgs_q)]
                    allb_g = CGb[i2][:len(tgs_q)]
                    segs = [(LAT0, 0, T)] + ([] if last else [(CTX0, T, CL)])
                    for (c0, o0, w) in segs:
                        act(CGt[i2][:, c0:c0 + w], CGt[i2][:, c0:c0 + w], AF.Silu, R=allb_g, W=allb_g)
                    for (c0, o0, w) in segs:
                        P.op("dve", ("tt", RO[i2][:, o0:o0 + w], CGt[i2][:, c0:c0 + w], CAt[i2][:, c0:c0 + w], ALU.mult),
                             R=allb_g + allb_a, W=(ROb[i2],))
                    store(RR[j][:, 0:wq], RO[i2][:, 0:wq], ROb[i2])
            end_phase()
            if check_stop(l, b, "P6"):
                break

            xdst = XA
            RTh = [A.alloc([22, 512], BF16) for _ in range(2)]
            Zt = [A.alloc([KC, 512], F32) for _ in range(2)]
            HO = A.alloc([KC, 512], BF16)
            STAT = A.alloc([2, 512], F32)
            RTb = [dbuf(f"rt{i}") for i in range(2)]
            Zb = [dbuf(f"z{i}") for i in range(2)]
            ZK = [[Buf(f"zk{i}_{k}") for k in range(KC)] for i in range(2)]
            HOb = dbuf("ho")
            STb = Buf("stat")
            tile_ctr = 0
            DQ = Deferred()
            ntg = len(tgs_q)
            halves = [(ti, jh) for ti in range(ntg) for jh in range(2)]
            want_h7 = (l + 1 < nlayers)

            def p7_load_half(idx_):
                ti_, jh_ = halves[idx_]
                t0_, w_ = tgs_q[ti_]
                bi = idx_ % 2
                for jq in range(2):
                    j0 = jh_ * 22 + jq * 11
                    load(RTh[bi][:, jq * 11:(jq + 1) * 11, 0:w_],
                         RR[j0:j0 + 11, :, t0_:t0_ + w_].rearrange("j p t -> p j t"), RTb[bi])

            def p7_load_z(ti_, src_):
                t0_, w_ = tgs_q[ti_]
                load(Zt[ti_ % 2][:, :, 0:w_], src_[:, :, t0_:t0_ + w_].rearrange("k p t -> p k t"), Zb[ti_ % 2])

            p7_load_half(0)
            p7_load_z(0, xsrc)
            if ntg > 1:
                p7_load_z(1, xsrc)
            for idx, (ti, jh) in enumerate(halves):
                t0, w = tgs_q[ti]
                i2 = ti % 2
                r = b if t0 < T else 2
                if idx + 1 < len(halves):
                    p7_load_half(idx + 1)
                for mp in range(8):
                    wt, wb = wload(WD16[l, jh, mp], 5632, extra_R=(PRE[l],))
                    wv = wt[:, 0:5632].rearrange("p (m j c) -> p m j c", m=2, j=22)
                    for mi in range(2):
                        m = mp * 2 + mi
                        bk = tile_ctr % 6
                        tile_ctr += 1
                        for j in range(22):
                            mm(psum[bk][:, 0:w], wv[:, mi, j, :], RTh[idx % 2][:, j, 0:w], j == 0, j == 21,
                               R=(wb, RTb[idx % 2]), W=(PS[bk],))
                        P.op("dve", ("stt", Zt[i2][:, m, 0:w], psum[bk][:, 0:w], mod_ap(l, 5, m, r),
                                     Zt[i2][:, m, 0:w], ALU.mult, ALU.add),
                             R=(PS[bk], Zb[i2], ZK[i2][m], GLOB), W=(ZK[i2][m],))
                        DQ.tick()
                if jh == 1:
                    def fin(ti=ti, t0=t0, w=w, i2=i2, src_=xsrc):
                        dst_ = outT[b] if last else xdst
                        P.op("sp", ("dma", dst_[:, :, t0:t0 + w].rearrange("k p t -> p k t"), Zt[i2][:, :, 0:w]),
                             R=[Zb[i2]] + ZK[i2], W=(), dma=Zb[i2])
                        if not last:
                            if want_h7:
                                store(H1[:, :, t0:t0 + w].rearrange("k p t -> p k t"), HO[:, :, 0:w], HOb)
                        if ti + 2 < ntg:
                            p7_load_z(ti + 2, src_)

                    ln_sched(DQ, Zt[i2], None, HO, STAT, w, l, 1, r, Zb[i2], HOb, None, STb, want_h7, fin, (0, 3, 7, 8, 1), ZK[i2])
            DQ.flush()
            end_phase()
            xsrc = xdst
            if check_stop(l, b, "P7"):
                break

    P.op("sp", ("nop",))
    P.op("act", ("nop",))

    P.finalize(nc, engsem)
    assert _deadlock_check(P), 'deadlock in sync plan'
    with sem_stack, nc.Block() as block:
        @block.tensor
        def _(e):
            P.emit("pe", e, engsem)

        @block.scalar
        def _(e):
            P.emit("act", e, engsem)

        @block.vector
        def _(e):
            P.emit("dve", e, engsem)

        @block.gpsimd
        def _(e):
            P.emit("pool", e, engsem)

        @block.sync
        def _(e):
            P.emit("sp", e, engsem)
    return nc, len(P.ops)


def _const_tables():
    bf = ml_dtypes.bfloat16
    cb = np.zeros((128, 2176), np.float32)
    cb[:, 0:128] = np.eye(128)
    cb[:, 128:256] = 1.0
    cb[:, 256:384] = 1.0 / D
    ps = np.zeros((128, 128), np.float32)
    for i in range(32):
        ps[32 + i, i] = -1.0
        ps[i, 32 + i] = 1.0
        ps[96 + i, 64 + i] = -1.0
        ps[64 + i, 96 + i] = 1.0
    cb[:, 384:512] = ps
    kk = np.arange(128)[:, None]
    qq = np.arange(128)[None, :]
    lo = np.where(kk >= qq, 0.0, NEG).astype(np.float32)
    up = np.where(kk <= qq, 0.0, NEG).astype(np.float32)
    cb[:, 512:896] = np.tile(lo, (1, 3))
    cb[:, 896:1280] = np.tile(up, (1, 3))
    c = np.arange(128, dtype=np.float64)
    ang = 2 * np.pi * np.outer(c, c) / 128.0
    cb[:, 1280:1408] = np.cos(ang) / np.sqrt(128.0)
    cb[:, 1408:1536] = -np.sin(ang) / np.sqrt(128.0)
    cbf = cb.astype(bf)
    inv = (10000.0 ** (-np.arange(0, 64, 2, dtype=np.float32) / np.float32(64))).astype(np.float32)
    row = np.repeat(np.arange(T // 64, dtype=np.float32), 64)
    col = np.tile(np.arange(64, dtype=np.float32), T // 64)
    ar = (row[None, :] * inv[:, None]).astype(np.float32)
    ac = (col[None, :] * inv[:, None]).astype(np.float32)
    angp = np.concatenate([ar, ar, ac, ac], axis=0)
    rope = np.stack([np.cos(angp), np.sin(angp)], axis=1).astype(np.float32)
    t = np.arange(T, dtype=np.float64)
    full = 2 * np.pi * ((np.outer(t, t)) % T) / T
    Cm = (np.cos(full) / np.sqrt(T)).astype(np.float32)
    Sm = (np.sin(full) / np.sqrt(T)).astype(np.float32)
    dft = np.zeros((8, 128, KC, 2, 256), np.float32)
    for G in range(8):
        cs = Cm[:, G * 256:(G + 1) * 256].reshape(KC, 128, 256).transpose(1, 0, 2)
        ss = Sm[:, G * 256:(G + 1) * 256].reshape(KC, 128, 256).transpose(1, 0, 2)
        dft[G, :, :, 0, :] = cs
        dft[G, :, :, 1, :] = ss
    dft = dft.reshape(8, 128, SLOT).astype(bf)
    tc = np.arange(CL, dtype=np.float64)
    fc = 2 * np.pi * (np.outer(tc, tc) % CL) / CL
    Cc = (np.cos(fc) / np.sqrt(CL)).astype(np.float32).reshape(2, 128, 256).transpose(1, 0, 2)
    Sc = (np.sin(fc) / np.sqrt(CL)).astype(np.float32).reshape(2, 128, 256).transpose(1, 0, 2)
    dftc = np.stack([Cc, Sc], axis=2).reshape(128, 1024).astype(bf)
    return cbf, rope, dft, dftc


def _layout_weights(w_ada, b_ada, w_in, sink, w_four, w_out, ln_g, ln_b, w_up, conv_w, conv_b, w_down, nl):
    f = np.float32
    res = {}
    wa = np.asarray(w_ada[:nl], f).reshape(nl, KC, 128, 24, 4, 128)
    res["w_ada_t"] = np.ascontiguousarray(wa.transpose(0, 3, 2, 4, 1, 5)).reshape(nl, 24, 128, SLOT)
    res["b_ada_t"] = np.ascontiguousarray(np.asarray(b_ada[:nl], f).reshape(nl, 96, 128).transpose(2, 0, 1))
    wi = np.asarray(w_in[:nl], f)
    cols = []
    cols.append(np.arange(2560, 3072))
    cols.append(np.arange(0, 512))
    for h in range(NKV):
        cc = [np.arange(2048 + h * 128, 2048 + (h + 1) * 128)]
        for g in range(3):
            hq = 3 * h + g
            cc.append(np.arange(512 + hq * 128, 512 + (hq + 1) * 128))
        cols.append(np.concatenate(cc))
    wit = np.empty((nl, 6, 128, KC, 512), f)
    for G in range(6):
        wit[:, G] = wi[:, :, cols[G]].reshape(nl, KC, 128, 512).transpose(0, 2, 1, 3)
    res["w_in_t"] = wit.reshape(nl, 6, 128, SLOT)
    res["w_four_t"] = np.ascontiguousarray(
        np.asarray(w_four[:nl], f).reshape(nl, 4, 128, 512).transpose(0, 2, 1, 3)).reshape(nl, 128, 2048)
    wo = np.asarray(w_out[:nl], f).reshape(nl, KC, 128, 4, 512)
    res["w_out_t"] = np.ascontiguousarray(wo.transpose(0, 3, 2, 1, 4)).reshape(nl, 4, 128, SLOT)
    wu = np.asarray(w_up[:nl], f).reshape(nl, KC, 128, 2, 22, 2, 128)
    res["w_up_t"] = np.ascontiguousarray(wu.transpose(0, 4, 2, 5, 3, 1, 6)).reshape(nl, 22, 128, SLOT)
    wd = np.asarray(w_down[:nl], f).reshape(nl, NJ, 128, KC, 128)
    res["w_down_t"] = np.ascontiguousarray(wd.transpose(0, 3, 2, 1, 4)).reshape(nl, KC, 128, NJ * 128)
    cv = np.concatenate([np.asarray(conv_w[:nl], f), np.asarray(conv_b[:nl], f)[:, None, :]], axis=1)
    res["conv_t"] = np.ascontiguousarray(cv.reshape(nl, 4, 88, 128).transpose(3, 0, 1, 2))
    lg = np.stack([np.asarray(ln_g[:nl], f), np.asarray(ln_b[:nl], f)], axis=2)
    res["ln_t"] = np.ascontiguousarray(lg.reshape(nl, 2, 2, KC, 128).transpose(4, 0, 1, 2, 3))
    res["sink_t"] = np.ascontiguousarray(np.broadcast_to(np.asarray(sink[:nl], f).reshape(1, nl * NQ), (128, nl * NQ)))
    return res


def _core_inputs(x, c, ctx, c_ctx, b0, nbc):
    xs = np.asarray(x[b0:b0 + nbc], np.float32)
    cs = np.asarray(ctx[b0:b0 + nbc], np.float32)
    xa = np.concatenate([xs, cs], axis=1)
    x0T = np.ascontiguousarray(xa.reshape(nbc, NT, KC, 128).transpose(0, 2, 3, 1))
    rows = [np.asarray(c[b0 + i], np.float32) for i in range(nbc)]
    while len(rows) < 2:
        rows.append(rows[-1])
    rows.append(np.asarray(c_ctx, np.float32))
    cc = np.stack(rows, axis=0)
    cT = np.ascontiguousarray(cc.reshape(3, KC, 128).transpose(2, 1, 0))
    return x0T, cT


_CACHE = {}


def kernel(x, c, ctx, c_ctx, w_ada, b_ada, w_in, sink, w_four, w_out, ln_g, ln_b,
           w_up, conv_w, conv_b, w_down):
    ncores = 8
    nbc = 2
    if "nc" not in _CACHE:
        _CACHE["nc"] = build_nc(L_ALL, nbc)[0]
        _CACHE["consts"] = _const_tables()
    nc = _CACHE["nc"]
    cbf, rope, dft, dftc = _CACHE["consts"]
    wl = _layout_weights(w_ada, b_ada, w_in, sink, w_four, w_out, ln_g, ln_b, w_up, conv_w, conv_b, w_down, L_ALL)
    in_maps = []
    for ci in range(ncores):
        x0T, cT = _core_inputs(x, c, ctx, c_ctx, ci * nbc, nbc)
        m = dict(wl)
        m.update({"x0T": x0T, "cT": cT, "rope_t": rope, "cbf_t": cbf, "dft_t": dft, "dftc_t": dftc})
        in_maps.append(m)
    res = run_bass_kernel_spmd(nc, in_maps, core_ids=list(range(ncores)))
    out = np.empty((ncores * nbc, T, D), np.float32)
    for ci in range(ncores):
        o = res.results[ci]["outT"]
        out[ci * nbc:(ci + 1) * nbc] = o.transpose(0, 3, 1, 2).reshape(nbc, T, D)
    return out


def _deadlock_check(P):
    engs = {}
    for o in P.ops:
        engs.setdefault(o.eng, []).append(o)
    pos = {e: 0 for e in engs}
    sem = {}
    dsem = {}
    progress = True
    while progress:
        progress = False
        for e, lst in engs.items():
            while pos[e] < len(lst):
                o = lst[pos[e]]
                ok = all(sem.get(p.eng, 0) >= p.ms for p in o.cdeps) and \
                    all(dsem.get(ds.id, 0) >= val for (ds, val) in o.ddeps)
                if not ok:
                    break
                if o.ds is not None:
                    dsem[o.ds.id] = dsem.get(o.ds.id, 0) + 16
                elif o.need:
                    sem[e] = sem.get(e, 0) + 1
                    assert sem[e] == o.ms, (e, sem[e], o.ms)
                pos[e] += 1
                progress = True
    stuck = {e: (pos[e], len(lst)) for e, lst in engs.items() if pos[e] < len(lst)}
    for e, (p_, n_) in stuck.items():
        o = engs[e][p_]
        print("STUCK", e, p_, n_, o.ins[0], [(p.eng, p.ms, sem.get(p.eng, 0)) for p in o.cdeps],
              [(ds.id, val, dsem.get(ds.id, 0)) for (ds, val) in o.ddeps])
    return not stuck
```

```python
import contextlib
import os
import numpy as np
import ml_dtypes
import concourse.bass as bass
import concourse.mybir as mybir
from concourse.bass_utils import run_bass_kernel_spmd

F32 = mybir.dt.float32
BF16 = mybir.dt.bfloat16
AF = mybir.ActivationFunctionType
ALU = mybir.AluOpType

D = 2048
KC = 16
T = 2048
CL = 256
NT = T + CL
NTT = NT // 128
DFF = 5632
NJ = DFF // 128
L_ALL = 4
NQ = 12
NKV = 4
ALPHA = (2 * L_ALL) ** 0.25
LN_EPS = 1e-5
EPSP = LN_EPS / (ALPHA * ALPHA)
ATTN_SCALE = 128 ** -0.5
NEG = -30000.0
TGS = [(0, 512), (512, 512), (1024, 512), (1536, 512), (2048, 256)]
SLOT = 8192
NSLOT = 3
ARENA_BYTES = 136 * 1024
SAME_ENGINE_SYNC = True


class DmaSem:
    __slots__ = ("sem", "count", "id")

    def __init__(self, sem, i):
        self.sem = sem
        self.count = 0
        self.id = i


class Buf:
    __slots__ = ("name", "w", "r", "ds", "psum")

    def __init__(self, name, ds=None, psum=False):
        self.psum = psum
        self.name = name
        self.w = []
        self.r = []
        self.ds = ds


class Op:
    __slots__ = ("eng", "ins", "cdeps", "ddeps", "ms", "need", "ds", "dval", "idx")


COMPUTE = ("pe", "act", "dve", "pool")


class Prog:
    def __init__(self):
        self.ops = []
        self.last = {}
        self.lastc = {}
        self.phase_dma = []
        self.pending_barrier = {}

    def op(self, eng, ins, R=(), W=(), dma=None):
        o = Op()
        o.eng = eng
        o.ins = ins
        o.ms = 0
        o.need = False
        o.idx = len(self.ops)
        o.ds = None
        o.dval = 0
        cd = {}
        dd = {}

        def add(p):
            if p.ds is not None:
                k = p.ds.id
                if k not in dd or dd[k][1] < p.dval:
                    dd[k] = (p.ds, p.dval)
            else:
                q = cd.get(p.eng)
                if q is None or q.idx < p.idx:
                    cd[p.eng] = p

        pb = self.pending_barrier.pop(eng, None)
        if pb is not None:
            for p in pb:
                add(p)
        for b in R:
            for p in b.w:
                add(p)
            if b.psum:
                for p in b.r:
                    if p.eng != eng:
                        add(p)
        for b in W:
            if b.r or (b in R):
                for p in b.r:
                    add(p)
                for p in b.w:
                    add(p)
                b.r = []
                b.w = [o]
            else:
                for p in b.w:
                    if p.eng != eng:
                        add(p)
                b.w.append(o)
        for b in R:
            if b not in W:
                b.r.append(o)
        if dma is not None:
            ds = dma.ds
            ds.count += 16
            o.ds = ds
            o.dval = ds.count
            if eng != "pool":
                self.phase_dma.append(o)
        if eng in cd:
            if eng == "pe" or not SAME_ENGINE_SYNC or eng in ("sp", "pool"):
                del cd[eng]
        o.cdeps = list(cd.values())
        o.ddeps = list(dd.values())
        self.ops.append(o)
        self.last[eng] = o
        if dma is None and ins[0] != "nop":
            self.lastc[eng] = o
        return o

    def barrier(self):
        prev = [self.lastc[e] for e in COMPUTE if e in self.lastc] + self.phase_dma
        self.phase_dma = []
        for e in ("pe", "act", "dve", "sp"):
            cur = self.pending_barrier.get(e, [])
            self.pending_barrier[e] = cur + prev

    def finalize(self, nc, engsem):
        for o in self.ops:
            for p in o.cdeps:
                p.need = True
        cnt = {}
        for o in self.ops:
            if o.need:
                cnt[o.eng] = cnt.get(o.eng, 0) + 1
                o.ms = cnt[o.eng]

    def emit(self, eng, e, engsem):
        waited = {}
        n = 0
        for o in self.ops:
            if o.eng != eng:
                continue
            for p in o.cdeps:
                key = ("c", p.eng)
                if waited.get(key, 0) < p.ms:
                    e.wait_ge(engsem[p.eng], p.ms)
                    waited[key] = p.ms
            for (ds, val) in o.ddeps:
                key = ("d", ds.id)
                if waited.get(key, 0) < val:
                    e.wait_ge(ds.sem, val)
                    waited[key] = val
            ins = o.ins
            k = ins[0]
            r = None
            if k == "mm":
                r = e.matmul(ins[1], ins[2], ins[3], start=ins[4], stop=ins[5])
            elif k == "act":
                r = e.activation(out=ins[1], in_=ins[2], func=ins[3], bias=ins[4], scale=ins[5])
            elif k == "tt":
                r = e.tensor_tensor(out=ins[1], in0=ins[2], in1=ins[3], op=ins[4])
            elif k == "ts":
                r = e.tensor_scalar(ins[1], ins[2], ins[3], ins[4], ins[5], ins[6])
            elif k == "stt":
                r = e.scalar_tensor_tensor(ins[1], ins[2], ins[3], ins[4], ins[5], ins[6])
            elif k == "copy":
                r = e.tensor_copy(out=ins[1], in_=ins[2])
            elif k == "recip":
                r = e.reciprocal(ins[1], ins[2])
            elif k == "memset":
                r = e.memset(ins[1], ins[2])
            elif k == "dma":
                r = e.dma_start(out=ins[1], in_=ins[2])
            elif k == "nop":
                r = None
            else:
                raise ValueError(k)
            if r is not None:
                if o.ds is not None:
                    r.then_inc(o.ds.sem, 16)
                elif o.need:
                    r.then_inc(engsem[eng], 1)
            n += 1
        return n


class Deferred:
    def __init__(self):
        self.q = []

    def add(self, delay, fn):
        self.q.append([delay, fn])

    def tick(self):
        for it in self.q:
            it[0] -= 1
        ready = [it for it in self.q if it[0] <= 0]
        self.q = [it for it in self.q if it[0] > 0]
        for it in ready:
            it[1]()

    def flush(self):
        while self.q:
            self.q.sort(key=lambda it: it[0])
            self.q.pop(0)[1]()


class Arena:
    def __init__(self, ap_f32, nbytes):
        self.ap = ap_f32
        self.nbytes = nbytes
        self.off = 0

    def reset(self):
        self.off = 0

    def alloc(self, free_shape, dtype):
        esz = 4 if dtype == F32 else 2
        n = int(np.prod(free_shape))
        nb = (n * esz + 63) // 64 * 64
        assert self.off + nb <= self.nbytes, ("arena overflow", self.off, nb)
        v = self.ap[:, self.off // 4:(self.off + nb) // 4]
        self.off += nb
        if dtype != F32:
            v = v.bitcast(dtype)
        v = v[:, 0:n]
        if len(free_shape) == 2:
            v = v.rearrange("p (a b) -> p a b", b=free_shape[1])
        elif len(free_shape) == 3:
            v = v.rearrange("p (a b c) -> p a b c", b=free_shape[1], c=free_shape[2])
        elif len(free_shape) == 4:
            v = v.rearrange("p (a b c d) -> p a b c d", b=free_shape[1], c=free_shape[2], d=free_shape[3])
        return v


def build_nc(nlayers=L_ALL, nb=2, dump=(), stop_after=None):
    nc = bass.Bass("TRN2", target_bir_lowering=False)
    P = Prog()
    last_layer_idx = L_ALL - 1

    def din(name, shape, dt=F32):
        return nc.dram_tensor(name, list(shape), dt, kind="ExternalInput").ap()

    def dscr(name, shape, dt):
        kind = "ExternalOutput" if name in dump else "Internal"
        return nc.dram_tensor(name, list(shape), dt, kind=kind).ap()

    x0T = din("x0T", [nb, KC, 128, NT])
    cT = din("cT", [128, KC, 3])
    w_ada = din("w_ada_t", [nlayers, 24, 128, SLOT])
    b_ada = din("b_ada_t", [128, nlayers, 96])
    w_in = din("w_in_t", [nlayers, 6, 128, SLOT])
    w_four = din("w_four_t", [nlayers, 128, 2048])
    w_out = din("w_out_t", [nlayers, 4, 128, SLOT])
    w_up = din("w_up_t", [nlayers, 22, 128, SLOT])
    w_down = din("w_down_t", [nlayers, 16, 128, NJ * 128])
    conv_in = din("conv_t", [128, nlayers, 4, 88])
    ln_in = din("ln_t", [128, nlayers, 2, 2, KC])
    sink_in = din("sink_t", [128, nlayers * NQ])
    rope_in = din("rope_t", [128, 2, T])
    cbf_in = din("cbf_t", [128, 2176], BF16)
    dft_in = din("dft_t", [8, 128, SLOT], BF16)
    dftc_in = din("dftc_t", [128, 1024], BF16)
    outT = nc.dram_tensor("outT", [nb, KC, 128, T], F32, kind="ExternalOutput").ap()
    XA = dscr("XA", [KC, 128, NT], F32)
    XB = dscr("XB", [KC, 128, NT], F32)
    H1 = dscr("H1", [KC, 128, NT], BF16)
    H2 = dscr("H2", [KC, 128, NT], BF16)
    H1B = dscr("H1B", [KC, 128, NT], BF16)
    QT = dscr("QT", [NQ, 128, NT], BF16)
    KT = dscr("KT", [NKV, 128, NT], BF16)
    VV = dscr("VV", [NTT, 128, 512], BF16)
    FT = dscr("FT", [4, 128, NT], BF16)
    MIX = dscr("MIX", [KC, 128, NT], BF16)
    RR = dscr("RR", [NJ, 128, NT], BF16)
    MODD = dscr("MODD", [128, nlayers * 96 * 3], F32)
    WD16 = dscr("WD16", [nlayers, 2, 8, 128, 5632], BF16)
    WO16 = dscr("WO16", [nlayers, 4, 128, SLOT], BF16)

    def sb(name, shape, dt):
        return nc.alloc_sbuf_tensor(name, list(shape), dt)

    arena_t = sb("arena", [128, ARENA_BYTES // 4], F32)
    ring_t = [sb(f"ring{i}", [128, SLOT], BF16) for i in range(NSLOT)]
    cbf = sb("cbf", [128, 2176], BF16)
    MOD = sb("mod", [128, nlayers, 96, 3], F32)
    FS = sb("fs", [128, nlayers, 2, KC, 3], F32)
    FB = sb("fb", [128, nlayers, 2, KC, 3], F32)
    LNP = sb("lnp", [128, nlayers, 2, 2, KC], F32)
    CONV = sb("conv", [128, nlayers, 4, 88], F32)
    BADA = sb("bada", [128, nlayers, 96], F32)
    ES = sb("es", [128, nlayers * NQ], F32)
    CIN = sb("cin", [128, KC, 3], F32)
    SCB = sb("scb", [128, KC, 3], BF16)
    EPS = sb("eps", [128, 1], F32)
    psum = [nc.alloc_psum_tensor(f"ps{i}", [128, 512], F32) for i in range(8)]

    IDENT = cbf[:, 0:128]
    ONES = cbf[:, 128:256]
    ONESD = cbf[:, 256:384]
    PSWAP = cbf[:, 384:512]
    MASKLO = cbf[:, 512:896]
    MASKUP = cbf[:, 896:1280]
    C128S = cbf[:, 1280:1536]

    A = Arena(arena_t[:], ARENA_BYTES)

    sem_handles = []

    sem_stack = contextlib.ExitStack()

    def new_sem(name):
        h = sem_stack.enter_context(nc.semaphore(name))
        sem_handles.append(h)
        return h

    engsem = {e: new_sem("s_" + e) for e in ("pe", "act", "dve", "pool")}
    dsem_pool = [DmaSem(new_sem(f"d{i}"), i) for i in range(60)]
    dsem_free = list(dsem_pool)
    phase_sems = []

    def dbuf(name, persistent=False):
        ds = dsem_free.pop()
        if not persistent:
            phase_sems.append(ds)
        return Buf(name, ds)

    def end_phase():
        P.barrier()
        dsem_free.extend(phase_sems)
        phase_sems.clear()
        A.reset()

    PS = [Buf(f"ps{i}", psum=True) for i in range(8)]
    RING = [dbuf(f"ring{i}", persistent=True) for i in range(NSLOT)]
    ring_ctr = [0]
    CONSTB = dbuf("consts", persistent=True)
    GLOB = Buf("glob")

    PRE = [dbuf(f"pre{i}", persistent=True) for i in range(nlayers)]

    def precast_steps(l):
        steps = []
        for G in range(4):
            steps.append(lambda G=G: P.op("pool", ("dma", WO16[l, G].rearrange("p (a b) -> p a b", b=2048),
                                                   w_out[l, G].rearrange("p (a b) -> p a b", b=2048)),
                                          R=(), W=(PRE[l],), dma=PRE[l]))
        for m in range(KC):
            def f(m=m):
                src = w_down[l, m].rearrange("p (h a b) -> p h a b", h=2, b=1408)
                dst = WD16[l, :, m // 2, :, (m % 2) * 2816:(m % 2 + 1) * 2816].rearrange("h p (a b) -> p h a b", b=1408)
                P.op("pool", ("dma", dst, src), R=(), W=(PRE[l],), dma=PRE[l])
            steps.append(f)
        return steps

    pre_pending = {}

    def wload(src_ap, nelem, extra_R=()):
        i = ring_ctr[0] % NSLOT
        ring_ctr[0] += 1
        dst_ap = ring_t[i][:, 0:nelem]
        if nelem > 2048:
            bb = 2048 if nelem % 2048 == 0 else 1408
            dst_ap = dst_ap.rearrange("p (a b) -> p a b", b=bb)
            src_ap = src_ap.rearrange("p (a b) -> p a b", b=bb)
        P.op("pool", ("dma", dst_ap, src_ap), R=extra_R, W=(RING[i],), dma=RING[i])
        return ring_t[i], RING[i]

    def mm(out, lhsT, rhs, start, stop, R, W):
        P.op("pe", ("mm", out, lhsT, rhs, start, stop), R=R, W=W)

    def act(out, in_, func, R, W, bias=0.0, scale=1.0):
        P.op("act", ("act", out, in_, func, bias, scale), R=R, W=W)

    def load(dst, src, buf):
        P.op("sp", ("dma", dst, src), R=(), W=(buf,), dma=buf)

    def store(dst, src, buf):
        P.op("sp", ("dma", dst, src), R=(buf,), W=(), dma=buf)

    load(cbf[:], cbf_in[:], CONSTB)
    load(CIN[:], cT[:], CONSTB)
    load(BADA[:], b_ada[:], CONSTB)
    load(LNP[:], ln_in[:], CONSTB)
    load(CONV[:], conv_in[:], CONSTB)
    load(ES[:], sink_in[:], CONSTB)
    P.op("dve", ("memset", EPS[:], EPSP), R=(), W=(GLOB,))
    act(ES[:], ES[:], AF.Exp, R=(CONSTB,), W=(CONSTB, GLOB))
    act(SCB[:], CIN[:], AF.Silu, R=(CONSTB,), W=(GLOB,))
    def ada_layer(l):
        pb = PS[l % 2]
        pt = psum[l % 2]
        for G in range(24):
            wt, wb = wload(w_ada[l, G], SLOT)
            wv = wt[:].rearrange("p (m k c) -> p m k c", m=4, k=KC)
            for mi in range(4):
                m = G * 4 + mi
                for k in range(KC):
                    mm(pt[:, m * 3:m * 3 + 3], wv[:, mi, k, :], SCB[:, k, :], k == 0, k == KC - 1,
                       R=(wb, GLOB), W=(pb,))
        for r in range(3):
            P.op("dve", ("tt", MOD[:, l, :, r], pt[:, 0:288].rearrange("p (m r) -> p m r", r=3)[:, :, r], BADA[:, l, :], ALU.add),
                 R=(pb, CONSTB), W=(GLOB,))
        for (lo, mul, add_) in ((16, 1.0, 1.0), (64, 1.0, 1.0), (32, 1.0 / ALPHA, 0.0), (80, 1.0 / ALPHA, 0.0)):
            P.op("dve", ("ts", MOD[:, l, lo:lo + 16, :], MOD[:, l, lo:lo + 16, :], mul, add_, ALU.mult, ALU.add),
                 R=(GLOB,), W=(GLOB,))
    def mod_ap(l, idx, k, r):
        return MOD[:, l, idx * 16 + k, r:r + 1]

    def mod0(b):
        dst = H1 if b == 0 else H1B
        for k in range(KC):
            i = k % 6
            load(XI[i], x0T[b][k], XIb[i])
            if k % 2 == 0:
                act(HOo[i][:, 0:T], XI[i][:, 0:T], AF.Identity, R=(XIb[i], GLOB), W=(HOb[i],),
                    bias=mod_ap(0, 0, k, b), scale=mod_ap(0, 1, k, b))
                act(HOo[i][:, T:NT], XI[i][:, T:NT], AF.Identity, R=(XIb[i], GLOB), W=(HOb[i],),
                    bias=mod_ap(0, 0, k, 2), scale=mod_ap(0, 1, k, 2))
            else:
                P.op("dve", ("ts", HOo[i][:, 0:T], XI[i][:, 0:T], mod_ap(0, 1, k, b), mod_ap(0, 0, k, b),
                             ALU.mult, ALU.add), R=(XIb[i], GLOB), W=(HOb[i],))
                P.op("dve", ("ts", HOo[i][:, T:NT], XI[i][:, T:NT], mod_ap(0, 1, k, 2), mod_ap(0, 0, k, 2),
                             ALU.mult, ALU.add), R=(XIb[i], GLOB), W=(HOb[i],))
            store(dst[k], HOo[i], HOb[i])

    ada_layer(0)
    XI = [A.alloc([NT], F32) for _ in range(6)]
    HOo = [A.alloc([NT], BF16) for _ in range(6)]
    XIb = [dbuf(f"xi{i}") for i in range(6)]
    HOb = [dbuf(f"ho{i}") for i in range(6)]
    for b_ in range(nb):
        mod0(b_)
    for l_ in range(1, nlayers):
        ada_layer(l_)
    for l in range(nlayers):
        for which in range(2):
            if which == 0:
                s_lo, b_lo, ls = 64, 48, l
            else:
                if l + 1 >= nlayers:
                    continue
                s_lo, b_lo, ls = 16, 0, l + 1
            for r in range(3):
                g_ap = LNP[:, l, which, 0, :]
                be_ap = LNP[:, l, which, 1, :]
                P.op("dve", ("tt", FS[:, l, which, :, r], MOD[:, ls, s_lo:s_lo + 16, r], g_ap, ALU.mult),
                     R=(GLOB, CONSTB), W=(GLOB,))
                P.op("dve", ("tt", FB[:, l, which, :, r], MOD[:, ls, s_lo:s_lo + 16, r], be_ap, ALU.mult),
                     R=(GLOB, CONSTB), W=(GLOB,))
                P.op("dve", ("tt", FB[:, l, which, :, r], FB[:, l, which, :, r], MOD[:, ls, b_lo:b_lo + 16, r], ALU.add),
                     R=(GLOB,), W=(GLOB,))
    if "MODD" in dump:
        MODb = dbuf("modd")
        P.op("sp", ("dma", MODD[:], MOD[:].rearrange("p l m r -> p (l m r)")), R=(GLOB, MODb), W=(), dma=MODb)
    end_phase()

    stopped = [False]

    def check_stop(l, b, ph):
        if stop_after is not None and (l, b, ph) == tuple(stop_after):
            stopped[0] = True
        return stopped[0]

    def ln_sched(DQ, Z, ZSQ, HO, STAT, w, l, which, r, zbuf, hbuf, sqbuf, stbuf, want_h, finish, delays, zk):
        pm, pq = psum[6], psum[7]
        MEAN = STAT[:, 0, 0:w]
        RSTD = STAT[:, 1, 0:w]
        two_pass = ZSQ is None

        def stepA():
            for hh in range(2):
                ks = slice(hh * 8, hh * 8 + 8)
                act(HO[:, ks, 0:w], Z[:, ks, 0:w], AF.Copy, R=[zbuf] + zk[ks], W=(hbuf,))
                if not two_pass:
                    act(ZSQ[:, ks, 0:w], Z[:, ks, 0:w], AF.Square, R=[zbuf] + zk[ks], W=(sqbuf,))

        def stepB():
            for k in range(KC):
                mm(pm[:, 0:w], ONESD, HO[:, k, 0:w], k == 0, k == KC - 1, R=(hbuf, CONSTB), W=(PS[6],))
            if not two_pass:
                for k in range(KC):
                    mm(pq[:, 0:w], ONESD, ZSQ[:, k, 0:w], k == 0, k == KC - 1, R=(sqbuf, CONSTB), W=(PS[7],))
            act(MEAN, pm[:, 0:w], AF.Copy, R=(PS[6],), W=(stbuf,))
            if two_pass:
                for hh in range(2):
                    ks = slice(hh * 8, hh * 8 + 8)
                    act(HO[:, ks, 0:w], Z[:, ks, 0:w], AF.Square, R=[zbuf] + zk[ks], W=(hbuf,))

        def stepC():
            if two_pass:
                for k in range(KC):
                    mm(pq[:, 0:w], ONESD, HO[:, k, 0:w], k == 0, k == KC - 1, R=(hbuf, CONSTB), W=(PS[7],))

        def stepD0():
            P.op("dve", ("tt", RSTD, MEAN, MEAN, ALU.mult), R=(stbuf,), W=(stbuf,))
            P.op("dve", ("tt", pq[:, 0:w], pq[:, 0:w], RSTD, ALU.subtract), R=(stbuf, PS[7]), W=(PS[7],))
            act(pq[:, 0:w], pq[:, 0:w], AF.Sqrt, R=(PS[7], GLOB), W=(PS[7],), bias=EPS[:, 0:1])
            P.op("dve", ("recip", pq[:, 0:w], pq[:, 0:w]), R=(PS[7],), W=(PS[7],))

        def stepDk(k0, k1):
            for k in range(k0, k1):
                P.op("dve", ("tt", Z[:, k, 0:w], Z[:, k, 0:w], pm[:, 0:w], ALU.subtract), R=(zk[k], PS[6]), W=(zk[k],))
                P.op("dve", ("tt", Z[:, k, 0:w], Z[:, k, 0:w], pq[:, 0:w], ALU.mult), R=(zk[k], PS[7]), W=(zk[k],))
                if want_h:
                    act(HO[:, k, 0:w], Z[:, k, 0:w], AF.Identity, R=(zk[k], GLOB), W=(hbuf,),
                        bias=FB[:, l, which, k, r:r + 1], scale=FS[:, l, which, k, r:r + 1])
                act(Z[:, k, 0:w], Z[:, k, 0:w], AF.Identity, R=(zk[k], CONSTB), W=(zk[k],),
                    bias=LNP[:, l, which, 1, k:k + 1], scale=LNP[:, l, which, 0, k:k + 1])
            if k1 == KC:
                finish()

        DQ.add(delays[0], stepA)
        DQ.add(delays[1], stepB)
        DQ.add(delays[2], stepC)
        DQ.add(delays[3], stepD0)
        kpt = delays[4]
        for i_, k0 in enumerate(range(0, KC, kpt)):
            DQ.add(delays[3] + 1 + i_, lambda k0=k0: stepDk(k0, k0 + kpt))

    if stop_after is not None and stop_after[2] == "ADA":
        stopped[0] = True
    for b in range(nb):
        if stopped[0]:
            break
        xsrc = x0T[b]
        for l in range(nlayers):
            if stopped[0]:
                break
            last = (l == last_layer_idx)
            tgs_q = TGS[:4] if last else TGS

            if b == 0:
                for f_ in pre_pending.pop(l, precast_steps(l) if l == 0 else []):
                    f_()
            HT = A.alloc([KC, NT], BF16)
            ROPE = A.alloc([2, T], F32)
            OUTC = [A.alloc([NT], BF16) for _ in range(2)]
            VST = [A.alloc([512], BF16) for _ in range(2)]
            XBb_t = [A.alloc([512], BF16) for _ in range(2)]
            T1 = [A.alloc([512], F32) for _ in range(2)]
            T2 = [A.alloc([512], F32) for _ in range(2)]
            HTb = [dbuf(f"ht{i}") for i in range(5)]
            ROPEb = dbuf("rope")
            OUTCb = [dbuf(f"outc{i}") for i in range(2)]
            VSTb = [dbuf(f"vst{i}") for i in range(2)]
            XBb = [Buf(f"xb{i}") for i in range(2)]
            T1b = [Buf(f"t1{i}") for i in range(2)]
            T2b = [Buf(f"t2{i}") for i in range(2)]
            for n_, (t0_, w_) in enumerate(TGS):
                for kh in range(2):
                    h1src = H1B if (l == 0 and b == 1) else H1
                    load(HT[:, kh * 8:kh * 8 + 8, t0_:t0_ + w_],
                         h1src[kh * 8:kh * 8 + 8, :, t0_:t0_ + w_].rearrange("k p t -> p k t"), HTb[n_])
                if n_ == 0:
                    load(ROPE[:], rope_in[:], ROPEb)
            wt, wb = wload(w_in[l, 0], SLOT)
            wv = wt[:].rearrange("p (k c) -> p k c", k=KC)
            for tt in range(NTT):
                bk = tt % 4
                for k in range(KC):
                    mm(psum[bk][:], HT[:, k, tt * 128:(tt + 1) * 128], wv[:, k, :], k == 0, k == KC - 1,
                       R=(wb, HTb[tt // 4]), W=(PS[bk],))
                i = tt % 2
                act(VST[i][:], psum[bk][:], AF.Copy, R=(PS[bk],), W=(VSTb[i],))
                store(VV[tt], VST[i], VSTb[i])
            tile_ctr = 0
            oc_ctr = 0
            pending = None

            def rope_tail(pd):
                (bk_, sw_, i_, oc_, t0_, w_, ocb_) = pd
                mm(psum[sw_][:, 0:w_], PSWAP, XBb_t[i_][:, 0:w_], True, True, R=(XBb[i_], CONSTB), W=(PS[sw_],))
                P.op("dve", ("tt", psum[bk_][:, 0:w_], psum[bk_][:, 0:w_], ROPE[:, 0, t0_:t0_ + w_], ALU.mult),
                     R=(PS[bk_], ROPEb, XBb[i_]), W=(PS[bk_],))
                P.op("dve", ("tt", T2[i_][:, 0:w_], psum[sw_][:, 0:w_], ROPE[:, 1, t0_:t0_ + w_], ALU.mult),
                     R=(PS[sw_], ROPEb), W=(T2b[i_],))
                P.op("dve", ("tt", oc_[:, t0_:t0_ + w_], psum[bk_][:, 0:w_], T2[i_][:, 0:w_], ALU.add),
                     R=(PS[bk_], T2b[i_]), W=(ocb_,))

            for G in range(1, 6):
                wt, wb = wload(w_in[l, G], SLOT)
                wv = wt[:].rearrange("p (k c) -> p k c", k=KC)
                for mi in range(4):
                    is_f = (G == 1)
                    is_k = (not is_f) and mi == 0
                    tgl = TGS if (is_k or not last) else TGS[:4]
                    oc = OUTC[oc_ctr % 2]
                    ocb = OUTCb[oc_ctr % 2]
                    oc_ctr += 1
                    for (t0, w) in tgl:
                        bk = tile_ctr % 4
                        for k in range(KC):
                            mm(psum[bk][:, 0:w], wv[:, k, mi * 128:(mi + 1) * 128], HT[:, k, t0:t0 + w],
                               k == 0, k == KC - 1, R=(wb, HTb[t0 // 512]), W=(PS[bk],))
                        if pending is not None:
                            rope_tail(pending)
                            pending = None
                        if is_f or t0 >= T:
                            act(oc[:, t0:t0 + w], psum[bk][:, 0:w], AF.Copy, R=(PS[bk],), W=(ocb,))
                        else:
                            i = tile_ctr % 2
                            act(XBb_t[i][:, 0:w], psum[bk][:, 0:w], AF.Copy, R=(PS[bk],), W=(XBb[i],))
                            pending = (bk, 4 + i, i, oc, t0, w, ocb)
                        tile_ctr += 1
                    if pending is not None:
                        rope_tail(pending)
                        pending = None
                    wcols = tgl[-1][0] + tgl[-1][1]
                    if is_f:
                        dst = FT[mi]
                    elif is_k:
                        dst = KT[G - 2]
                    else:
                        dst = QT[3 * (G - 2) + mi - 1]
                    store(dst[:, 0:wcols], oc[:, 0:wcols], ocb)
            end_phase()
            if check_stop(l, b, "P2"):
                break

            KTs = [A.alloc([NT], BF16) for _ in range(2)]
            QTs = [A.alloc([3, NT], BF16) for _ in range(2)]
            VH = [A.alloc([NTT, 128], BF16) for _ in range(2)]
            OT = [A.alloc([3, NT], BF16) for _ in range(2)]
            PT = [A.alloc([384], BF16) for _ in range(4)]
            REC = [A.alloc([384], F32) for _ in range(2)]
            KQVb = [dbuf(f"kqv{i}") for i in range(2)]
            OTb = [dbuf(f"ot{i}") for i in range(2)]
            PTb = [Buf(f"pt{i}") for i in range(4)]
            RECb = [Buf(f"rec{i}") for i in range(2)]
            RECb2 = [Buf(f"recb{i}") for i in range(2)]
            nqb = 16 if last else 18
            pt_ctr = 0
            s_ctr = 0
            pend_norm = []
            LA = 2
            wq = T if last else NT

            def p3_load(h_):
                j2 = h_ % 2
                load(KTs[j2], KT[h_], KQVb[j2])
                load(QTs[j2][:, :, 0:wq], QT[3 * h_:3 * h_ + 3, :, 0:wq].rearrange("g p t -> p g t"), KQVb[j2])
                load(VH[j2], VV[:, :, h_ * 128:(h_ + 1) * 128].rearrange("t p c -> p t c"), KQVb[j2])

            p3_load(0)
            items = []
            for h in range(NKV):
                for qb in range(nqb):
                    if qb < 16:
                        keys = []
                        if qb > 0:
                            keys.append((qb - 1, MASKLO))
                        keys.append((qb, None))
                        if qb < 15:
                            keys.append((qb + 1, MASKUP))
                        keys += [(16, None), (17, None)]
                    else:
                        keys = [(16, None), (17, None)]
                    for ii, (kb, mask) in enumerate(keys):
                        items.append((h, qb, ii, kb, mask, len(keys)))
            ptis = {}

            def norm(h, qb):
                i2 = h % 2
                ob = qb % 2
                po, psm = psum[3 + ob], psum[5 + ob]
                pob, psb = PS[3 + ob], PS[5 + ob]
                rc = REC[ob]
                for g in range(3):
                    hq = 3 * h + g
                    P.op("dve", ("ts", rc[:, g * 128:(g + 1) * 128], psm[:, g * 128:(g + 1) * 128],
                                 ES[:, l * NQ + hq:l * NQ + hq + 1], None, ALU.add, ALU.bypass),
                         R=(psb, GLOB), W=(RECb[ob], RECb2[ob]))
                act(rc[:, 192:384], rc[:, 192:384], AF.Ln, R=(RECb2[ob],), W=(RECb2[ob],))
                act(rc[:, 192:384], rc[:, 192:384], AF.Exp, R=(RECb2[ob],), W=(RECb2[ob],), scale=-1.0)
                P.op("dve", ("recip", rc[:, 0:192], rc[:, 0:192]), R=(RECb[ob],), W=(RECb[ob],))
                P.op("dve", ("tt", OT[i2][:, :, qb * 128:(qb + 1) * 128],
                             po[:, 0:384].rearrange("p (g q) -> p g q", g=3),
                             rc[:].rearrange("p (g q) -> p g q", g=3), ALU.mult),
                     R=(pob, RECb[ob], RECb2[ob]), W=(OTb[i2],))
                if qb == nqb - 1:
                    store(MIX[4 + 3 * h:4 + 3 * h + 3, :, 0:wq].rearrange("g p t -> p g t"), OT[i2][:, :, 0:wq], OTb[i2])

            pend_n = []
            for n in range(len(items) + LA):
                if n < len(items):
                    (h, qb, ii, kb, mask, nk) = items[n]
                    i2 = h % 2
                    sb_ = s_ctr % 3
                    s_ctr += 1
                    qsl = QTs[i2][:, :, qb * 128:(qb + 1) * 128]
                    mm(psum[sb_][:, 0:384], KTs[i2][:, kb * 128:(kb + 1) * 128], qsl, True, mask is None,
                       R=(KQVb[i2],), W=(PS[sb_],))
                    if mask is not None:
                        mm(psum[sb_][:, 0:384], IDENT, mask, False, True, R=(CONSTB,), W=(PS[sb_],))
                    pti = pt_ctr % 4
                    pt_ctr += 1
                    ptis[n] = pti
                    act(PT[pti][:], psum[sb_][:, 0:384], AF.Exp, R=(PS[sb_],), W=(PTb[pti],), scale=ATTN_SCALE)
                m_ = n - LA
                if m_ >= 0:
                    (h, qb, ii, kb, mask, nk) = items[m_]
                    i2 = h % 2
                    ob = qb % 2
                    if qb == 0 and ii == 0 and h + 1 < NKV:
                        p3_load(h + 1)
                    pti = ptis.pop(m_)
                    mm(psum[3 + ob][:, 0:384], VH[i2][:, kb, :], PT[pti][:], ii == 0, ii == nk - 1,
                       R=(KQVb[i2], PTb[pti]), W=(PS[3 + ob],))
                    mm(psum[5 + ob][:, 0:384], ONES, PT[pti][:], ii == 0, ii == nk - 1,
                       R=(CONSTB, PTb[pti]), W=(PS[5 + ob],))
                    if ii == 1 and pend_n:
                        norm(*pend_n.pop(0))
                    if ii == nk - 1:
                        pend_n.append((h, qb))
            while pend_n:
                norm(*pend_n.pop(0))
            end_phase()
            if check_stop(l, b, "P3"):
                break

            FTs = A.alloc([4, NT], BF16)
            ZCS = A.alloc([NTT, 4, 256], BF16)
            YT = A.alloc([4, NT], BF16)
            OUTC = [A.alloc([NT], BF16) for _ in range(2)]
            DFC = A.alloc([2, 2, 256], BF16)
            FTb = dbuf("fts")
            DFCb = dbuf("dfc")
            ZCSb = Buf("zcs")
            YTb = Buf("yt")
            OUTCb = [dbuf(f"outc{i}") for i in range(2)]
            ntt_f = 16 if last else NTT
            wf = T if last else NT
            load(FTs[:, :, 0:wf], FT[:, :, 0:wf].rearrange("g p t -> p g t"), FTb)
            load(DFC[:], dftc_in[:], DFCb)
            for tt in range(ntt_f):
                for gp in range(2):
                    bk = (tt * 2 + gp) % 4
                    for gi in range(2):
                        g = gp * 2 + gi
                        mm(psum[bk][:, gi * 256:(gi + 1) * 256], FTs[:, g, tt * 128:(tt + 1) * 128], C128S, True, True,
                           R=(FTb, CONSTB), W=(PS[bk],))
                    act(ZCS[:, tt, gp * 2:gp * 2 + 2, :], psum[bk][:].rearrange("p (g c) -> p g c", g=2), AF.Copy,
                        R=(PS[bk],), W=(ZCSb,))
            tile_ctr = 0
            for Gt in range(8):
                wt, wb = wload(dft_in[Gt], SLOT)
                wv = wt[:].rearrange("p (k s c) -> p k s c", k=KC, s=2)
                for g in range(4):
                    bk = tile_ctr % 4
                    tile_ctr += 1
                    for k in range(KC):
                        mm(psum[bk][:, 0:256], ZCS[:, k, g, 0:128], wv[:, k, 0, :], k == 0, False,
                           R=(ZCSb, wb), W=(PS[bk],))
                        mm(psum[bk][:, 0:256], ZCS[:, k, g, 128:256], wv[:, k, 1, :], False, k == KC - 1,
                           R=(ZCSb, wb), W=(PS[bk],))
                    act(YT[:, g, Gt * 256:(Gt + 1) * 256], psum[bk][:, 0:256], AF.Copy, R=(PS[bk],), W=(YTb,))
            if not last:
                for g in range(4):
                    bk = tile_ctr % 4
                    tile_ctr += 1
                    for k in range(2):
                        mm(psum[bk][:, 0:256], ZCS[:, 16 + k, g, 0:128], DFC[:, k, 0, :], k == 0, False,
                           R=(ZCSb, DFCb), W=(PS[bk],))
                        mm(psum[bk][:, 0:256], ZCS[:, 16 + k, g, 128:256], DFC[:, k, 1, :], False, k == 1,
                           R=(ZCSb, DFCb), W=(PS[bk],))
                    act(YT[:, g, T:NT], psum[bk][:, 0:256], AF.Copy, R=(PS[bk],), W=(YTb,))
            wt, wb = wload(w_four[l], 2048)
            wv = wt[:, 0:2048].rearrange("p (k c) -> p k c", k=4)
            for mi in range(4):
                oc = OUTC[mi % 2]
                ocb = OUTCb[mi % 2]
                for (t0, w) in tgs_q:
                    bk = tile_ctr % 4
                    tile_ctr += 1
                    for k in range(4):
                        mm(psum[bk][:, 0:w], wv[:, k, mi * 128:(mi + 1) * 128], YT[:, k, t0:t0 + w], k == 0, k == 3,
                           R=(wb, YTb), W=(PS[bk],))
                    act(oc[:, t0:t0 + w], psum[bk][:, 0:w], AF.Copy, R=(PS[bk],), W=(ocb,))
                store(MIX[mi][:, 0:wf], oc[:, 0:wf], ocb)
            end_phase()
            if check_stop(l, b, "P4"):
                break

            xdst = XB
            MX = [A.alloc([KC, 512], BF16) for _ in range(2)]
            Zt = [A.alloc([KC, 512], F32) for _ in range(2)]
            ZB2 = [A.alloc([512], BF16) for _ in range(4)]
            SQ2 = [A.alloc([512], BF16) for _ in range(4)]
            HO = A.alloc([KC, 512], BF16)
            STAT = A.alloc([2, 512], F32)
            MXb = [dbuf(f"mx{i}") for i in range(2)]
            Zb = [dbuf(f"z{i}") for i in range(2)]
            ZK = [[Buf(f"zk{i}_{k}") for k in range(KC)] for i in range(2)]
            ZB2b = [Buf(f"zb2{i}") for i in range(4)]
            SQ2b = [Buf(f"sq2{i}") for i in range(4)]
            HOb = dbuf("ho")
            STb = Buf("stat")
            tile_ctr = 0
            DQ = Deferred()
            ntg = len(tgs_q)

            def p5_load_mx(ti_):
                t0_, w_ = tgs_q[ti_]
                load(MX[ti_ % 2][:, :, 0:w_], MIX[:, :, t0_:t0_ + w_].rearrange("k p t -> p k t"), MXb[ti_ % 2])

            def p5_load_z(ti_, src_):
                t0_, w_ = tgs_q[ti_]
                load(Zt[ti_ % 2][:, :, 0:w_], src_[:, :, t0_:t0_ + w_].rearrange("k p t -> p k t"), Zb[ti_ % 2])

            p5_load_mx(0)
            p5_load_z(0, xsrc)
            if ntg > 1:
                p5_load_z(1, xsrc)
            st_ctr = 0
            for ti, (t0, w) in enumerate(tgs_q):
                i2 = ti % 2
                r = b if t0 < T else 2
                sp_ = 4 + 2 * (ti % 2)
                pm, pq = psum[sp_], psum[sp_ + 1]
                pmb, pqb = PS[sp_], PS[sp_ + 1]
                if ti + 1 < ntg:
                    p5_load_mx(ti + 1)
                for G in range(4):
                    wt, wb = wload(WO16[l, G], SLOT, extra_R=(PRE[l],))
                    wv = wt[:].rearrange("p (k c) -> p k c", k=KC)
                    for mi in range(4):
                        m = G * 4 + mi
                        bk = tile_ctr % 4
                        tile_ctr += 1
                        for k in range(KC):
                            mm(psum[bk][:, 0:w], wv[:, k, mi * 128:(mi + 1) * 128], MX[i2][:, k, 0:w],
                               k == 0, k == KC - 1, R=(wb, MXb[i2]), W=(PS[bk],))
                        P.op("dve", ("stt", Zt[i2][:, m, 0:w], psum[bk][:, 0:w], mod_ap(l, 2, m, r),
                                     Zt[i2][:, m, 0:w], ALU.mult, ALU.add),
                             R=(PS[bk], Zb[i2], ZK[i2][m], GLOB), W=(ZK[i2][m],))
                        si = st_ctr % 4
                        st_ctr += 1
                        act(ZB2[si][:, 0:w], Zt[i2][:, m, 0:w], AF.Copy, R=(ZK[i2][m],), W=(ZB2b[si],))
                        act(SQ2[si][:, 0:w], Zt[i2][:, m, 0:w], AF.Square, R=(ZK[i2][m],), W=(SQ2b[si],))

                        def stat_mm(m=m, si=si, w=w, pm=pm, pq=pq, pmb=pmb, pqb=pqb):
                            mm(pm[:, 0:w], ONESD, ZB2[si][:, 0:w], m == 0, m == KC - 1, R=(ZB2b[si], CONSTB), W=(pmb,))
                            mm(pq[:, 0:w], ONESD, SQ2[si][:, 0:w], m == 0, m == KC - 1, R=(SQ2b[si], CONSTB), W=(pqb,))

                        DQ.tick()
                        DQ.add(2, stat_mm)

                def ln_tail(ti=ti, t0=t0, w=w, i2=i2, r=r, pm=pm, pq=pq, pmb=pmb, pqb=pqb, src_=xsrc):
                    Z = Zt[i2]
                    zk = ZK[i2]
                    MEAN = STAT[:, 0, 0:w]
                    RSTD = STAT[:, 1, 0:w]

                    def d0():
                        act(MEAN, pm[:, 0:w], AF.Copy, R=(pmb,), W=(STb,))
                        P.op("dve", ("tt", RSTD, MEAN, MEAN, ALU.mult), R=(STb,), W=(STb,))
                        P.op("dve", ("tt", pq[:, 0:w], pq[:, 0:w], RSTD, ALU.subtract), R=(STb, pqb), W=(pqb,))
                        act(pq[:, 0:w], pq[:, 0:w], AF.Sqrt, R=(pqb, GLOB), W=(pqb,), bias=EPS[:, 0:1])
                        P.op("dve", ("recip", pq[:, 0:w], pq[:, 0:w]), R=(pqb,), W=(pqb,))

                    def dk(k0, k1):
                        for k in range(k0, k1):
                            P.op("dve", ("tt", Z[:, k, 0:w], Z[:, k, 0:w], pm[:, 0:w], ALU.subtract), R=(zk[k], pmb), W=(zk[k],))
                            P.op("dve", ("tt", Z[:, k, 0:w], Z[:, k, 0:w], pq[:, 0:w], ALU.mult), R=(zk[k], pqb), W=(zk[k],))
                            act(HO[:, k, 0:w], Z[:, k, 0:w], AF.Identity, R=(zk[k], GLOB), W=(HOb,),
                                bias=FB[:, l, 0, k, r:r + 1], scale=FS[:, l, 0, k, r:r + 1])
                            act(Z[:, k, 0:w], Z[:, k, 0:w], AF.Identity, R=(zk[k], CONSTB), W=(zk[k],),
                                bias=LNP[:, l, 0, 1, k:k + 1], scale=LNP[:, l, 0, 0, k:k + 1])
                        if k1 == KC:
                            P.op("sp", ("dma", xdst[:, :, t0:t0 + w].rearrange("k p t -> p k t"), Z[:, :, 0:w]),
                                 R=[Zb[i2]] + zk, W=(), dma=Zb[i2])
                            store(H2[:, :, t0:t0 + w].rearrange("k p t -> p k t"), HO[:, :, 0:w], HOb)
                            if ti + 2 < ntg:
                                p5_load_z(ti + 2, src_)

                    DQ.add(3, d0)
                    for i_, k0 in enumerate(range(0, KC, 2)):
                        DQ.add(4 + i_, lambda k0=k0: dk(k0, k0 + 2))

                ln_tail()
            DQ.flush()
            end_phase()
            xsrc = xdst
            if check_stop(l, b, "P5"):
                break

            CW = 2308
            LAT0, CTX0 = 1, 2051
            HT = A.alloc([KC, NT], BF16)
            CAt = [A.alloc([CW], F32) for _ in range(2)]
            CGt = [A.alloc([CW], F32) for _ in range(2)]
            RO = [A.alloc([NT], BF16) for _ in range(2)]
            HTb = [dbuf(f"ht{i}") for i in range(5)]
            ROb = [dbuf(f"ro{i}") for i in range(2)]
            CAb = [[Buf(f"ca{i}_{n}") for n in range(5)] for i in range(2)]
            CGb = [[Buf(f"cg{i}_{n}") for n in range(5)] for i in range(2)]
            wq = T if last else NT
            for n_, (t0_, w_) in enumerate(tgs_q):
                for kh in range(2):
                    load(HT[:, kh * 8:kh * 8 + 8, t0_:t0_ + w_],
                         H2[kh * 8:kh * 8 + 8, :, t0_:t0_ + w_].rearrange("k p t -> p k t"), HTb[n_])

            def ccol(t0):
                return (LAT0 + t0) if t0 < T else (CTX0 + t0 - T)

            tile_ctr = 0
            if b == 0 and l + 1 < nlayers:
                pre_pending[l + 1] = precast_steps(l + 1)
            for G in range(22):
                if b == 0 and pre_pending.get(l + 1):
                    pre_pending[l + 1].pop(0)()
                wt, wb = wload(w_up[l, G], SLOT)
                wv = wt[:].rearrange("p (j a k c) -> p j a k c", j=2, a=2, k=KC)
                for jj in range(2):
                    j = 2 * G + jj
                    i2 = j % 2
                    streams = ((0, CAt[i2], CAb[i2], j, 0), (1, CGt[i2], CGb[i2], NJ + j, 4))
                    pend = None

                    def taps(pd):
                        (n_, t0_, w_, bka_) = pd
                        for (ag, Ct, Cb, ch, bo) in streams:
                            c0 = ccol(t0_)
                            nb_ = [Cb[n_]]
                            if t0_ < T:
                                if n_ > 0:
                                    nb_.append(Cb[n_ - 1])
                                if n_ < 3:
                                    nb_.append(Cb[n_ + 1])
                            pt_ = psum[bo + bka_]
                            P.op("dve", ("stt", Ct[:, c0 + 1:c0 + 1 + w_], pt_[:, 0:w_], CONV[:, l, 0, ch:ch + 1],
                                         Ct[:, c0 + 1:c0 + 1 + w_], ALU.mult, ALU.add),
                                 R=[PS[bo + bka_], CONSTB] + nb_, W=nb_)
                            P.op("dve", ("stt", Ct[:, c0 - 1:c0 - 1 + w_], pt_[:, 0:w_], CONV[:, l, 2, ch:ch + 1],
                                         Ct[:, c0 - 1:c0 - 1 + w_], ALU.mult, ALU.add),
                                 R=[PS[bo + bka_], CONSTB] + nb_, W=nb_)

                    for n, (t0, w) in enumerate(tgs_q):
                        bka = tile_ctr % 4
                        tile_ctr += 1
                        for (ag, Ct, Cb, ch, bo) in streams:
                            for k in range(KC):
                                mm(psum[bo + bka][:, 0:w], wv[:, jj, ag, k, :], HT[:, k, t0:t0 + w], k == 0, k == KC - 1,
                                   R=(wb, HTb[n]), W=(PS[bo + bka],))
                        for (ag, Ct, Cb, ch, bo) in streams:
                            c0 = ccol(t0)
                            act(Ct[:, c0:c0 + w], psum[bo + bka][:, 0:w], AF.Identity, R=(PS[bo + bka], CONSTB), W=(Cb[n],),
                                bias=CONV[:, l, 3, ch:ch + 1], scale=CONV[:, l, 1, ch:ch + 1])
                        if pend is not None and t0 < T:
                            taps(pend)
                            pend = None
                        if pend is not None:
                            taps(pend)
                            pend = None
                        pend = (n, t0, w, bka)
                    taps(pend)
                    pend = None
                    allb_a = CAb[i2][:len(tgs_q)]
                    allb_g = CGb[i2][:len(tgs_q)]
                    segs = [(LAT0, 0, T)] + ([] if last else [(CTX0, T, CL)])
                    for (c0, o0, w) in segs:
                        act(CGt[i2][:, c0:c0 + w], CGt[i2][:, c0:c0 + w], AF.Silu, R=allb_g, W=allb_g)
                    for (c0, o0, w) in segs:
                        P.op("dve", ("tt", RO[i2][:, o0:o0 + w], CGt[i2][:, c0:c0 + w], CAt[i2][:, c0:c0 + w], ALU.mult),
                             R=allb_g + allb_a, W=(ROb[i2],))
                    store(RR[j][:, 0:wq], RO[i2][:, 0:wq], ROb[i2])
            end_phase()
            if check_stop(l, b, "P6"):
                break

            xdst = XA
            RTh = [A.alloc([22, 512], BF16) for _ in range(2)]
            Zt = [A.alloc([KC, 512], F32) for _ in range(2)]
            HO = A.alloc([KC, 512], BF16)
            STAT = A.alloc([2, 512], F32)
            RTb = [dbuf(f"rt{i}") for i in range(2)]
            Zb = [dbuf(f"z{i}") for i in range(2)]
            ZK = [[Buf(f"zk{i}_{k}") for k in range(KC)] for i in range(2)]
            HOb = dbuf("ho")
            STb = Buf("stat")
            tile_ctr = 0
            DQ = Deferred()
            ntg = len(tgs_q)
            halves = [(ti, jh) for ti in range(ntg) for jh in range(2)]
            want_h7 = (l + 1 < nlayers)

            def p7_load_half(idx_):
                ti_, jh_ = halves[idx_]
                t0_, w_ = tgs_q[ti_]
                bi = idx_ % 2
                for jq in range(2):
                    j0 = jh_ * 22 + jq * 11
                    load(RTh[bi][:, jq * 11:(jq + 1) * 11, 0:w_],
                         RR[j0:j0 + 11, :, t0_:t0_ + w_].rearrange("j p t -> p j t"), RTb[bi])

            def p7_load_z(ti_, src_):
                t0_, w_ = tgs_q[ti_]
                load(Zt[ti_ % 2][:, :, 0:w_], src_[:, :, t0_:t0_ + w_].rearrange("k p t -> p k t"), Zb[ti_ % 2])

            p7_load_half(0)
            p7_load_z(0, xsrc)
            if ntg > 1:
                p7_load_z(1, xsrc)
            for idx, (ti, jh) in enumerate(halves):
                t0, w = tgs_q[ti]
                i2 = ti % 2
                r = b if t0 < T else 2
                if idx + 1 < len(halves):
                    p7_load_half(idx + 1)
                for mp in range(8):
                    wt, wb = wload(WD16[l, jh, mp], 5632, extra_R=(PRE[l],))
                    wv = wt[:, 0:5632].rearrange("p (m j c) -> p m j c", m=2, j=22)
                    for mi in range(2):
                        m = mp * 2 + mi
                        bk = tile_ctr % 6
                        tile_ctr += 1
                        for j in range(22):
                            mm(psum[bk][:, 0:w], wv[:, mi, j, :], RTh[idx % 2][:, j, 0:w], j == 0, j == 21,
                               R=(wb, RTb[idx % 2]), W=(PS[bk],))
                        P.op("dve", ("stt", Zt[i2][:, m, 0:w], psum[bk][:, 0:w], mod_ap(l, 5, m, r),
                                     Zt[i2][:, m, 0:w], ALU.mult, ALU.add),
                             R=(PS[bk], Zb[i2], ZK[i2][m], GLOB), W=(ZK[i2][m],))
                        DQ.tick()
                if jh == 1:
                    def fin(ti=ti, t0=t0, w=w, i2=i2, src_=xsrc):
                        dst_ = outT[b] if last else xdst
                        P.op("sp", ("dma", dst_[:, :, t0:t0 + w].rearrange("k p t -> p k t"), Zt[i2][:, :, 0:w]),
                             R=[Zb[i2]] + ZK[i2], W=(), dma=Zb[i2])
                        if not last:
                            if want_h7:
                                store(H1[:, :, t0:t0 + w].rearrange("k p t -> p k t"), HO[:, :, 0:w], HOb)
                        if ti + 2 < ntg:
                            p7_load_z(ti + 2, src_)

                    ln_sched(DQ, Zt[i2], None, HO, STAT, w, l, 1, r, Zb[i2], HOb, None, STb, want_h7, fin, (0, 3, 7, 8, 1), ZK[i2])
            DQ.flush()
            end_phase()
            xsrc = xdst
            if check_stop(l, b, "P7"):
                break

    P.op("sp", ("nop",))
    P.op("act", ("nop",))

    P.finalize(nc, engsem)
    assert _deadlock_check(P), 'deadlock in sync plan'
    with sem_stack, nc.Block() as block:
        @block.tensor
        def _(e):
            P.emit("pe", e, engsem)

        @block.scalar
        def _(e):
            P.emit("act", e, engsem)

        @block.vector
        def _(e):
            P.emit("dve", e, engsem)

        @block.gpsimd
        def _(e):
            P.emit("pool", e, engsem)

        @block.sync
        def _(e):
            P.emit("sp", e, engsem)
    return nc, len(P.ops)


def _const_tables():
    bf = ml_dtypes.bfloat16
    cb = np.zeros((128, 2176), np.float32)
    cb[:, 0:128] = np.eye(128)
    cb[:, 128:256] = 1.0
    cb[:, 256:384] = 1.0 / D
    ps = np.zeros((128, 128), np.float32)
    for i in range(32):
        ps[32 + i, i] = -1.0
        ps[i, 32 + i] = 1.0
        ps[96 + i, 64 + i] = -1.0
        ps[64 + i, 96 + i] = 1.0
    cb[:, 384:512] = ps
    kk = np.arange(128)[:, None]
    qq = np.arange(128)[None, :]
    lo = np.where(kk >= qq, 0.0, NEG).astype(np.float32)
    up = np.where(kk <= qq, 0.0, NEG).astype(np.float32)
    cb[:, 512:896] = np.tile(lo, (1, 3))
    cb[:, 896:1280] = np.tile(up, (1, 3))
    c = np.arange(128, dtype=np.float64)
    ang = 2 * np.pi * np.outer(c, c) / 128.0
    cb[:, 1280:1408] = np.cos(ang) / np.sqrt(128.0)
    cb[:, 1408:1536] = -np.sin(ang) / np.sqrt(128.0)
    cbf = cb.astype(bf)
    inv = (10000.0 ** (-np.arange(0, 64, 2, dtype=np.float32) / np.float32(64))).astype(np.float32)
    row = np.repeat(np.arange(T // 64, dtype=np.float32), 64)
    col = np.tile(np.arange(64, dtype=np.float32), T // 64)
    ar = (row[None, :] * inv[:, None]).astype(np.float32)
    ac = (col[None, :] * inv[:, None]).astype(np.float32)
    angp = np.concatenate([ar, ar, ac, ac], axis=0)
    rope = np.stack([np.cos(angp), np.sin(angp)], axis=1).astype(np.float32)
    t = np.arange(T, dtype=np.float64)
    full = 2 * np.pi * ((np.outer(t, t)) % T) / T
    Cm = (np.cos(full) / np.sqrt(T)).astype(np.float32)
    Sm = (np.sin(full) / np.sqrt(T)).astype(np.float32)
    dft = np.zeros((8, 128, KC, 2, 256), np.float32)
    for G in range(8):
        cs = Cm[:, G * 256:(G + 1) * 256].reshape(KC, 128, 256).transpose(1, 0, 2)
        ss = Sm[:, G * 256:(G + 1) * 256].reshape(KC, 128, 256).transpose(1, 0, 2)
        dft[G, :, :, 0, :] = cs
        dft[G, :, :, 1, :] = ss
    dft = dft.reshape(8, 128, SLOT).astype(bf)
    tc = np.arange(CL, dtype=np.float64)
    fc = 2 * np.pi * (np.outer(tc, tc) % CL) / CL
    Cc = (np.cos(fc) / np.sqrt(CL)).astype(np.float32).reshape(2, 128, 256).transpose(1, 0, 2)
    Sc = (np.sin(fc) / np.sqrt(CL)).astype(np.float32).reshape(2, 128, 256).transpose(1, 0, 2)
    dftc = np.stack([Cc, Sc], axis=2).reshape(128, 1024).astype(bf)
    return cbf, rope, dft, dftc


def _layout_weights(w_ada, b_ada, w_in, sink, w_four, w_out, ln_g, ln_b, w_up, conv_w, conv_b, w_down, nl):
    f = np.float32
    res = {}
    wa = np.asarray(w_ada[:nl], f).reshape(nl, KC, 128, 24, 4, 128)
    res["w_ada_t"] = np.ascontiguousarray(wa.transpose(0, 3, 2, 4, 1, 5)).reshape(nl, 24, 128, SLOT)
    res["b_ada_t"] = np.ascontiguousarray(np.asarray(b_ada[:nl], f).reshape(nl, 96, 128).transpose(2, 0, 1))
    wi = np.asarray(w_in[:nl], f)
    cols = []
    cols.append(np.arange(2560, 3072))
    cols.append(np.arange(0, 512))
    for h in range(NKV):
        cc = [np.arange(2048 + h * 128, 2048 + (h + 1) * 128)]
        for g in range(3):
            hq = 3 * h + g
            cc.append(np.arange(512 + hq * 128, 512 + (hq + 1) * 128))
        cols.append(np.concatenate(cc))
    wit = np.empty((nl, 6, 128, KC, 512), f)
    for G in range(6):
        wit[:, G] = wi[:, :, cols[G]].reshape(nl, KC, 128, 512).transpose(0, 2, 1, 3)
    res["w_in_t"] = wit.reshape(nl, 6, 128, SLOT)
    res["w_four_t"] = np.ascontiguousarray(
        np.asarray(w_four[:nl], f).reshape(nl, 4, 128, 512).transpose(0, 2, 1, 3)).reshape(nl, 128, 2048)
    wo = np.asarray(w_out[:nl], f).reshape(nl, KC, 128, 4, 512)
    res["w_out_t"] = np.ascontiguousarray(wo.transpose(0, 3, 2, 1, 4)).reshape(nl, 4, 128, SLOT)
    wu = np.asarray(w_up[:nl], f).reshape(nl, KC, 128, 2, 22, 2, 128)
    res["w_up_t"] = np.ascontiguousarray(wu.transpose(0, 4, 2, 5, 3, 1, 6)).reshape(nl, 22, 128, SLOT)
    wd = np.asarray(w_down[:nl], f).reshape(nl, NJ, 128, KC, 128)
    res["w_down_t"] = np.ascontiguousarray(wd.transpose(0, 3, 2, 1, 4)).reshape(nl, KC, 128, NJ * 128)
    cv = np.concatenate([np.asarray(conv_w[:nl], f), np.asarray(conv_b[:nl], f)[:, None, :]], axis=1)
    res["conv_t"] = np.ascontiguousarray(cv.reshape(nl, 4, 88, 128).transpose(3, 0, 1, 2))
    lg = np.stack([np.asarray(ln_g[:nl], f), np.asarray(ln_b[:nl], f)], axis=2)
    res["ln_t"] = np.ascontiguousarray(lg.reshape(nl, 2, 2, KC, 128).transpose(4, 0, 1, 2, 3))
    res["sink_t"] = np.ascontiguousarray(np.broadcast_to(np.asarray(sink[:nl], f).reshape(1, nl * NQ), (128, nl * NQ)))
    return res


def _core_inputs(x, c, ctx, c_ctx, b0, nbc):
    xs = np.asarray(x[b0:b0 + nbc], np.float32)
    cs = np.asarray(ctx[b0:b0 + nbc], np.float32)
    xa = np.concatenate([xs, cs], axis=1)
    x0T = np.ascontiguousarray(xa.reshape(nbc, NT, KC, 128).transpose(0, 2, 3, 1))
    rows = [np.asarray(c[b0 + i], np.float32) for i in range(nbc)]
    while len(rows) < 2:
        rows.append(rows[-1])
    rows.append(np.asarray(c_ctx, np.float32))
    cc = np.stack(rows, axis=0)
    cT = np.ascontiguousarray(cc.reshape(3, KC, 128).transpose(2, 1, 0))
    return x0T, cT


_CACHE = {}


def kernel(x, c, ctx, c_ctx, w_ada, b_ada, w_in, sink, w_four, w_out, ln_g, ln_b,
           w_up, conv_w, conv_b, w_down):
    ncores = 8
    nbc = 2
    if "nc" not in _CACHE:
        _CACHE["nc"] = build_nc(L_ALL, nbc)[0]
        _CACHE["consts"] = _const_tables()
    nc = _CACHE["nc"]
    cbf, rope, dft, dftc = _CACHE["consts"]
    wl = _layout_weights(w_ada, b_ada, w_in, sink, w_four, w_out, ln_g, ln_b, w_up, conv_w, conv_b, w_down, L_ALL)
    in_maps = []
    for ci in range(ncores):
        x0T, cT = _core_inputs(x, c, ctx, c_ctx, ci * nbc, nbc)
        m = dict(wl)
        m.update({"x0T": x0T, "cT": cT, "rope_t": rope, "cbf_t": cbf, "dft_t": dft, "dftc_t": dftc})
        in_maps.append(m)
    res = run_bass_kernel_spmd(nc, in_maps, core_ids=list(range(ncores)))
    out = np.empty((ncores * nbc, T, D), np.float32)
    for ci in range(ncores):
        o = res.results[ci]["outT"]
        out[ci * nbc:(ci + 1) * nbc] = o.transpose(0, 3, 1, 2).reshape(nbc, T, D)
    return out


def _deadlock_check(P):
    engs = {}
    for o in P.ops:
        engs.setdefault(o.eng, []).append(o)
    pos = {e: 0 for e in engs}
    sem = {}
    dsem = {}
    progress = True
    while progress:
        progress = False
        for e, lst in engs.items():
            while pos[e] < len(lst):
                o = lst[pos[e]]
                ok = all(sem.get(p.eng, 0) >= p.ms for p in o.cdeps) and \
                    all(dsem.get(ds.id, 0) >= val for (ds, val) in o.ddeps)
                if not ok:
                    break
                if o.ds is not None:
                    dsem[o.ds.id] = dsem.get(o.ds.id, 0) + 16
                elif o.need:
                    sem[e] = sem.get(e, 0) + 1
                    assert sem[e] == o.ms, (e, sem[e], o.ms)
                pos[e] += 1
                progress = True
    stuck = {e: (pos[e], len(lst)) for e, lst in engs.items() if pos[e] < len(lst)}
    for e, (p_, n_) in stuck.items():
        o = engs[e][p_]
        print("STUCK", e, p_, n_, o.ins[0], [(p.eng, p.ms, sem.get(p.eng, 0)) for p in o.cdeps],
              [(ds.id, val, dsem.get(ds.id, 0)) for (ds, val) in o.ddeps])
    return not stuck
```

```python
import contextlib
import os
import numpy as np
import ml_dtypes
import concourse.bass as bass
import concourse.mybir as mybir
from concourse.bass_utils import run_bass_kernel_spmd

F32 = mybir.dt.float32
BF16 = mybir.dt.bfloat16
AF = mybir.ActivationFunctionType
ALU = mybir.AluOpType

D = 2048
KC = 16
T = 2048
CL = 256
NT = T + CL
NTT = NT // 128
DFF = 5632
NJ = DFF // 128
L_ALL = 4
NQ = 12
NKV = 4
ALPHA = (2 * L_ALL) ** 0.25
LN_EPS = 1e-5
EPSP = LN_EPS / (ALPHA * ALPHA)
ATTN_SCALE = 128 ** -0.5
NEG = -30000.0
TGS = [(0, 512), (512, 512), (1024, 512), (1536, 512), (2048, 256)]
SLOT = 8192
NSLOT = 3
ARENA_BYTES = 136 * 1024
SAME_ENGINE_SYNC = True


class DmaSem:
    __slots__ = ("sem", "count", "id")

    def __init__(self, sem, i):
        self.sem = sem
        self.count = 0
        self.id = i


class Buf:
    __slots__ = ("name", "w", "r", "ds", "psum")

    def __init__(self, name, ds=None, psum=False):
        self.psum = psum
        self.name = name
        self.w = []
        self.r = []
        self.ds = ds


class Op:
    __slots__ = ("eng", "ins", "cdeps", "ddeps", "ms", "need", "ds", "dval", "idx")


COMPUTE = ("pe", "act", "dve", "pool")


class Prog:
    def __init__(self):
        self.ops = []
        self.last = {}
        self.lastc = {}
        self.phase_dma = []
        self.pending_barrier = {}

    def op(self, eng, ins, R=(), W=(), dma=None):
        o = Op()
        o.eng = eng
        o.ins = ins
        o.ms = 0
        o.need = False
        o.idx = len(self.ops)
        o.ds = None
        o.dval = 0
        cd = {}
        dd = {}

        def add(p):
            if p.ds is not None:
                k = p.ds.id
                if k not in dd or dd[k][1] < p.dval:
                    dd[k] = (p.ds, p.dval)
            else:
                q = cd.get(p.eng)
                if q is None or q.idx < p.idx:
                    cd[p.eng] = p

        pb = self.pending_barrier.pop(eng, None)
        if pb is not None:
            for p in pb:
                add(p)
        for b in R:
            for p in b.w:
                add(p)
            if b.psum:
                for p in b.r:
                    if p.eng != eng:
                        add(p)
        for b in W:
            if b.r or (b in R):
                for p in b.r:
                    add(p)
                for p in b.w:
                    add(p)
                b.r = []
                b.w = [o]
            else:
                for p in b.w:
                    if p.eng != eng:
                        add(p)
                b.w.append(o)
        for b in R:
            if b not in W:
                b.r.append(o)
        if dma is not None:
            ds = dma.ds
            ds.count += 16
            o.ds = ds
            o.dval = ds.count
            if eng != "pool":
                self.phase_dma.append(o)
        if eng in cd:
            if eng == "pe" or not SAME_ENGINE_SYNC or eng in ("sp", "pool"):
                del cd[eng]
        o.cdeps = list(cd.values())
        o.ddeps = list(dd.values())
        self.ops.append(o)
        self.last[eng] = o
        if dma is None and ins[0] != "nop":
            self.lastc[eng] = o
        return o

    def barrier(self):
        prev = [self.lastc[e] for e in COMPUTE if e in self.lastc] + self.phase_dma
        self.phase_dma = []
        for e in ("pe", "act", "dve", "sp"):
            cur = self.pending_barrier.get(e, [])
            self.pending_barrier[e] = cur + prev

    def finalize(self, nc, engsem):
        for o in self.ops:
            for p in o.cdeps:
                p.need = True
        cnt = {}
        for o in self.ops:
            if o.need:
                cnt[o.eng] = cnt.get(o.eng, 0) + 1
                o.ms = cnt[o.eng]

    def emit(self, eng, e, engsem):
        waited = {}
        n = 0
        for o in self.ops:
            if o.eng != eng:
                continue
            for p in o.cdeps:
                key = ("c", p.eng)
                if waited.get(key, 0) < p.ms:
                    e.wait_ge(engsem[p.eng], p.ms)
                    waited[key] = p.ms
            for (ds, val) in o.ddeps:
                key = ("d", ds.id)
                if waited.get(key, 0) < val:
                    e.wait_ge(ds.sem, val)
                    waited[key] = val
            ins = o.ins
            k = ins[0]
            r = None
            if k == "mm":
                r = e.matmul(ins[1], ins[2], ins[3], start=ins[4], stop=ins[5])
            elif k == "act":
                r = e.activation(out=ins[1], in_=ins[2], func=ins[3], bias=ins[4], scale=ins[5])
            elif k == "tt":
                r = e.tensor_tensor(out=ins[1], in0=ins[2], in1=ins[3], op=ins[4])
            elif k == "ts":
                r = e.tensor_scalar(ins[1], ins[2], ins[3], ins[4], ins[5], ins[6])
            elif k == "stt":
                r = e.scalar_tensor_tensor(ins[1], ins[2], ins[3], ins[4], ins[5], ins[6])
            elif k == "copy":
                r = e.tensor_copy(out=ins[1], in_=ins[2])
            elif k == "recip":
                r = e.reciprocal(ins[1], ins[2])
            elif k == "memset":
                r = e.memset(ins[1], ins[2])
            elif k == "dma":
                r = e.dma_start(out=ins[1], in_=ins[2])
            elif k == "nop":
                r = None
            else:
                raise ValueError(k)
            if r is not None:
                if o.ds is not None:
                    r.then_inc(o.ds.sem, 16)
                elif o.need:
                    r.then_inc(engsem[eng], 1)
            n += 1
        return n


class Deferred:
    def __init__(self):
        self.q = []

    def add(self, delay, fn):
        self.q.append([delay, fn])

    def tick(self):
        for it in self.q:
            it[0] -= 1
        ready = [it for it in self.q if it[0] <= 0]
        self.q = [it for it in self.q if it[0] > 0]
        for it in ready:
            it[1]()

    def flush(self):
        while self.q:
            self.q.sort(key=lambda it: it[0])
            self.q.pop(0)[1]()


class Arena:
    def __init__(self, ap_f32, nbytes):
        self.ap = ap_f32
        self.nbytes = nbytes
        self.off = 0

    def reset(self):
        self.off = 0

    def alloc(self, free_shape, dtype):
        esz = 4 if dtype == F32 else 2
        n = int(np.prod(free_shape))
        nb = (n * esz + 63) // 64 * 64
        assert self.off + nb <= self.nbytes, ("arena overflow", self.off, nb)
        v = self.ap[:, self.off // 4:(self.off + nb) // 4]
        self.off += nb
        if dtype != F32:
            v = v.bitcast(dtype)
        v = v[:, 0:n]
        if len(free_shape) == 2:
            v = v.rearrange("p (a b) -> p a b", b=free_shape[1])
        elif len(free_shape) == 3:
            v = v.rearrange("p (a b c) -> p a b c", b=free_shape[1], c=free_shape[2])
        elif len(free_shape) == 4:
            v = v.rearrange("p (a b c d) -> p a b c d", b=free_shape[1], c=free_shape[2], d=free_shape[3])
        return v


def build_nc(nlayers=L_ALL, nb=2, dump=(), stop_after=None):
    nc = bass.Bass("TRN2", target_bir_lowering=False)
    P = Prog()
    last_layer_idx = L_ALL - 1

    def din(name, shape, dt=F32):
        return nc.dram_tensor(name, list(shape), dt, kind="ExternalInput").ap()

    def dscr(name, shape, dt):
        kind = "ExternalOutput" if name in dump else "Internal"
        return nc.dram_tensor(name, list(shape), dt, kind=kind).ap()

    x0T = din("x0T", [nb, KC, 128, NT])
    cT = din("cT", [128, KC, 3])
    w_ada = din("w_ada_t", [nlayers, 24, 128, SLOT])
    b_ada = din("b_ada_t", [128, nlayers, 96])
    w_in = din("w_in_t", [nlayers, 6, 128, SLOT])
    w_four = din("w_four_t", [nlayers, 128, 2048])
    w_out = din("w_out_t", [nlayers, 4, 128, SLOT])
    w_up = din("w_up_t", [nlayers, 22, 128, SLOT])
    w_down = din("w_down_t", [nlayers, 16, 128, NJ * 128])
    conv_in = din("conv_t", [128, nlayers, 4, 88])
    ln_in = din("ln_t", [128, nlayers, 2, 2, KC])
    sink_in = din("sink_t", [128, nlayers * NQ])
    rope_in = din("rope_t", [128, 2, T])
    cbf_in = din("cbf_t", [128, 2176], BF16)
    dft_in = din("dft_t", [8, 128, SLOT], BF16)
    dftc_in = din("dftc_t", [128, 1024], BF16)
    outT = nc.dram_tensor("outT", [nb, KC, 128, T], F32, kind="ExternalOutput").ap()
    XA = dscr("XA", [KC, 128, NT], F32)
    XB = dscr("XB", [KC, 128, NT], F32)
    H1 = dscr("H1", [KC, 128, NT], BF16)
    H2 = dscr("H2", [KC, 128, NT], BF16)
    H1B = dscr("H1B", [KC, 128, NT], BF16)
    QT = dscr("QT", [NQ, 128, NT], BF16)
    KT = dscr("KT", [NKV, 128, NT], BF16)
    VV = dscr("VV", [NTT, 128, 512], BF16)
    FT = dscr("FT", [4, 128, NT], BF16)
    MIX = dscr("MIX", [KC, 128, NT], BF16)
    RR = dscr("RR", [NJ, 128, NT], BF16)
    MODD = dscr("MODD", [128, nlayers * 96 * 3], F32)
    WD16 = dscr("WD16", [nlayers, 2, 8, 128, 5632], BF16)
    WO16 = dscr("WO16", [nlayers, 4, 128, SLOT], BF16)

    def sb(name, shape, dt):
        return nc.alloc_sbuf_tensor(name, list(shape), dt)

    arena_t = sb("arena", [128, ARENA_BYTES // 4], F32)
    ring_t = [sb(f"ring{i}", [128, SLOT], BF16) for i in range(NSLOT)]
    cbf = sb("cbf", [128, 2176], BF16)
    MOD = sb("mod", [128, nlayers, 96, 3], F32)
    FS = sb("fs", [128, nlayers, 2, KC, 3], F32)
    FB = sb("fb", [128, nlayers, 2, KC, 3], F32)
    LNP = sb("lnp", [128, nlayers, 2, 2, KC], F32)
    CONV = sb("conv", [128, nlayers, 4, 88], F32)
    BADA = sb("bada", [128, nlayers, 96], F32)
    ES = sb("es", [128, nlayers * NQ], F32)
    CIN = sb("cin", [128, KC, 3], F32)
    SCB = sb("scb", [128, KC, 3], BF16)
    EPS = sb("eps", [128, 1], F32)
    psum = [nc.alloc_psum_tensor(f"ps{i}", [128, 512], F32) for i in range(8)]

    IDENT = cbf[:, 0:128]
    ONES = cbf[:, 128:256]
    ONESD = cbf[:, 256:384]
    PSWAP = cbf[:, 384:512]
    MASKLO = cbf[:, 512:896]
    MASKUP = cbf[:, 896:1280]
    C128S = cbf[:, 1280:1536]

    A = Arena(arena_t[:], ARENA_BYTES)

    sem_handles = []

    sem_stack = contextlib.ExitStack()

    def new_sem(name):
        h = sem_stack.enter_context(nc.semaphore(name))
        sem_handles.append(h)
        return h

    engsem = {e: new_sem("s_" + e) for e in ("pe", "act", "dve", "pool")}
    dsem_pool = [DmaSem(new_sem(f"d{i}"), i) for i in range(60)]
    dsem_free = list(dsem_pool)
    phase_sems = []

    def dbuf(name, persistent=False):
        ds = dsem_free.pop()
        if not persistent:
            phase_sems.append(ds)
        return Buf(name, ds)

    def end_phase():
        P.barrier()
        dsem_free.extend(phase_sems)
        phase_sems.clear()
        A.reset()

    PS = [Buf(f"ps{i}", psum=True) for i in range(8)]
    RING = [dbuf(f"ring{i}", persistent=True) for i in range(NSLOT)]
    ring_ctr = [0]
    CONSTB = dbuf("consts", persistent=True)
    GLOB = Buf("glob")

    PRE = [dbuf(f"pre{i}", persistent=True) for i in range(nlayers)]

    def precast_steps(l):
        steps = []
        for G in range(4):
            steps.append(lambda G=G: P.op("pool", ("dma", WO16[l, G].rearrange("p (a b) -> p a b", b=2048),
                                                   w_out[l, G].rearrange("p (a b) -> p a b", b=2048)),
                                          R=(), W=(PRE[l],), dma=PRE[l]))
        for m in range(KC):
            def f(m=m):
                src = w_down[l, m].rearrange("p (h a b) -> p h a b", h=2, b=1408)
                dst = WD16[l, :, m // 2, :, (m % 2) * 2816:(m % 2 + 1) * 2816].rearrange("h p (a b) -> p h a b", b=1408)
                P.op("pool", ("dma", dst, src), R=(), W=(PRE[l],), dma=PRE[l])
            steps.append(f)
        return steps

    pre_pending = {}

    def wload(src_ap, nelem, extra_R=()):
        i = ring_ctr[0] % NSLOT
        ring_ctr[0] += 1
        dst_ap = ring_t[i][:, 0:nelem]
        if nelem > 2048:
            bb = 2048 if nelem % 2048 == 0 else 1408
            dst_ap = dst_ap.rearrange("p (a b) -> p a b", b=bb)
            src_ap = src_ap.rearrange("p (a b) -> p a b", b=bb)
        P.op("pool", ("dma", dst_ap, src_ap), R=extra_R, W=(RING[i],), dma=RING[i])
        return ring_t[i], RING[i]

    def mm(out, lhsT, rhs, start, stop, R, W):
        P.op("pe", ("mm", out, lhsT, rhs, start, stop), R=R, W=W)

    def act(out, in_, func, R, W, bias=0.0, scale=1.0):
        P.op("act", ("act", out, in_, func, bias, scale), R=R, W=W)

    def load(dst, src, buf):
        P.op("sp", ("dma", dst, src), R=(), W=(buf,), dma=buf)

    def store(dst, src, buf):
        P.op("sp", ("dma", dst, src), R=(buf,), W=(), dma=buf)

    load(cbf[:], cbf_in[:], CONSTB)
    load(CIN[:], cT[:], CONSTB)
    load(BADA[:], b_ada[:], CONSTB)
    load(LNP[:], ln_in[:], CONSTB)
    load(CONV[:], conv_in[:], CONSTB)
    load(ES[:], sink_in[:], CONSTB)
    P.op("dve", ("memset", EPS[:], EPSP), R=(), W=(GLOB,))
    act(ES[:], ES[:], AF.Exp, R=(CONSTB,), W=(CONSTB, GLOB))
    act(SCB[:], CIN[:], AF.Silu, R=(CONSTB,), W=(GLOB,))
    def ada_layer(l):
        pb = PS[l % 2]
        pt = psum[l % 2]
        for G in range(24):
            wt, wb = wload(w_ada[l, G], SLOT)
            wv = wt[:].rearrange("p (m k c) -> p m k c", m=4, k=KC)
            for mi in range(4):
                m = G * 4 + mi
                for k in range(KC):
                    mm(pt[:, m * 3:m * 3 + 3], wv[:, mi, k, :], SCB[:, k, :], k == 0, k == KC - 1,
                       R=(wb, GLOB), W=(pb,))
        for r in range(3):
            P.op("dve", ("tt", MOD[:, l, :, r], pt[:, 0:288].rearrange("p (m r) -> p m r", r=3)[:, :, r], BADA[:, l, :], ALU.add),
                 R=(pb, CONSTB), W=(GLOB,))
        for (lo, mul, add_) in ((16, 1.0, 1.0), (64, 1.0, 1.0), (32, 1.0 / ALPHA, 0.0), (80, 1.0 / ALPHA, 0.0)):
            P.op("dve", ("ts", MOD[:, l, lo:lo + 16, :], MOD[:, l, lo:lo + 16, :], mul, add_, ALU.mult, ALU.add),
                 R=(GLOB,), W=(GLOB,))
    def mod_ap(l, idx, k, r):
        return MOD[:, l, idx * 16 + k, r:r + 1]

    def mod0(b):
        dst = H1 if b == 0 else H1B
        for k in range(KC):
            i = k % 6
            load(XI[i], x0T[b][k], XIb[i])
            if k % 2 == 0:
                act(HOo[i][:, 0:T], XI[i][:, 0:T], AF.Identity, R=(XIb[i], GLOB), W=(HOb[i],),
                    bias=mod_ap(0, 0, k, b), scale=mod_ap(0, 1, k, b))
                act(HOo[i][:, T:NT], XI[i][:, T:NT], AF.Identity, R=(XIb[i], GLOB), W=(HOb[i],),
                    bias=mod_ap(0, 0, k, 2), scale=mod_ap(0, 1, k, 2))
            else:
                P.op("dve", ("ts", HOo[i][:, 0:T], XI[i][:, 0:T], mod_ap(0, 1, k, b), mod_ap(0, 0, k, b),
                             ALU.mult, ALU.add), R=(XIb[i], GLOB), W=(HOb[i],))
                P.op("dve", ("ts", HOo[i][:, T:NT], XI[i][:, T:NT], mod_ap(0, 1, k, 2), mod_ap(0, 0, k, 2),
                             ALU.mult, ALU.add), R=(XIb[i], GLOB), W=(HOb[i],))
            store(dst[k], HOo[i], HOb[i])

    ada_layer(0)
    XI = [A.alloc([NT], F32) for _ in range(6)]
    HOo = [A.alloc([NT], BF16) for _ in range(6)]
    XIb = [dbuf(f"xi{i}") for i in range(6)]
    HOb = [dbuf(f"ho{i}") for i in range(6)]
    for b_ in range(nb):
        mod0(b_)
    for l_ in range(1, nlayers):
        ada_layer(l_)
    for l in range(nlayers):
        for which in range(2):
            if which == 0:
                s_lo, b_lo, ls = 64, 48, l
            else:
                if l + 1 >= nlayers:
                    continue
                s_lo, b_lo, ls = 16, 0, l + 1
            for r in range(3):
                g_ap = LNP[:, l, which, 0, :]
                be_ap = LNP[:, l, which, 1, :]
                P.op("dve", ("tt", FS[:, l, which, :, r], MOD[:, ls, s_lo:s_lo + 16, r], g_ap, ALU.mult),
                     R=(GLOB, CONSTB), W=(GLOB,))
                P.op("dve", ("tt", FB[:, l, which, :, r], MOD[:, ls, s_lo:s_lo + 16, r], be_ap, ALU.mult),
                     R=(GLOB, CONSTB), W=(GLOB,))
                P.op("dve", ("tt", FB[:, l, which, :, r], FB[:, l, which, :, r], MOD[:, ls, b_lo:b_lo + 16, r], ALU.add),
                     R=(GLOB,), W=(GLOB,))
    if "MODD" in dump:
        MODb = dbuf("modd")
        P.op("sp", ("dma", MODD[:], MOD[:].rearrange("p l m r -> p (l m r)")), R=(GLOB, MODb), W=(), dma=MODb)
    end_phase()

    stopped = [False]

    def check_stop(l, b, ph):
        if stop_after is not None and (l, b, ph) == tuple(stop_after):
            stopped[0] = True
        return stopped[0]

    def ln_sched(DQ, Z, ZSQ, HO, STAT, w, l, which, r, zbuf, hbuf, sqbuf, stbuf, want_h, finish, delays, zk):
        pm, pq = psum[6], psum[7]
        MEAN = STAT[:, 0, 0:w]
        RSTD = STAT[:, 1, 0:w]
        two_pass = ZSQ is None

        def stepA():
            for hh in range(2):
                ks = slice(hh * 8, hh * 8 + 8)
                act(HO[:, ks, 0:w], Z[:, ks, 0:w], AF.Copy, R=[zbuf] + zk[ks], W=(hbuf,))
                if not two_pass:
                    act(ZSQ[:, ks, 0:w], Z[:, ks, 0:w], AF.Square, R=[zbuf] + zk[ks], W=(sqbuf,))

        def stepB():
            for k in range(KC):
                mm(pm[:, 0:w], ONESD, HO[:, k, 0:w], k == 0, k == KC - 1, R=(hbuf, CONSTB), W=(PS[6],))
            if not two_pass:
                for k in range(KC):
                    mm(pq[:, 0:w], ONESD, ZSQ[:, k, 0:w], k == 0, k == KC - 1, R=(sqbuf, CONSTB), W=(PS[7],))
            act(MEAN, pm[:, 0:w], AF.Copy, R=(PS[6],), W=(stbuf,))
            if two_pass:
                for hh in range(2):
                    ks = slice(hh * 8, hh * 8 + 8)
                    act(HO[:, ks, 0:w], Z[:, ks, 0:w], AF.Square, R=[zbuf] + zk[ks], W=(hbuf,))

        def stepC():
            if two_pass:
                for k in range(KC):
                    mm(pq[:, 0:w], ONESD, HO[:, k, 0:w], k == 0, k == KC - 1, R=(hbuf, CONSTB), W=(PS[7],))

        def stepD0():
            P.op("dve", ("tt", RSTD, MEAN, MEAN, ALU.mult), R=(stbuf,), W=(stbuf,))
            P.op("dve", ("tt", pq[:, 0:w], pq[:, 0:w], RSTD, ALU.subtract), R=(stbuf, PS[7]), W=(PS[7],))
            act(pq[:, 0:w], pq[:, 0:w], AF.Sqrt, R=(PS[7], GLOB), W=(PS[7],), bias=EPS[:, 0:1])
            P.op("dve", ("recip", pq[:, 0:w], pq[:, 0:w]), R=(PS[7],), W=(PS[7],))

        def stepDk(k0, k1):
            for k in range(k0, k1):
                P.op("dve", ("tt", Z[:, k, 0:w], Z[:, k, 0:w], pm[:, 0:w], ALU.subtract), R=(zk[k], PS[6]), W=(zk[k],))
                P.op("dve", ("tt", Z[:, k, 0:w], Z[:, k, 0:w], pq[:, 0:w], ALU.mult), R=(zk[k], PS[7]), W=(zk[k],))
                if want_h:
                    act(HO[:, k, 0:w], Z[:, k, 0:w], AF.Identity, R=(zk[k], GLOB), W=(hbuf,),
                        bias=FB[:, l, which, k, r:r + 1], scale=FS[:, l, which, k, r:r + 1])
                act(Z[:, k, 0:w], Z[:, k, 0:w], AF.Identity, R=(zk[k], CONSTB), W=(zk[k],),
                    bias=LNP[:, l, which, 1, k:k + 1], scale=LNP[:, l, which, 0, k:k + 1])
            if k1 == KC:
                finish()

        DQ.add(delays[0], stepA)
        DQ.add(delays[1], stepB)
        DQ.add(delays[2], stepC)
        DQ.add(delays[3], stepD0)
        kpt = delays[4]
        for i_, k0 in enumerate(range(0, KC, kpt)):
            DQ.add(delays[3] + 1 + i_, lambda k0=k0: stepDk(k0, k0 + kpt))

    if stop_after is not None and stop_after[2] == "ADA":
        stopped[0] = True
    for b in range(nb):
        if stopped[0]:
            break
        xsrc = x0T[b]
        for l in range(nlayers):
            if stopped[0]:
                break
            last = (l == last_layer_idx)
            tgs_q = TGS[:4] if last else TGS

            if b == 0:
                for f_ in pre_pending.pop(l, precast_steps(l) if l == 0 else []):
                    f_()
            HT = A.alloc([KC, NT], BF16)
            ROPE = A.alloc([2, T], F32)
            OUTC = [A.alloc([NT], BF16) for _ in range(2)]
            VST = [A.alloc([512], BF16) for _ in range(2)]
            XBb_t = [A.alloc([512], BF16) for _ in range(2)]
            T1 = [A.alloc([512], F32) for _ in range(2)]
            T2 = [A.alloc([512], F32) for _ in range(2)]
            HTb = [dbuf(f"ht{i}") for i in range(5)]
            ROPEb = dbuf("rope")
            OUTCb = [dbuf(f"outc{i}") for i in range(2)]
            VSTb = [dbuf(f"vst{i}") for i in range(2)]
            XBb = [Buf(f"xb{i}") for i in range(2)]
            T1b = [Buf(f"t1{i}") for i in range(2)]
            T2b = [Buf(f"t2{i}") for i in range(2)]
            for n_, (t0_, w_) in enumerate(TGS):
                for kh in range(2):
                    h1src = H1B if (l == 0 and b == 1) else H1
                    load(HT[:, kh * 8:kh * 8 + 8, t0_:t0_ + w_],
                         h1src[kh * 8:kh * 8 + 8, :, t0_:t0_ + w_].rearrange("k p t -> p k t"), HTb[n_])
                if n_ == 0:
                    load(ROPE[:], rope_in[:], ROPEb)
            wt, wb = wload(w_in[l, 0], SLOT)
            wv = wt[:].rearrange("p (k c) -> p k c", k=KC)
            for tt in range(NTT):
                bk = tt % 4
                for k in range(KC):
                    mm(psum[bk][:], HT[:, k, tt * 128:(tt + 1) * 128], wv[:, k, :], k == 0, k == KC - 1,
                       R=(wb, HTb[tt // 4]), W=(PS[bk],))
                i = tt % 2
                act(VST[i][:], psum[bk][:], AF.Copy, R=(PS[bk],), W=(VSTb[i],))
                store(VV[tt], VST[i], VSTb[i])
            tile_ctr = 0
            oc_ctr = 0
            pending = None

            def rope_tail(pd):
                (bk_, sw_, i_, oc_, t0_, w_, ocb_) = pd
                mm(psum[sw_][:, 0:w_], PSWAP, XBb_t[i_][:, 0:w_], True, True, R=(XBb[i_], CONSTB), W=(PS[sw_],))
                P.op("dve", ("tt", psum[bk_][:, 0:w_], psum[bk_][:, 0:w_], ROPE[:, 0, t0_:t0_ + w_], ALU.mult),
                     R=(PS[bk_], ROPEb, XBb[i_]), W=(PS[bk_],))
                P.op("dve", ("tt", T2[i_][:, 0:w_], psum[sw_][:, 0:w_], ROPE[:, 1, t0_:t0_ + w_], ALU.mult),
                     R=(PS[sw_], ROPEb), W=(T2b[i_],))
                P.op("dve", ("tt", oc_[:, t0_:t0_ + w_], psum[bk_][:, 0:w_], T2[i_][:, 0:w_], ALU.add),
                     R=(PS[bk_], T2b[i_]), W=(ocb_,))

            for G in range(1, 6):
                wt, wb = wload(w_in[l, G], SLOT)
                wv = wt[:].rearrange("p (k c) -> p k c", k=KC)
                for mi in range(4):
                    is_f = (G == 1)
                    is_k = (not is_f) and mi == 0
                    tgl = TGS if (is_k or not last) else TGS[:4]
                    oc = OUTC[oc_ctr % 2]
                    ocb = OUTCb[oc_ctr % 2]
                    oc_ctr += 1
                    for (t0, w) in tgl:
                        bk = tile_ctr % 4
                        for k in range(KC):
                            mm(psum[bk][:, 0:w], wv[:, k, mi * 128:(mi + 1) * 128], HT[:, k, t0:t0 + w],
                               k == 0, k == KC - 1, R=(wb, HTb[t0 // 512]), W=(PS[bk],))
                        if pending is not None:
                            rope_tail(pending)
                            pending = None
                        if is_f or t0 >= T:
                            act(oc[:, t0:t0 + w], psum[bk][:, 0:w], AF.Copy, R=(PS[bk],), W=(ocb,))
                        else:
                            i = tile_ctr % 2
                            act(XBb_t[i][:, 0:w], psum[bk][:, 0:w], AF.Copy, R=(PS[bk],), W=(XBb[i],))
                            pending = (bk, 4 + i, i, oc, t0, w, ocb)
                        tile_ctr += 1
                    if pending is not None:
                        rope_tail(pending)
                        pending = None
                    wcols = tgl[-1][0] + tgl[-1][1]
                    if is_f:
                        dst = FT[mi]
                    elif is_k:
                        dst = KT[G - 2]
                    else:
                        dst = QT[3 * (G - 2) + mi - 1]
                    store(dst[:, 0:wcols], oc[:, 0:wcols], ocb)
            end_phase()
            if check_stop(l, b, "P2"):
                break

            KTs = [A.alloc([NT], BF16) for _ in range(2)]
            QTs = [A.alloc([3, NT], BF16) for _ in range(2)]
            VH = [A.alloc([NTT, 128], BF16) for _ in range(2)]
            OT = [A.alloc([3, NT], BF16) for _ in range(2)]
            PT = [A.alloc([384], BF16) for _ in range(4)]
            REC = [A.alloc([384], F32) for _ in range(2)]
            KQVb = [dbuf(f"kqv{i}") for i in range(2)]
            OTb = [dbuf(f"ot{i}") for i in range(2)]
            PTb = [Buf(f"pt{i}") for i in range(4)]
            RECb = [Buf(f"rec{i}") for i in range(2)]
            RECb2 = [Buf(f"recb{i}") for i in range(2)]
            nqb = 16 if last else 18
            pt_ctr = 0
            s_ctr = 0
            pend_norm = []
            LA = 2
            wq = T if last else NT

            def p3_load(h_):
                j2 = h_ % 2
                load(KTs[j2], KT[h_], KQVb[j2])
                load(QTs[j2][:, :, 0:wq], QT[3 * h_:3 * h_ + 3, :, 0:wq].rearrange("g p t -> p g t"), KQVb[j2])
                load(VH[j2], VV[:, :, h_ * 128:(h_ + 1) * 128].rearrange("t p c -> p t c"), KQVb[j2])

            p3_load(0)
            items = []
            for h in range(NKV):
                for qb in range(nqb):
                    if qb < 16:
                        keys = []
                        if qb > 0:
                            keys.append((qb - 1, MASKLO))
                        keys.append((qb, None))
                        if qb < 15:
                            keys.append((qb + 1, MASKUP))
                        keys += [(16, None), (17, None)]
                    else:
                        keys = [(16, None), (17, None)]
                    for ii, (kb, mask) in enumerate(keys):
                        items.append((h, qb, ii, kb, mask, len(keys)))
            ptis = {}

            def norm(h, qb):
                i2 = h % 2
                ob = qb % 2
                po, psm = psum[3 + ob], psum[5 + ob]
                pob, psb = PS[3 + ob], PS[5 + ob]
                rc = REC[ob]
                for g in range(3):
                    hq = 3 * h + g
                    P.op("dve", ("ts", rc[:, g * 128:(g + 1) * 128], psm[:, g * 128:(g + 1) * 128],
                                 ES[:, l * NQ + hq:l * NQ + hq + 1], None, ALU.add, ALU.bypass),
                         R=(psb, GLOB), W=(RECb[ob], RECb2[ob]))
                act(rc[:, 192:384], rc[:, 192:384], AF.Ln, R=(RECb2[ob],), W=(RECb2[ob],))
                act(rc[:, 192:384], rc[:, 192:384], AF.Exp, R=(RECb2[ob],), W=(RECb2[ob],), scale=-1.0)
                P.op("dve", ("recip", rc[:, 0:192], rc[:, 0:192]), R=(RECb[ob],), W=(RECb[ob],))
                P.op("dve", ("tt", OT[i2][:, :, qb * 128:(qb + 1) * 128],
                             po[:, 0:384].rearrange("p (g q) -> p g q", g=3),
                             rc[:].rearrange("p (g q) -> p g q", g=3), ALU.mult),
                     R=(pob, RECb[ob], RECb2[ob]), W=(OTb[i2],))
                if qb == nqb - 1:
                    store(MIX[4 + 3 * h:4 + 3 * h + 3, :, 0:wq].rearrange("g p t -> p g t"), OT[i2][:, :, 0:wq], OTb[i2])

            pend_n = []
            for n in range(len(items) + LA):
                if n < len(items):
                    (h, qb, ii, kb, mask, nk) = items[n]
                    i2 = h % 2
                    sb_ = s_ctr % 3
                    s_ctr += 1
                    qsl = QTs[i2][:, :, qb * 128:(qb + 1) * 128]
                    mm(psum[sb_][:, 0:384], KTs[i2][:, kb * 128:(kb + 1) * 128], qsl, True, mask is None,
                       R=(KQVb[i2],), W=(PS[sb_],))
                    if mask is not None:
                        mm(psum[sb_][:, 0:384], IDENT, mask, False, True, R=(CONSTB,), W=(PS[sb_],))
                    pti = pt_ctr % 4
                    pt_ctr += 1
                    ptis[n] = pti
                    act(PT[pti][:], psum[sb_][:, 0:384], AF.Exp, R=(PS[sb_],), W=(PTb[pti],), scale=ATTN_SCALE)
                m_ = n - LA
                if m_ >= 0:
                    (h, qb, ii, kb, mask, nk) = items[m_]
                    i2 = h % 2
                    ob = qb % 2
                    if qb == 0 and ii == 0 and h + 1 < NKV:
                        p3_load(h + 1)
                    pti = ptis.pop(m_)
                    mm(psum[3 + ob][:, 0:384], VH[i2][:, kb, :], PT[pti][:], ii == 0, ii == nk - 1,
                       R=(KQVb[i2], PTb[pti]), W=(PS[3 + ob],))
                    mm(psum[5 + ob][:, 0:384], ONES, PT[pti][:], ii == 0, ii == nk - 1,
                       R=(CONSTB, PTb[pti]), W=(PS[5 + ob],))
                    if ii == 1 and pend_n:
                        norm(*pend_n.pop(0))
                    if ii == nk - 1:
                        pend_n.append((h, qb))
            while pend_n:
                norm(*pend_n.pop(0))
            end_phase()
            if check_stop(l, b, "P3"):
                break

            FTs = A.alloc([4, NT], BF16)
            ZCS = A.alloc([NTT, 4, 256], BF16)
            YT = A.alloc([4, NT], BF16)
            OUTC = [A.alloc([NT], BF16) for _ in range(2)]
            DFC = A.alloc([2, 2, 256], BF16)
            FTb = dbuf("fts")
            DFCb = dbuf("dfc")
            ZCSb = Buf("zcs")
            YTb = Buf("yt")
            OUTCb = [dbuf(f"outc{i}") for i in range(2)]
            ntt_f = 16 if last else NTT
            wf = T if last else NT
            load(FTs[:, :, 0:wf], FT[:, :, 0:wf].rearrange("g p t -> p g t"), FTb)
            load(DFC[:], dftc_in[:], DFCb)
            for tt in range(ntt_f):
                for gp in range(2):
                    bk = (tt * 2 + gp) % 4
                    for gi in range(2):
                        g = gp * 2 + gi
                        mm(psum[bk][:, gi * 256:(gi + 1) * 256], FTs[:, g, tt * 128:(tt + 1) * 128], C128S, True, True,
                           R=(FTb, CONSTB), W=(PS[bk],))
                    act(ZCS[:, tt, gp * 2:gp * 2 + 2, :], psum[bk][:].rearrange("p (g c) -> p g c", g=2), AF.Copy,
                        R=(PS[bk],), W=(ZCSb,))
            tile_ctr = 0
            for Gt in range(8):
                wt, wb = wload(dft_in[Gt], SLOT)
                wv = wt[:].rearrange("p (k s c) -> p k s c", k=KC, s=2)
                for g in range(4):
                    bk = tile_ctr % 4
                    tile_ctr += 1
                    for k in range(KC):
                        mm(psum[bk][:, 0:256], ZCS[:, k, g, 0:128], wv[:, k, 0, :], k == 0, False,
                           R=(ZCSb, wb), W=(PS[bk],))
                        mm(psum[bk][:, 0:256], ZCS[:, k, g, 128:256], wv[:, k, 1, :], False, k == KC - 1,
                           R=(ZCSb, wb), W=(PS[bk],))
                    act(YT[:, g, Gt * 256:(Gt + 1) * 256], psum[bk][:, 0:256], AF.Copy, R=(PS[bk],), W=(YTb,))
            if not last:
                for g in range(4):
                    bk = tile_ctr % 4
                    tile_ctr += 1
                    for k in range(2):
                        mm(psum[bk][:, 0:256], ZCS[:, 16 + k, g, 0:128], DFC[:, k, 0, :], k == 0, False,
                           R=(ZCSb, DFCb), W=(PS[bk],))
                        mm(psum[bk][:, 0:256], ZCS[:, 16 + k, g, 128:256], DFC[:, k, 1, :], False, k == 1,
                           R=(ZCSb, DFCb), W=(PS[bk],))
                    act(YT[:, g, T:NT], psum[bk][:, 0:256], AF.Copy, R=(PS[bk],), W=(YTb,))
            wt, wb = wload(w_four[l], 2048)
            wv = wt[:, 0:2048].rearrange("p (k c) -> p k c", k=4)
            for mi in range(4):
                oc = OUTC[mi % 2]
                ocb = OUTCb[mi % 2]
                for (t0, w) in tgs_q:
                    bk = tile_ctr % 4
                    tile_ctr += 1
                    for k in range(4):
                        mm(psum[bk][:, 0:w], wv[:, k, mi * 128:(mi + 1) * 128], YT[:, k, t0:t0 + w], k == 0, k == 3,
                           R=(wb, YTb), W=(PS[bk],))
                    act(oc[:, t0:t0 + w], psum[bk][:, 0:w], AF.Copy, R=(PS[bk],), W=(ocb,))
                store(MIX[mi][:, 0:wf], oc[:, 0:wf], ocb)
            end_phase()
            if check_stop(l, b, "P4"):
                break

            xdst = XB
            MX = [A.alloc([KC, 512], BF16) for _ in range(2)]
            Zt = [A.alloc([KC, 512], F32) for _ in range(2)]
            ZB2 = [A.alloc([512], BF16) for _ in range(4)]
            SQ2 = [A.alloc([512], BF16) for _ in range(4)]
            HO = A.alloc([KC, 512], BF16)
            STAT = A.alloc([2, 512], F32)
            MXb = [dbuf(f"mx{i}") for i in range(2)]
            Zb = [dbuf(f"z{i}") for i in range(2)]
            ZK = [[Buf(f"zk{i}_{k}") for k in range(KC)] for i in range(2)]
            ZB2b = [Buf(f"zb2{i}") for i in range(4)]
            SQ2b = [Buf(f"sq2{i}") for i in range(4)]
            HOb = dbuf("ho")
            STb = Buf("stat")
            tile_ctr = 0
            DQ = Deferred()
            ntg = len(tgs_q)

            def p5_load_mx(ti_):
                t0_, w_ = tgs_q[ti_]
                load(MX[ti_ % 2][:, :, 0:w_], MIX[:, :, t0_:t0_ + w_].rearrange("k p t -> p k t"), MXb[ti_ % 2])

            def p5_load_z(ti_, src_):
                t0_, w_ = tgs_q[ti_]
                load(Zt[ti_ % 2][:, :, 0:w_], src_[:, :, t0_:t0_ + w_].rearrange("k p t -> p k t"), Zb[ti_ % 2])

            p5_load_mx(0)
            p5_load_z(0, xsrc)
            if ntg > 1:
                p5_load_z(1, xsrc)
            st_ctr = 0
            for ti, (t0, w) in enumerate(tgs_q):
                i2 = ti % 2
                r = b if t0 < T else 2
                sp_ = 4 + 2 * (ti % 2)
                pm, pq = psum[sp_], psum[sp_ + 1]
                pmb, pqb = PS[sp_], PS[sp_ + 1]
                if ti + 1 < ntg:
                    p5_load_mx(ti + 1)
                for G in range(4):
                    wt, wb = wload(WO16[l, G], SLOT, extra_R=(PRE[l],))
                    wv = wt[:].rearrange("p (k c) -> p k c", k=KC)
                    for mi in range(4):
                        m = G * 4 + mi
                        bk = tile_ctr % 4
                        tile_ctr += 1
                        for k in range(KC):
                            mm(psum[bk][:, 0:w], wv[:, k, mi * 128:(mi + 1) * 128], MX[i2][:, k, 0:w],
                               k == 0, k == KC - 1, R=(wb, MXb[i2]), W=(PS[bk],))
                        P.op("dve", ("stt", Zt[i2][:, m, 0:w], psum[bk][:, 0:w], mod_ap(l, 2, m, r),
                                     Zt[i2][:, m, 0:w], ALU.mult, ALU.add),
                             R=(PS[bk], Zb[i2], ZK[i2][m], GLOB), W=(ZK[i2][m],))
                        si = st_ctr % 4
                        st_ctr += 1
                        act(ZB2[si][:, 0:w], Zt[i2][:, m, 0:w], AF.Copy, R=(ZK[i2][m],), W=(ZB2b[si],))
                        act(SQ2[si][:, 0:w], Zt[i2][:, m, 0:w], AF.Square, R=(ZK[i2][m],), W=(SQ2b[si],))

                        def stat_mm(m=m, si=si, w=w, pm=pm, pq=pq, pmb=pmb, pqb=pqb):
                            mm(pm[:, 0:w], ONESD, ZB2[si][:, 0:w], m == 0, m == KC - 1, R=(ZB2b[si], CONSTB), W=(pmb,))
                            mm(pq[:, 0:w], ONESD, SQ2[si][:, 0:w], m == 0, m == KC - 1, R=(SQ2b[si], CONSTB), W=(pqb,))

                        DQ.tick()
                        DQ.add(2, stat_mm)

                def ln_tail(ti=ti, t0=t0, w=w, i2=i2, r=r, pm=pm, pq=pq, pmb=pmb, pqb=pqb, src_=xsrc):
                    Z = Zt[i2]
                    zk = ZK[i2]
                    MEAN = STAT[:, 0, 0:w]
                    RSTD = STAT[:, 1, 0:w]

                    def d0():
                        act(MEAN, pm[:, 0:w], AF.Copy, R=(pmb,), W=(STb,))
                        P.op("dve", ("tt", RSTD, MEAN, MEAN, ALU.mult), R=(STb,), W=(STb,))
                        P.op("dve", ("tt", pq[:, 0:w], pq[:, 0:w], RSTD, ALU.subtract), R=(STb, pqb), W=(pqb,))
                        act(pq[:, 0:w], pq[:, 0:w], AF.Sqrt, R=(pqb, GLOB), W=(pqb,), bias=EPS[:, 0:1])
                        P.op("dve", ("recip", pq[:, 0:w], pq[:, 0:w]), R=(pqb,), W=(pqb,))

                    def dk(k0, k1):
                        for k in range(k0, k1):
                            P.op("dve", ("tt", Z[:, k, 0:w], Z[:, k, 0:w], pm[:, 0:w], ALU.subtract), R=(zk[k], pmb), W=(zk[k],))
                            P.op("dve", ("tt", Z[:, k, 0:w], Z[:, k, 0:w], pq[:, 0:w], ALU.mult), R=(zk[k], pqb), W=(zk[k],))
                            act(HO[:, k, 0:w], Z[:, k, 0:w], AF.Identity, R=(zk[k], GLOB), W=(HOb,),
                                bias=FB[:, l, 0, k, r:r + 1], scale=FS[:, l, 0, k, r:r + 1])
                            act(Z[:, k, 0:w], Z[:, k, 0:w], AF.Identity, R=(zk[k], CONSTB), W=(zk[k],),
                                bias=LNP[:, l, 0, 1, k:k + 1], scale=LNP[:, l, 0, 0, k:k + 1])
                        if k1 == KC:
                            P.op("sp", ("dma", xdst[:, :, t0:t0 + w].rearrange("k p t -> p k t"), Z[:, :, 0:w]),
                                 R=[Zb[i2]] + zk, W=(), dma=Zb[i2])
                            store(H2[:, :, t0:t0 + w].rearrange("k p t -> p k t"), HO[:, :, 0:w], HOb)
                            if ti + 2 < ntg:
                                p5_load_z(ti + 2, src_)

                    DQ.add(3, d0)
                    for i_, k0 in enumerate(range(0, KC, 2)):
                        DQ.add(4 + i_, lambda k0=k0: dk(k0, k0 + 2))

                ln_tail()
            DQ.flush()
            end_phase()
            xsrc = xdst
            if check_stop(l, b, "P5"):
                break

            CW = 2308
            LAT0, CTX0 = 1, 2051
            HT = A.alloc([KC, NT], BF16)
            CAt = [A.alloc([CW], F32) for _ in range(2)]
            CGt = [A.alloc([CW], F32) for _ in range(2)]
            RO = [A.alloc([NT], BF16) for _ in range(2)]
            HTb = [dbuf(f"ht{i}") for i in range(5)]
            ROb = [dbuf(f"ro{i}") for i in range(2)]
            CAb = [[Buf(f"ca{i}_{n}") for n in range(5)] for i in range(2)]
            CGb = [[Buf(f"cg{i}_{n}") for n in range(5)] for i in range(2)]
            wq = T if last else NT
            for n_, (t0_, w_) in enumerate(tgs_q):
                for kh in range(2):
                    load(HT[:, kh * 8:kh * 8 + 8, t0_:t0_ + w_],
                         H2[kh * 8:kh * 8 + 8, :, t0_:t0_ + w_].rearrange("k p t -> p k t"), HTb[n_])

            def ccol(t0):
                return (LAT0 + t0) if t0 < T else (CTX0 + t0 - T)

            tile_ctr = 0
            if b == 0 and l + 1 < nlayers:
                pre_pending[l + 1] = precast_steps(l + 1)
            for G in range(22):
                if b == 0 and pre_pending.get(l + 1):
                    pre_pending[l + 1].pop(0)()
                wt, wb = wload(w_up[l, G], SLOT)
                wv = wt[:].rearrange("p (j a k c) -> p j a k c", j=2, a=2, k=KC)
                for jj in range(2):
                    j = 2 * G + jj
                    i2 = j % 2
                    streams = ((0, CAt[i2], CAb[i2], j, 0), (1, CGt[i2], CGb[i2], NJ + j, 4))
                    pend = None

                    def taps(pd):
                        (n_, t0_, w_, bka_) = pd
                        for (ag, Ct, Cb, ch, bo) in streams:
                            c0 = ccol(t0_)
                            nb_ = [Cb[n_]]
                            if t0_ < T:
                                if n_ > 0:
                                    nb_.append(Cb[n_ - 1])
                                if n_ < 3:
                                    nb_.append(Cb[n_ + 1])
                            pt_ = psum[bo + bka_]
                            P.op("dve", ("stt", Ct[:, c0 + 1:c0 + 1 + w_], pt_[:, 0:w_], CONV[:, l, 0, ch:ch + 1],
                                         Ct[:, c0 + 1:c0 + 1 + w_], ALU.mult, ALU.add),
                                 R=[PS[bo + bka_], CONSTB] + nb_, W=nb_)
                            P.op("dve", ("stt", Ct[:, c0 - 1:c0 - 1 + w_], pt_[:, 0:w_], CONV[:, l, 2, ch:ch + 1],
                                         Ct[:, c0 - 1:c0 - 1 + w_], ALU.mult, ALU.add),
                                 R=[PS[bo + bka_], CONSTB] + nb_, W=nb_)

                    for n, (t0, w) in enumerate(tgs_q):
                        bka = tile_ctr % 4
                        tile_ctr += 1
                        for (ag, Ct, Cb, ch, bo) in streams:
                            for k in range(KC):
                                mm(psum[bo + bka][:, 0:w], wv[:, jj, ag, k, :], HT[:, k, t0:t0 + w], k == 0, k == KC - 1,
                                   R=(wb, HTb[n]), W=(PS[bo + bka],))
                        for (ag, Ct, Cb, ch, bo) in streams:
                            c0 = ccol(t0)
                            act(Ct[:, c0:c0 + w], psum[bo + bka][:, 0:w], AF.Identity, R=(PS[bo + bka], CONSTB), W=(Cb[n],),
                                bias=CONV[:, l, 3, ch:ch + 1], scale=CONV[:, l, 1, ch:ch + 1])
                        if pend is not None and t0 < T:
                            taps(pend)
                            pend = None
                        if pend is not None:
                            taps(pend)
                            pend = None
                        pend = (n, t0, w, bka)
                    taps(pend)
                    pend = None
                    allb_a = CAb[i2][:len(tgs_q)]
                    allb_g = CGb[i2][:len(tgs_q)]
                    segs = [(LAT0, 0, T)] + ([] if last else [(CTX0, T, CL)])
                    for (c0, o0, w) in segs:
                        act(CGt[i2][:, c0:c0 + w], CGt[i2][:, c0:c0 + w], AF.Silu, R=allb_g, W=allb_g)
                    for (c0, o0, w) in segs:
                        P.op("dve", ("tt", RO[i2][:, o0:o0 + w], CGt[i2][:, c0:c0 + w], CAt[i2][:, c0:c0 + w], ALU.mult),
                             R=allb_g + allb_a, W=(ROb[i2],))
                    store(RR[j][:, 0:wq], RO[i2][:, 0:wq], ROb[i2])
            end_phase()
            if check_stop(l, b, "P6"):
                break

            xdst = XA
            RTh = [A.alloc([22, 512], BF16) for _ in range(2)]
            Zt = [A.alloc([KC, 512], F32) for _ in range(2)]
            ZB2 = [A.alloc([512], BF16) for _ in range(2)]
            SQ2 = [A.alloc([512], BF16) for _ in range(2)]
            HO = A.alloc([KC, 512], BF16)
            STAT = A.alloc([2, 512], F32)
            RTb = [dbuf(f"rt{i}") for i in range(2)]
            Zb = [dbuf(f"z{i}") for i in range(2)]
            ZK = [[Buf(f"zk{i}_{k}") for k in range(KC)] for i in range(2)]
            ZB2b = [Buf(f"zb2{i}") for i in range(2)]
            SQ2b = [Buf(f"sq2{i}") for i in range(2)]
            HOb = dbuf("ho")
            STb = Buf("stat")
            tile_ctr = 0
            st_ctr = 0
            DQ = Deferred()
            ntg = len(tgs_q)
            halves = [(ti, jh) for ti in range(ntg) for jh in range(2)]
            want_h7 = (l + 1 < nlayers)

            def p7_load_half(idx_):
                ti_, jh_ = halves[idx_]
                t0_, w_ = tgs_q[ti_]
                bi = idx_ % 2
                for jq in range(2):
                    j0 = jh_ * 22 + jq * 11
                    load(RTh[bi][:, jq * 11:(jq + 1) * 11, 0:w_],
                         RR[j0:j0 + 11, :, t0_:t0_ + w_].rearrange("j p t -> p j t"), RTb[bi])

            def p7_load_z(ti_, src_):
                t0_, w_ = tgs_q[ti_]
                load(Zt[ti_ % 2][:, :, 0:w_], src_[:, :, t0_:t0_ + w_].rearrange("k p t -> p k t"), Zb[ti_ % 2])

            p7_load_half(0)
            p7_load_z(0, xsrc)
            if ntg > 1:
                p7_load_z(1, xsrc)
            for idx, (ti, jh) in enumerate(halves):
                t0, w = tgs_q[ti]
                i2 = ti % 2
                r = b if t0 < T else 2
                sp_ = 4 + 2 * (ti % 2)
                pm, pq = psum[sp_], psum[sp_ + 1]
                pmb, pqb = PS[sp_], PS[sp_ + 1]
                if idx + 1 < len(halves):
                    p7_load_half(idx + 1)
                for mp in range(8):
                    wt, wb = wload(WD16[l, jh, mp], 5632, extra_R=(PRE[l],))
                    wv = wt[:, 0:5632].rearrange("p (m j c) -> p m j c", m=2, j=22)
                    for mi in range(2):
                        m = mp * 2 + mi
                        bk = tile_ctr % 4
                        tile_ctr += 1
                        for j in range(22):
                            mm(psum[bk][:, 0:w], wv[:, mi, j, :], RTh[idx % 2][:, j, 0:w], j == 0, j == 21,
                               R=(wb, RTb[idx % 2]), W=(PS[bk],))
                        P.op("dve", ("stt", Zt[i2][:, m, 0:w], psum[bk][:, 0:w], mod_ap(l, 5, m, r),
                                     Zt[i2][:, m, 0:w], ALU.mult, ALU.add),
                             R=(PS[bk], Zb[i2], ZK[i2][m], GLOB), W=(ZK[i2][m],))
                        if jh == 1:
                            si = st_ctr % 2
                            st_ctr += 1
                            act(ZB2[si][:, 0:w], Zt[i2][:, m, 0:w], AF.Copy, R=(ZK[i2][m],), W=(ZB2b[si],))
                            act(SQ2[si][:, 0:w], Zt[i2][:, m, 0:w], AF.Square, R=(ZK[i2][m],), W=(SQ2b[si],))

                            def stat_mm(m=m, si=si, w=w, pm=pm, pq=pq, pmb=pmb, pqb=pqb):
                                mm(pm[:, 0:w], ONESD, ZB2[si][:, 0:w], m == 0, m == KC - 1, R=(ZB2b[si], CONSTB), W=(pmb,))
                                mm(pq[:, 0:w], ONESD, SQ2[si][:, 0:w], m == 0, m == KC - 1, R=(SQ2b[si], CONSTB), W=(pqb,))

                            DQ.tick()
                            DQ.add(1, stat_mm)
                        else:
                            DQ.tick()
                if jh == 1:
                    def ln_tail(ti=ti, t0=t0, w=w, i2=i2, r=r, pm=pm, pq=pq, pmb=pmb, pqb=pqb, src_=xsrc):
                        Z = Zt[i2]
                        zk = ZK[i2]
                        MEAN = STAT[:, 0, 0:w]
                        RSTD = STAT[:, 1, 0:w]

                        def d0():
                            act(MEAN, pm[:, 0:w], AF.Copy, R=(pmb,), W=(STb,))
                            P.op("dve", ("tt", RSTD, MEAN, MEAN, ALU.mult), R=(STb,), W=(STb,))
                            P.op("dve", ("tt", pq[:, 0:w], pq[:, 0:w], RSTD, ALU.subtract), R=(STb, pqb), W=(pqb,))
                            act(pq[:, 0:w], pq[:, 0:w], AF.Sqrt, R=(pqb, GLOB), W=(pqb,), bias=EPS[:, 0:1])
                            P.op("dve", ("recip", pq[:, 0:w], pq[:, 0:w]), R=(pqb,), W=(pqb,))

                        def dk(k):
                            P.op("dve", ("tt", Z[:, k, 0:w], Z[:, k, 0:w], pm[:, 0:w], ALU.subtract), R=(zk[k], pmb), W=(zk[k],))
                            P.op("dve", ("tt", Z[:, k, 0:w], Z[:, k, 0:w], pq[:, 0:w], ALU.mult), R=(zk[k], pqb), W=(zk[k],))
                            if want_h7:
                                act(HO[:, k, 0:w], Z[:, k, 0:w], AF.Identity, R=(zk[k], GLOB), W=(HOb,),
                                    bias=FB[:, l, 1, k, r:r + 1], scale=FS[:, l, 1, k, r:r + 1])
                            act(Z[:, k, 0:w], Z[:, k, 0:w], AF.Identity, R=(zk[k], CONSTB), W=(zk[k],),
                                bias=LNP[:, l, 1, 1, k:k + 1], scale=LNP[:, l, 1, 0, k:k + 1])
                            if k == KC - 1:
                                dst_ = outT[b] if last else xdst
                                P.op("sp", ("dma", dst_[:, :, t0:t0 + w].rearrange("k p t -> p k t"), Z[:, :, 0:w]),
                                     R=[Zb[i2]] + zk, W=(), dma=Zb[i2])
                                if (not last) and want_h7:
                                    store(H1[:, :, t0:t0 + w].rearrange("k p t -> p k t"), HO[:, :, 0:w], HOb)
                                if ti + 2 < ntg:
                                    p7_load_z(ti + 2, src_)

                        DQ.add(3, d0)
                        for k in range(KC):
                            DQ.add(4 + k, lambda k=k: dk(k))

                    ln_tail()
            DQ.flush()
            end_phase()
            xsrc = xdst
            if check_stop(l, b, "P7"):
                break

    P.op("sp", ("nop",))
    P.op("act", ("nop",))

    P.finalize(nc, engsem)
    assert _deadlock_check(P), 'deadlock in sync plan'
    with sem_stack, nc.Block() as block:
        @block.tensor
        def _(e):
            P.emit("pe", e, engsem)

        @block.scalar
        def _(e):
            P.emit("act", e, engsem)

        @block.vector
        def _(e):
            P.emit("dve", e, engsem)

        @block.gpsimd
        def _(e):
            P.emit("pool", e, engsem)

        @block.sync
        def _(e):
            P.emit("sp", e, engsem)
    return nc, len(P.ops)


def _const_tables():
    bf = ml_dtypes.bfloat16
    cb = np.zeros((128, 2176), np.float32)
    cb[:, 0:128] = np.eye(128)
    cb[:, 128:256] = 1.0
    cb[:, 256:384] = 1.0 / D
    ps = np.zeros((128, 128), np.float32)
    for i in range(32):
        ps[32 + i, i] = -1.0
        ps[i, 32 + i] = 1.0
        ps[96 + i, 64 + i] = -1.0
        ps[64 + i, 96 + i] = 1.0
    cb[:, 384:512] = ps
    kk = np.arange(128)[:, None]
    qq = np.arange(128)[None, :]
    lo = np.where(kk >= qq, 0.0, NEG).astype(np.float32)
    up = np.where(kk <= qq, 0.0, NEG).astype(np.float32)
    cb[:, 512:896] = np.tile(lo, (1, 3))
    cb[:, 896:1280] = np.tile(up, (1, 3))
    c = np.arange(128, dtype=np.float64)
    ang = 2 * np.pi * np.outer(c, c) / 128.0
    cb[:, 1280:1408] = np.cos(ang) / np.sqrt(128.0)
    cb[:, 1408:1536] = -np.sin(ang) / np.sqrt(128.0)
    cbf = cb.astype(bf)
    inv = (10000.0 ** (-np.arange(0, 64, 2, dtype=np.float32) / np.float32(64))).astype(np.float32)
    row = np.repeat(np.arange(T // 64, dtype=np.float32), 64)
    col = np.tile(np.arange(64, dtype=np.float32), T // 64)
    ar = (row[None, :] * inv[:, None]).astype(np.float32)
    ac = (col[None, :] * inv[:, None]).astype(np.float32)
    angp = np.concatenate([ar, ar, ac, ac], axis=0)
    rope = np.stack([np.cos(angp), np.sin(angp)], axis=1).astype(np.float32)
    t = np.arange(T, dtype=np.float64)
    full = 2 * np.pi * ((np.outer(t, t)) % T) / T
    Cm = (np.cos(full) / np.sqrt(T)).astype(np.float32)
    Sm = (np.sin(full) / np.sqrt(T)).astype(np.float32)
    dft = np.zeros((8, 128, KC, 2, 256), np.float32)
    for G in range(8):
        cs = Cm[:, G * 256:(G + 1) * 256].reshape(KC, 128, 256).transpose(1, 0, 2)
        ss = Sm[:, G * 256:(G + 1) * 256].reshape(KC, 128, 256).transpose(1, 0, 2)
        dft[G, :, :, 0, :] = cs
        dft[G, :, :, 1, :] = ss
    dft = dft.reshape(8, 128, SLOT).astype(bf)
    tc = np.arange(CL, dtype=np.float64)
    fc = 2 * np.pi * (np.outer(tc, tc) % CL) / CL
    Cc = (np.cos(fc) / np.sqrt(CL)).astype(np.float32).reshape(2, 128, 256).transpose(1, 0, 2)
    Sc = (np.sin(fc) / np.sqrt(CL)).astype(np.float32).reshape(2, 128, 256).transpose(1, 0, 2)
    dftc = np.stack([Cc, Sc], axis=2).reshape(128, 1024).astype(bf)
    return cbf, rope, dft, dftc


def _layout_weights(w_ada, b_ada, w_in, sink, w_four, w_out, ln_g, ln_b, w_up, conv_w, conv_b, w_down, nl):
    f = np.float32
    res = {}
    wa = np.asarray(w_ada[:nl], f).reshape(nl, KC, 128, 24, 4, 128)
    res["w_ada_t"] = np.ascontiguousarray(wa.transpose(0, 3, 2, 4, 1, 5)).reshape(nl, 24, 128, SLOT)
    res["b_ada_t"] = np.ascontiguousarray(np.asarray(b_ada[:nl], f).reshape(nl, 96, 128).transpose(2, 0, 1))
    wi = np.asarray(w_in[:nl], f)
    cols = []
    cols.append(np.arange(2560, 3072))
    cols.append(np.arange(0, 512))
    for h in range(NKV):
        cc = [np.arange(2048 + h * 128, 2048 + (h + 1) * 128)]
        for g in range(3):
            hq = 3 * h + g
            cc.append(np.arange(512 + hq * 128, 512 + (hq + 1) * 128))
        cols.append(np.concatenate(cc))
    wit = np.empty((nl, 6, 128, KC, 512), f)
    for G in range(6):
        wit[:, G] = wi[:, :, cols[G]].reshape(nl, KC, 128, 512).transpose(0, 2, 1, 3)
    res["w_in_t"] = wit.reshape(nl, 6, 128, SLOT)
    res["w_four_t"] = np.ascontiguousarray(
        np.asarray(w_four[:nl], f).reshape(nl, 4, 128, 512).transpose(0, 2, 1, 3)).reshape(nl, 128, 2048)
    wo = np.asarray(w_out[:nl], f).reshape(nl, KC, 128, 4, 512)
    res["w_out_t"] = np.ascontiguousarray(wo.transpose(0, 3, 2, 1, 4)).reshape(nl, 4, 128, SLOT)
    wu = np.asarray(w_up[:nl], f).reshape(nl, KC, 128, 2, 22, 2, 128)
    res["w_up_t"] = np.ascontiguousarray(wu.transpose(0, 4, 2, 5, 3, 1, 6)).reshape(nl, 22, 128, SLOT)
    wd = np.asarray(w_down[:nl], f).reshape(nl, NJ, 128, KC, 128)
    res["w_down_t"] = np.ascontiguousarray(wd.transpose(0, 3, 2, 1, 4)).reshape(nl, KC, 128, NJ * 128)
    cv = np.concatenate([np.asarray(conv_w[:nl], f), np.asarray(conv_b[:nl], f)[:, None, :]], axis=1)
    res["conv_t"] = np.ascontiguousarray(cv.reshape(nl, 4, 88, 128).transpose(3, 0, 1, 2))
    lg = np.stack([np.asarray(ln_g[:nl], f), np.asarray(ln_b[:nl], f)], axis=2)
    res["ln_t"] = np.ascontiguousarray(lg.reshape(nl, 2, 2, KC, 128).transpose(4, 0, 1, 2, 3))
    res["sink_t"] = np.ascontiguousarray(np.broadcast_to(np.asarray(sink[:nl], f).reshape(1, nl * NQ), (128, nl * NQ)))
    return res


def _core_inputs(x, c, ctx, c_ctx, b0, nbc):
    xs = np.asarray(x[b0:b0 + nbc], np.float32)
    cs = np.asarray(ctx[b0:b0 + nbc], np.float32)
    xa = np.concatenate([xs, cs], axis=1)
    x0T = np.ascontiguousarray(xa.reshape(nbc, NT, KC, 128).transpose(0, 2, 3, 1))
    rows = [np.asarray(c[b0 + i], np.float32) for i in range(nbc)]
    while len(rows) < 2:
        rows.append(rows[-1])
    rows.append(np.asarray(c_ctx, np.float32))
    cc = np.stack(rows, axis=0)
    cT = np.ascontiguousarray(cc.reshape(3, KC, 128).transpose(2, 1, 0))
    return x0T, cT


_CACHE = {}


def kernel(x, c, ctx, c_ctx, w_ada, b_ada, w_in, sink, w_four, w_out, ln_g, ln_b,
           w_up, conv_w, conv_b, w_down):
    ncores = 8
    nbc = 2
    if "nc" not in _CACHE:
        _CACHE["nc"] = build_nc(L_ALL, nbc)[0]
        _CACHE["consts"] = _const_tables()
    nc = _CACHE["nc"]
    cbf, rope, dft, dftc = _CACHE["consts"]
    wl = _layout_weights(w_ada, b_ada, w_in, sink, w_four, w_out, ln_g, ln_b, w_up, conv_w, conv_b, w_down, L_ALL)
    in_maps = []
    for ci in range(ncores):
        x0T, cT = _core_inputs(x, c, ctx, c_ctx, ci * nbc, nbc)
        m = dict(wl)
        m.update({"x0T": x0T, "cT": cT, "rope_t": rope, "cbf_t": cbf, "dft_t": dft, "dftc_t": dftc})
        in_maps.append(m)
    res = run_bass_kernel_spmd(nc, in_maps, core_ids=list(range(ncores)))
    out = np.empty((ncores * nbc, T, D), np.float32)
    for ci in range(ncores):
        o = res.results[ci]["outT"]
        out[ci * nbc:(ci + 1) * nbc] = o.transpose(0, 3, 1, 2).reshape(nbc, T, D)
    return out


def _deadlock_check(P):
    engs = {}
    for o in P.ops:
        engs.setdefault(o.eng, []).append(o)
    pos = {e: 0 for e in engs}
    sem = {}
    dsem = {}
    progress = True
    while progress:
        progress = False
        for e, lst in engs.items():
            while pos[e] < len(lst):
                o = lst[pos[e]]
                ok = all(sem.get(p.eng, 0) >= p.ms for p in o.cdeps) and \
                    all(dsem.get(ds.id, 0) >= val for (ds, val) in o.ddeps)
                if not ok:
                    break
                if o.ds is not None:
                    dsem[o.ds.id] = dsem.get(o.ds.id, 0) + 16
                elif o.need:
                    sem[e] = sem.get(e, 0) + 1
                    assert sem[e] == o.ms, (e, sem[e], o.ms)
                pos[e] += 1
                progress = True
    stuck = {e: (pos[e], len(lst)) for e, lst in engs.items() if pos[e] < len(lst)}
    for e, (p_, n_) in stuck.items():
        o = engs[e][p_]
        print("STUCK", e, p_, n_, o.ins[0], [(p.eng, p.ms, sem.get(p.eng, 0)) for p in o.cdeps],
              [(ds.id, val, dsem.get(ds.id, 0)) for (ds, val) in o.ddeps])
    return not stuck
```

```python
import contextlib
import os
import numpy as np
import ml_dtypes
import concourse.bass as bass
import concourse.mybir as mybir
from concourse.bass_utils import run_bass_kernel_spmd

F32 = mybir.dt.float32
BF16 = mybir.dt.bfloat16
AF = mybir.ActivationFunctionType
ALU = mybir.AluOpType

D = 2048
KC = 16
T = 2048
CL = 256
NT = T + CL
NTT = NT // 128
DFF = 5632
NJ = DFF // 128
L_ALL = 4
NQ = 12
NKV = 4
ALPHA = (2 * L_ALL) ** 0.25
LN_EPS = 1e-5
EPSP = LN_EPS / (ALPHA * ALPHA)
ATTN_SCALE = 128 ** -0.5
NEG = -30000.0
TGS = [(0, 512), (512, 512), (1024, 512), (1536, 512), (2048, 256)]
SLOT = 8192
NSLOT = 3
ARENA_BYTES = 136 * 1024
SAME_ENGINE_SYNC = True


class DmaSem:
    __slots__ = ("sem", "count", "id")

    def __init__(self, sem, i):
        self.sem = sem
        self.count = 0
        self.id = i


class Buf:
    __slots__ = ("name", "w", "r", "ds", "psum")

    def __init__(self, name, ds=None, psum=False):
        self.psum = psum
        self.name = name
        self.w = []
        self.r = []
        self.ds = ds


class Op:
    __slots__ = ("eng", "ins", "cdeps", "ddeps", "ms", "need", "ds", "dval", "idx")


COMPUTE = ("pe", "act", "dve", "pool")


class Prog:
    def __init__(self):
        self.ops = []
        self.last = {}
        self.lastc = {}
        self.phase_dma = []
        self.pending_barrier = {}

    def op(self, eng, ins, R=(), W=(), dma=None):
        o = Op()
        o.eng = eng
        o.ins = ins
        o.ms = 0
        o.need = False
        o.idx = len(self.ops)
        o.ds = None
        o.dval = 0
        cd = {}
        dd = {}

        def add(p):
            if p.ds is not None:
                k = p.ds.id
                if k not in dd or dd[k][1] < p.dval:
                    dd[k] = (p.ds, p.dval)
            else:
                q = cd.get(p.eng)
                if q is None or q.idx < p.idx:
                    cd[p.eng] = p

        pb = self.pending_barrier.pop(eng, None)
        if pb is not None:
            for p in pb:
                add(p)
        for b in R:
            for p in b.w:
                add(p)
            if b.psum:
                for p in b.r:
                    if p.eng != eng:
                        add(p)
        for b in W:
            if b.r or (b in R):
                for p in b.r:
                    add(p)
                for p in b.w:
                    add(p)
                b.r = []
                b.w = [o]
            else:
                for p in b.w:
                    if p.eng != eng:
                        add(p)
                b.w.append(o)
        for b in R:
            if b not in W:
                b.r.append(o)
        if dma is not None:
            ds = dma.ds
            ds.count += 16
            o.ds = ds
            o.dval = ds.count
            if eng != "pool":
                self.phase_dma.append(o)
        if eng in cd:
            if eng == "pe" or not SAME_ENGINE_SYNC or eng in ("sp", "pool"):
                del cd[eng]
        o.cdeps = list(cd.values())
        o.ddeps = list(dd.values())
        self.ops.append(o)
        self.last[eng] = o
        if dma is None and ins[0] != "nop":
            self.lastc[eng] = o
        return o

    def barrier(self):
        prev = [self.lastc[e] for e in COMPUTE if e in self.lastc] + self.phase_dma
        self.phase_dma = []
        for e in ("pe", "act", "dve", "sp"):
            cur = self.pending_barrier.get(e, [])
            self.pending_barrier[e] = cur + prev

    def finalize(self, nc, engsem):
        for o in self.ops:
            for p in o.cdeps:
                p.need = True
        cnt = {}
        for o in self.ops:
            if o.need:
                cnt[o.eng] = cnt.get(o.eng, 0) + 1
                o.ms = cnt[o.eng]

    def emit(self, eng, e, engsem):
        waited = {}
        n = 0
        for o in self.ops:
            if o.eng != eng:
                continue
            for p in o.cdeps:
                key = ("c", p.eng)
                if waited.get(key, 0) < p.ms:
                    e.wait_ge(engsem[p.eng], p.ms)
                    waited[key] = p.ms
            for (ds, val) in o.ddeps:
                key = ("d", ds.id)
                if waited.get(key, 0) < val:
                    e.wait_ge(ds.sem, val)
                    waited[key] = val
            ins = o.ins
            k = ins[0]
            r = None
            if k == "mm":
                r = e.matmul(ins[1], ins[2], ins[3], start=ins[4], stop=ins[5])
            elif k == "act":
                r = e.activation(out=ins[1], in_=ins[2], func=ins[3], bias=ins[4], scale=ins[5])
            elif k == "tt":
                r = e.tensor_tensor(out=ins[1], in0=ins[2], in1=ins[3], op=ins[4])
            elif k == "ts":
                r = e.tensor_scalar(ins[1], ins[2], ins[3], ins[4], ins[5], ins[6])
            elif k == "stt":
                r = e.scalar_tensor_tensor(ins[1], ins[2], ins[3], ins[4], ins[5], ins[6])
            elif k == "copy":
                r = e.tensor_copy(out=ins[1], in_=ins[2])
            elif k == "recip":
                r = e.reciprocal(ins[1], ins[2])
            elif k == "memset":
                r = e.memset(ins[1], ins[2])
            elif k == "dma":
                r = e.dma_start(out=ins[1], in_=ins[2])
            elif k == "nop":
                r = None
            else:
                raise ValueError(k)
            if r is not None:
                if o.ds is not None:
                    r.then_inc(o.ds.sem, 16)
                elif o.need:
                    r.then_inc(engsem[eng], 1)
            n += 1
        return n


class Deferred:
    def __init__(self):
        self.q = []

    def add(self, delay, fn):
        self.q.append([delay, fn])

    def tick(self):
        for it in self.q:
            it[0] -= 1
        ready = [it for it in self.q if it[0] <= 0]
        self.q = [it for it in self.q if it[0] > 0]
        for it in ready:
            it[1]()

    def flush(self):
        while self.q:
            self.q.sort(key=lambda it: it[0])
            self.q.pop(0)[1]()


class Arena:
    def __init__(self, ap_f32, nbytes):
        self.ap = ap_f32
        self.nbytes = nbytes
        self.off = 0

    def reset(self):
        self.off = 0

    def alloc(self, free_shape, dtype):
        esz = 4 if dtype == F32 else 2
        n = int(np.prod(free_shape))
        nb = (n * esz + 63) // 64 * 64
        assert self.off + nb <= self.nbytes, ("arena overflow", self.off, nb)
        v = self.ap[:, self.off // 4:(self.off + nb) // 4]
        self.off += nb
        if dtype != F32:
            v = v.bitcast(dtype)
        v = v[:, 0:n]
        if len(free_shape) == 2:
            v = v.rearrange("p (a b) -> p a b", b=free_shape[1])
        elif len(free_shape) == 3:
            v = v.rearrange("p (a b c) -> p a b c", b=free_shape[1], c=free_shape[2])
        elif len(free_shape) == 4:
            v = v.rearrange("p (a b c d) -> p a b c d", b=free_shape[1], c=free_shape[2], d=free_shape[3])
        return v


def build_nc(nlayers=L_ALL, nb=2, dump=(), stop_after=None):
    nc = bass.Bass("TRN2", target_bir_lowering=False)
    P = Prog()
    last_layer_idx = L_ALL - 1

    def din(name, shape, dt=F32):
        return nc.dram_tensor(name, list(shape), dt, kind="ExternalInput").ap()

    def dscr(name, shape, dt):
        kind = "ExternalOutput" if name in dump else "Internal"
        return nc.dram_tensor(name, list(shape), dt, kind=kind).ap()

    x0T = din("x0T", [nb, KC, 128, NT])
    cT = din("cT", [128, KC, 3])
    w_ada = din("w_ada_t", [nlayers, 24, 128, SLOT])
    b_ada = din("b_ada_t", [128, nlayers, 96])
    w_in = din("w_in_t", [nlayers, 6, 128, SLOT])
    w_four = din("w_four_t", [nlayers, 128, 2048])
    w_out = din("w_out_t", [nlayers, 4, 128, SLOT])
    w_up = din("w_up_t", [nlayers, 22, 128, SLOT])
    w_down = din("w_down_t", [nlayers, 16, 128, NJ * 128])
    conv_in = din("conv_t", [128, nlayers, 4, 88])
    ln_in = din("ln_t", [128, nlayers, 2, 2, KC])
    sink_in = din("sink_t", [128, nlayers * NQ])
    rope_in = din("rope_t", [128, 2, T])
    cbf_in = din("cbf_t", [128, 2176], BF16)
    dft_in = din("dft_t", [8, 128, SLOT], BF16)
    dftc_in = din("dftc_t", [128, 1024], BF16)
    outT = nc.dram_tensor("outT", [nb, KC, 128, T], F32, kind="ExternalOutput").ap()
    XA = dscr("XA", [KC, 128, NT], F32)
    XB = dscr("XB", [KC, 128, NT], F32)
    H1 = dscr("H1", [KC, 128, NT], BF16)
    H2 = dscr("H2", [KC, 128, NT], BF16)
    H1B = dscr("H1B", [KC, 128, NT], BF16)
    QT = dscr("QT", [NQ, 128, NT], BF16)
    KT = dscr("KT", [NKV, 128, NT], BF16)
    VV = dscr("VV", [NTT, 128, 512], BF16)
    FT = dscr("FT", [4, 128, NT], BF16)
    MIX = dscr("MIX", [KC, 128, NT], BF16)
    RR = dscr("RR", [NJ, 128, NT], BF16)
    MODD = dscr("MODD", [128, nlayers * 96 * 3], F32)
    WD16 = dscr("WD16", [nlayers, 2, 8, 128, 5632], BF16)
    WO16 = dscr("WO16", [nlayers, 4, 128, SLOT], BF16)

    def sb(name, shape, dt):
        return nc.alloc_sbuf_tensor(name, list(shape), dt)

    arena_t = sb("arena", [128, ARENA_BYTES // 4], F32)
    ring_t = [sb(f"ring{i}", [128, SLOT], BF16) for i in range(NSLOT)]
    cbf = sb("cbf", [128, 2176], BF16)
    MOD = sb("mod", [128, nlayers, 96, 3], F32)
    FS = sb("fs", [128, nlayers, 2, KC, 3], F32)
    FB = sb("fb", [128, nlayers, 2, KC, 3], F32)
    LNP = sb("lnp", [128, nlayers, 2, 2, KC], F32)
    CONV = sb("conv", [128, nlayers, 4, 88], F32)
    BADA = sb("bada", [128, nlayers, 96], F32)
    ES = sb("es", [128, nlayers * NQ], F32)
    CIN = sb("cin", [128, KC, 3], F32)
    SCB = sb("scb", [128, KC, 3], BF16)
    EPS = sb("eps", [128, 1], F32)
    psum = [nc.alloc_psum_tensor(f"ps{i}", [128, 512], F32) for i in range(8)]

    IDENT = cbf[:, 0:128]
    ONES = cbf[:, 128:256]
    ONESD = cbf[:, 256:384]
    PSWAP = cbf[:, 384:512]
    MASKLO = cbf[:, 512:896]
    MASKUP = cbf[:, 896:1280]
    C128S = cbf[:, 1280:1536]

    A = Arena(arena_t[:], ARENA_BYTES)

    sem_handles = []

    sem_stack = contextlib.ExitStack()

    def new_sem(name):
        h = sem_stack.enter_context(nc.semaphore(name))
        sem_handles.append(h)
        return h

    engsem = {e: new_sem("s_" + e) for e in ("pe", "act", "dve", "pool")}
    dsem_pool = [DmaSem(new_sem(f"d{i}"), i) for i in range(60)]
    dsem_free = list(dsem_pool)
    phase_sems = []

    def dbuf(name, persistent=False):
        ds = dsem_free.pop()
        if not persistent:
            phase_sems.append(ds)
        return Buf(name, ds)

    def end_phase():
        P.barrier()
        dsem_free.extend(phase_sems)
        phase_sems.clear()
        A.reset()

    PS = [Buf(f"ps{i}", psum=True) for i in range(8)]
    RING = [dbuf(f"ring{i}", persistent=True) for i in range(NSLOT)]
    ring_ctr = [0]
    CONSTB = dbuf("consts", persistent=True)
    GLOB = Buf("glob")

    PRE = [dbuf(f"pre{i}", persistent=True) for i in range(nlayers)]

    def precast_steps(l):
        steps = []
        for G in range(4):
            steps.append(lambda G=G: P.op("pool", ("dma", WO16[l, G].rearrange("p (a b) -> p a b", b=2048),
                                                   w_out[l, G].rearrange("p (a b) -> p a b", b=2048)),
                                          R=(), W=(PRE[l],), dma=PRE[l]))
        for m in range(KC):
            def f(m=m):
                src = w_down[l, m].rearrange("p (h a b) -> p h a b", h=2, b=1408)
                dst = WD16[l, :, m // 2, :, (m % 2) * 2816:(m % 2 + 1) * 2816].rearrange("h p (a b) -> p h a b", b=1408)
                P.op("pool", ("dma", dst, src), R=(), W=(PRE[l],), dma=PRE[l])
            steps.append(f)
        return steps

    pre_pending = {}

    def wload(src_ap, nelem, extra_R=()):
        i = ring_ctr[0] % NSLOT
        ring_ctr[0] += 1
        dst_ap = ring_t[i][:, 0:nelem]
        if nelem > 2048:
            bb = 2048 if nelem % 2048 == 0 else 1408
            dst_ap = dst_ap.rearrange("p (a b) -> p a b", b=bb)
            src_ap = src_ap.rearrange("p (a b) -> p a b", b=bb)
        P.op("pool", ("dma", dst_ap, src_ap), R=extra_R, W=(RING[i],), dma=RING[i])
        return ring_t[i], RING[i]

    def mm(out, lhsT, rhs, start, stop, R, W):
        P.op("pe", ("mm", out, lhsT, rhs, start, stop), R=R, W=W)

    def act(out, in_, func, R, W, bias=0.0, scale=1.0):
        P.op("act", ("act", out, in_, func, bias, scale), R=R, W=W)

    def load(dst, src, buf):
        P.op("sp", ("dma", dst, src), R=(), W=(buf,), dma=buf)

    def store(dst, src, buf):
        P.op("sp", ("dma", dst, src), R=(buf,), W=(), dma=buf)

    load(cbf[:], cbf_in[:], CONSTB)
    load(CIN[:], cT[:], CONSTB)
    load(BADA[:], b_ada[:], CONSTB)
    load(LNP[:], ln_in[:], CONSTB)
    load(CONV[:], conv_in[:], CONSTB)
    load(ES[:], sink_in[:], CONSTB)
    P.op("dve", ("memset", EPS[:], EPSP), R=(), W=(GLOB,))
    act(ES[:], ES[:], AF.Exp, R=(CONSTB,), W=(CONSTB, GLOB))
    act(SCB[:], CIN[:], AF.Silu, R=(CONSTB,), W=(GLOB,))
    def ada_layer(l):
        pb = PS[l % 2]
        pt = psum[l % 2]
        for G in range(24):
            wt, wb = wload(w_ada[l, G], SLOT)
            wv = wt[:].rearrange("p (m k c) -> p m k c", m=4, k=KC)
            for mi in range(4):
                m = G * 4 + mi
                for k in range(KC):
                    mm(pt[:, m * 3:m * 3 + 3], wv[:, mi, k, :], SCB[:, k, :], k == 0, k == KC - 1,
                       R=(wb, GLOB), W=(pb,))
        for r in range(3):
            P.op("dve", ("tt", MOD[:, l, :, r], pt[:, 0:288].rearrange("p (m r) -> p m r", r=3)[:, :, r], BADA[:, l, :], ALU.add),
                 R=(pb, CONSTB), W=(GLOB,))
        for (lo, mul, add_) in ((16, 1.0, 1.0), (64, 1.0, 1.0), (32, 1.0 / ALPHA, 0.0), (80, 1.0 / ALPHA, 0.0)):
            P.op("dve", ("ts", MOD[:, l, lo:lo + 16, :], MOD[:, l, lo:lo + 16, :], mul, add_, ALU.mult, ALU.add),
                 R=(GLOB,), W=(GLOB,))
    def mod_ap(l, idx, k, r):
        return MOD[:, l, idx * 16 + k, r:r + 1]

    def mod0(b):
        dst = H1 if b == 0 else H1B
        for k in range(KC):
            i = k % 6
            load(XI[i], x0T[b][k], XIb[i])
            if k % 2 == 0:
                act(HOo[i][:, 0:T], XI[i][:, 0:T], AF.Identity, R=(XIb[i], GLOB), W=(HOb[i],),
                    bias=mod_ap(0, 0, k, b), scale=mod_ap(0, 1, k, b))
                act(HOo[i][:, T:NT], XI[i][:, T:NT], AF.Identity, R=(XIb[i], GLOB), W=(HOb[i],),
                    bias=mod_ap(0, 0, k, 2), scale=mod_ap(0, 1, k, 2))
            else:
                P.op("dve", ("ts", HOo[i][:, 0:T], XI[i][:, 0:T], mod_ap(0, 1, k, b), mod_ap(0, 0, k, b),
                             ALU.mult, ALU.add), R=(XIb[i], GLOB), W=(HOb[i],))
                P.op("dve", ("ts", HOo[i][:, T:NT], XI[i][:, T:NT], mod_ap(0, 1, k, 2), mod_ap(0, 0, k, 2),
                             ALU.mult, ALU.add), R=(XIb[i], GLOB), W=(HOb[i],))
            store(dst[k], HOo[i], HOb[i])

    ada_layer(0)
    XI = [A.alloc([NT], F32) for _ in range(6)]
    HOo = [A.alloc([NT], BF16) for _ in range(6)]
    XIb = [dbuf(f"xi{i}") for i in range(6)]
    HOb = [dbuf(f"ho{i}") for i in range(6)]
    for b_ in range(nb):
        mod0(b_)
    for l_ in range(1, nlayers):
        ada_layer(l_)
    for l in range(nlayers):
        for which in range(2):
            if which == 0:
                s_lo, b_lo, ls = 64, 48, l
            else:
                if l + 1 >= nlayers:
                    continue
                s_lo, b_lo, ls = 16, 0, l + 1
            for r in range(3):
                g_ap = LNP[:, l, which, 0, :]
                be_ap = LNP[:, l, which, 1, :]
                P.op("dve", ("tt", FS[:, l, which, :, r], MOD[:, ls, s_lo:s_lo + 16, r], g_ap, ALU.mult),
                     R=(GLOB, CONSTB), W=(GLOB,))
                P.op("dve", ("tt", FB[:, l, which, :, r], MOD[:, ls, s_lo:s_lo + 16, r], be_ap, ALU.mult),
                     R=(GLOB, CONSTB), W=(GLOB,))
                P.op("dve", ("tt", FB[:, l, which, :, r], FB[:, l, which, :, r], MOD[:, ls, b_lo:b_lo + 16, r], ALU.add),
                     R=(GLOB,), W=(GLOB,))
    if "MODD" in dump:
        MODb = dbuf("modd")
        P.op("sp", ("dma", MODD[:], MOD[:].rearrange("p l m r -> p (l m r)")), R=(GLOB, MODb), W=(), dma=MODb)
    end_phase()

    stopped = [False]

    def check_stop(l, b, ph):
        if stop_after is not None and (l, b, ph) == tuple(stop_after):
            stopped[0] = True
        return stopped[0]

    def ln_sched(DQ, Z, ZSQ, HO, STAT, w, l, which, r, zbuf, hbuf, sqbuf, stbuf, want_h, finish, delays, zk):
        pm, pq = psum[6], psum[7]
        MEAN = STAT[:, 0, 0:w]
        RSTD = STAT[:, 1, 0:w]
        two_pass = ZSQ is None

        def stepA():
            for hh in range(2):
                ks = slice(hh * 8, hh * 8 + 8)
                act(HO[:, ks, 0:w], Z[:, ks, 0:w], AF.Copy, R=[zbuf] + zk[ks], W=(hbuf,))
                if not two_pass:
                    act(ZSQ[:, ks, 0:w], Z[:, ks, 0:w], AF.Square, R=[zbuf] + zk[ks], W=(sqbuf,))

        def stepB():
            for k in range(KC):
                mm(pm[:, 0:w], ONESD, HO[:, k, 0:w], k == 0, k == KC - 1, R=(hbuf, CONSTB), W=(PS[6],))
            if not two_pass:
                for k in range(KC):
                    mm(pq[:, 0:w], ONESD, ZSQ[:, k, 0:w], k == 0, k == KC - 1, R=(sqbuf, CONSTB), W=(PS[7],))
            act(MEAN, pm[:, 0:w], AF.Copy, R=(PS[6],), W=(stbuf,))
            if two_pass:
                for hh in range(2):
                    ks = slice(hh * 8, hh * 8 + 8)
                    act(HO[:, ks, 0:w], Z[:, ks, 0:w], AF.Square, R=[zbuf] + zk[ks], W=(hbuf,))

        def stepC():
            if two_pass:
                for k in range(KC):
                    mm(pq[:, 0:w], ONESD, HO[:, k, 0:w], k == 0, k == KC - 1, R=(hbuf, CONSTB), W=(PS[7],))

        def stepD0():
            P.op("dve", ("tt", RSTD, MEAN, MEAN, ALU.mult), R=(stbuf,), W=(stbuf,))
            P.op("dve", ("tt", pq[:, 0:w], pq[:, 0:w], RSTD, ALU.subtract), R=(stbuf, PS[7]), W=(PS[7],))
            act(pq[:, 0:w], pq[:, 0:w], AF.Sqrt, R=(PS[7], GLOB), W=(PS[7],), bias=EPS[:, 0:1])
            P.op("dve", ("recip", pq[:, 0:w], pq[:, 0:w]), R=(PS[7],), W=(PS[7],))

        def stepDk(k0, k1):
            for k in range(k0, k1):
                P.op("dve", ("tt", Z[:, k, 0:w], Z[:, k, 0:w], pm[:, 0:w], ALU.subtract), R=(zk[k], PS[6]), W=(zk[k],))
                P.op("dve", ("tt", Z[:, k, 0:w], Z[:, k, 0:w], pq[:, 0:w], ALU.mult), R=(zk[k], PS[7]), W=(zk[k],))
                if want_h:
                    act(HO[:, k, 0:w], Z[:, k, 0:w], AF.Identity, R=(zk[k], GLOB), W=(hbuf,),
                        bias=FB[:, l, which, k, r:r + 1], scale=FS[:, l, which, k, r:r + 1])
                act(Z[:, k, 0:w], Z[:, k, 0:w], AF.Identity, R=(zk[k], CONSTB), W=(zk[k],),
                    bias=LNP[:, l, which, 1, k:k + 1], scale=LNP[:, l, which, 0, k:k + 1])
            if k1 == KC:
                finish()

        DQ.add(delays[0], stepA)
        DQ.add(delays[1], stepB)
        DQ.add(delays[2], stepC)
        DQ.add(delays[3], stepD0)
        kpt = delays[4]
        for i_, k0 in enumerate(range(0, KC, kpt)):
            DQ.add(delays[3] + 1 + i_, lambda k0=k0: stepDk(k0, k0 + kpt))

    if stop_after is not None and stop_after[2] == "ADA":
        stopped[0] = True
    for b in range(nb):
        if stopped[0]:
            break
        xsrc = x0T[b]
        for l in range(nlayers):
            if stopped[0]:
                break
            last = (l == last_layer_idx)
            tgs_q = TGS[:4] if last else TGS

            if b == 0:
                for f_ in pre_pending.pop(l, precast_steps(l) if l == 0 else []):
                    f_()
            HT = A.alloc([KC, NT], BF16)
            ROPE = A.alloc([2, T], F32)
            OUTC = [A.alloc([NT], BF16) for _ in range(2)]
            VST = [A.alloc([512], BF16) for _ in range(2)]
            XBb_t = [A.alloc([512], BF16) for _ in range(2)]
            T1 = [A.alloc([512], F32) for _ in range(2)]
            T2 = [A.alloc([512], F32) for _ in range(2)]
            HTb = [dbuf(f"ht{i}") for i in range(5)]
            ROPEb = dbuf("rope")
            OUTCb = [dbuf(f"outc{i}") for i in range(2)]
            VSTb = [dbuf(f"vst{i}") for i in range(2)]
            XBb = [Buf(f"xb{i}") for i in range(2)]
            T1b = [Buf(f"t1{i}") for i in range(2)]
            T2b = [Buf(f"t2{i}") for i in range(2)]
            for n_, (t0_, w_) in enumerate(TGS):
                for kh in range(2):
                    h1src = H1B if (l == 0 and b == 1) else H1
                    load(HT[:, kh * 8:kh * 8 + 8, t0_:t0_ + w_],
                         h1src[kh * 8:kh * 8 + 8, :, t0_:t0_ + w_].rearrange("k p t -> p k t"), HTb[n_])
                if n_ == 0:
                    load(ROPE[:], rope_in[:], ROPEb)
            wt, wb = wload(w_in[l, 0], SLOT)
            wv = wt[:].rearrange("p (k c) -> p k c", k=KC)
            for tt in range(NTT):
                bk = tt % 4
                for k in range(KC):
                    mm(psum[bk][:], HT[:, k, tt * 128:(tt + 1) * 128], wv[:, k, :], k == 0, k == KC - 1,
                       R=(wb, HTb[tt // 4]), W=(PS[bk],))
                i = tt % 2
                act(VST[i][:], psum[bk][:], AF.Copy, R=(PS[bk],), W=(VSTb[i],))
                store(VV[tt], VST[i], VSTb[i])
            tile_ctr = 0
            oc_ctr = 0
            pending = None

            def rope_tail(pd):
                (bk_, sw_, i_, oc_, t0_, w_, ocb_) = pd
                mm(psum[sw_][:, 0:w_], PSWAP, XBb_t[i_][:, 0:w_], True, True, R=(XBb[i_], CONSTB), W=(PS[sw_],))
                P.op("dve", ("tt", psum[bk_][:, 0:w_], psum[bk_][:, 0:w_], ROPE[:, 0, t0_:t0_ + w_], ALU.mult),
                     R=(PS[bk_], ROPEb, XBb[i_]), W=(PS[bk_],))
                P.op("dve", ("tt", T2[i_][:, 0:w_], psum[sw_][:, 0:w_], ROPE[:, 1, t0_:t0_ + w_], ALU.mult),
                     R=(PS[sw_], ROPEb), W=(T2b[i_],))
                P.op("dve", ("tt", oc_[:, t0_:t0_ + w_], psum[bk_][:, 0:w_], T2[i_][:, 0:w_], ALU.add),
                     R=(PS[bk_], T2b[i_]), W=(ocb_,))

            for G in range(1, 6):
                wt, wb = wload(w_in[l, G], SLOT)
                wv = wt[:].rearrange("p (k c) -> p k c", k=KC)
                for mi in range(4):
                    is_f = (G == 1)
                    is_k = (not is_f) and mi == 0
                    tgl = TGS if (is_k or not last) else TGS[:4]
                    oc = OUTC[oc_ctr % 2]
                    ocb = OUTCb[oc_ctr % 2]
                    oc_ctr += 1
                    for (t0, w) in tgl:
                        bk = tile_ctr % 4
                        for k in range(KC):
                            mm(psum[bk][:, 0:w], wv[:, k, mi * 128:(mi + 1) * 128], HT[:, k, t0:t0 + w],
                               k == 0, k == KC - 1, R=(wb, HTb[t0 // 512]), W=(PS[bk],))
                        if pending is not None:
                            rope_tail(pending)
                            pending = None
                        if is_f or t0 >= T:
                            act(oc[:, t0:t0 + w], psum[bk][:, 0:w], AF.Copy, R=(PS[bk],), W=(ocb,))
                        else:
                            i = tile_ctr % 2
                            act(XBb_t[i][:, 0:w], psum[bk][:, 0:w], AF.Copy, R=(PS[bk],), W=(XBb[i],))
                            pending = (bk, 4 + i, i, oc, t0, w, ocb)
                        tile_ctr += 1
                    if pending is not None:
                        rope_tail(pending)
                        pending = None
                    wcols = tgl[-1][0] + tgl[-1][1]
                    if is_f:
                        dst = FT[mi]
                    elif is_k:
                        dst = KT[G - 2]
                    else:
                        dst = QT[3 * (G - 2) + mi - 1]
                    store(dst[:, 0:wcols], oc[:, 0:wcols], ocb)
            end_phase()
            if check_stop(l, b, "P2"):
                break

            KTs = [A.alloc([NT], BF16) for _ in range(2)]
            QTs = [A.alloc([3, NT], BF16) for _ in range(2)]
            VH = [A.alloc([NTT, 128], BF16) for _ in range(2)]
            OT = [A.alloc([3, NT], BF16) for _ in range(2)]
            PT = [A.alloc([384], BF16) for _ in range(4)]
            REC = [A.alloc([384], F32) for _ in range(2)]
            KQVb = [dbuf(f"kqv{i}") for i in range(2)]
            OTb = [dbuf(f"ot{i}") for i in range(2)]
            PTb = [Buf(f"pt{i}") for i in range(4)]
            RECb = [Buf(f"rec{i}") for i in range(2)]
            RECb2 = [Buf(f"recb{i}") for i in range(2)]
            nqb = 16 if last else 18
            pt_ctr = 0
            s_ctr = 0
            pend_norm = []
            LA = 2
            wq = T if last else NT

            def p3_load(h_):
                j2 = h_ % 2
                load(KTs[j2], KT[h_], KQVb[j2])
                load(QTs[j2][:, :, 0:wq], QT[3 * h_:3 * h_ + 3, :, 0:wq].rearrange("g p t -> p g t"), KQVb[j2])
                load(VH[j2], VV[:, :, h_ * 128:(h_ + 1) * 128].rearrange("t p c -> p t c"), KQVb[j2])

            p3_load(0)
            items = []
            for h in range(NKV):
                for qb in range(nqb):
                    if qb < 16:
                        keys = []
                        if qb > 0:
                            keys.append((qb - 1, MASKLO))
                        keys.append((qb, None))
                        if qb < 15:
                            keys.append((qb + 1, MASKUP))
                        keys += [(16, None), (17, None)]
                    else:
                        keys = [(16, None), (17, None)]
                    for ii, (kb, mask) in enumerate(keys):
                        items.append((h, qb, ii, kb, mask, len(keys)))
            ptis = {}

            def norm(h, qb):
                i2 = h % 2
                ob = qb % 2
                po, psm = psum[3 + ob], psum[5 + ob]
                pob, psb = PS[3 + ob], PS[5 + ob]
                rc = REC[ob]
                for g in range(3):
                    hq = 3 * h + g
                    P.op("dve", ("ts", rc[:, g * 128:(g + 1) * 128], psm[:, g * 128:(g + 1) * 128],
                                 ES[:, l * NQ + hq:l * NQ + hq + 1], None, ALU.add, ALU.bypass),
                         R=(psb, GLOB), W=(RECb[ob], RECb2[ob]))
                act(rc[:, 192:384], rc[:, 192:384], AF.Ln, R=(RECb2[ob],), W=(RECb2[ob],))
                act(rc[:, 192:384], rc[:, 192:384], AF.Exp, R=(RECb2[ob],), W=(RECb2[ob],), scale=-1.0)
                P.op("dve", ("recip", rc[:, 0:192], rc[:, 0:192]), R=(RECb[ob],), W=(RECb[ob],))
                P.op("dve", ("tt", OT[i2][:, :, qb * 128:(qb + 1) * 128],
                             po[:, 0:384].rearrange("p (g q) -> p g q", g=3),
                             rc[:].rearrange("p (g q) -> p g q", g=3), ALU.mult),
                     R=(pob, RECb[ob], RECb2[ob]), W=(OTb[i2],))
                if qb == nqb - 1:
                    store(MIX[4 + 3 * h:4 + 3 * h + 3, :, 0:wq].rearrange("g p t -> p g t"), OT[i2][:, :, 0:wq], OTb[i2])

            pend_n = []
            for n in range(len(items) + LA):
                if n < len(items):
                    (h, qb, ii, kb, mask, nk) = items[n]
                    i2 = h % 2
                    sb_ = s_ctr % 3
                    s_ctr += 1
                    qsl = QTs[i2][:, :, qb * 128:(qb + 1) * 128]
                    mm(psum[sb_][:, 0:384], KTs[i2][:, kb * 128:(kb + 1) * 128], qsl, True, mask is None,
                       R=(KQVb[i2],), W=(PS[sb_],))
                    if mask is not None:
                        mm(psum[sb_][:, 0:384], IDENT, mask, False, True, R=(CONSTB,), W=(PS[sb_],))
                    pti = pt_ctr % 4
                    pt_ctr += 1
                    ptis[n] = pti
                    act(PT[pti][:], psum[sb_][:, 0:384], AF.Exp, R=(PS[sb_],), W=(PTb[pti],), scale=ATTN_SCALE)
                m_ = n - LA
                if m_ >= 0:
                    (h, qb, ii, kb, mask, nk) = items[m_]
                    i2 = h % 2
                    ob = qb % 2
                    if qb == 0 and ii == 0 and h + 1 < NKV:
                        p3_load(h + 1)
                    pti = ptis.pop(m_)
                    mm(psum[3 + ob][:, 0:384], VH[i2][:, kb, :], PT[pti][:], ii == 0, ii == nk - 1,
                       R=(KQVb[i2], PTb[pti]), W=(PS[3 + ob],))
                    mm(psum[5 + ob][:, 0:384], ONES, PT[pti][:], ii == 0, ii == nk - 1,
                       R=(CONSTB, PTb[pti]), W=(PS[5 + ob],))
                    if ii == 1 and pend_n:
                        norm(*pend_n.pop(0))
                    if ii == nk - 1:
                        pend_n.append((h, qb))
            while pend_n:
                norm(*pend_n.pop(0))
            end_phase()
            if check_stop(l, b, "P3"):
                break

            FTs = A.alloc([4, NT], BF16)
            ZCS = A.alloc([NTT, 4, 256], BF16)
            YT = A.alloc([4, NT], BF16)
            OUTC = [A.alloc([NT], BF16) for _ in range(2)]
            DFC = A.alloc([2, 2, 256], BF16)
            FTb = dbuf("fts")
            DFCb = dbuf("dfc")
            ZCSb = Buf("zcs")
            YTb = Buf("yt")
            OUTCb = [dbuf(f"outc{i}") for i in range(2)]
            ntt_f = 16 if last else NTT
            wf = T if last else NT
            load(FTs[:, :, 0:wf], FT[:, :, 0:wf].rearrange("g p t -> p g t"), FTb)
            load(DFC[:], dftc_in[:], DFCb)
            for tt in range(ntt_f):
                for gp in range(2):
                    bk = (tt * 2 + gp) % 4
                    for gi in range(2):
                        g = gp * 2 + gi
                        mm(psum[bk][:, gi * 256:(gi + 1) * 256], FTs[:, g, tt * 128:(tt + 1) * 128], C128S, True, True,
                           R=(FTb, CONSTB), W=(PS[bk],))
                    act(ZCS[:, tt, gp * 2:gp * 2 + 2, :], psum[bk][:].rearrange("p (g c) -> p g c", g=2), AF.Copy,
                        R=(PS[bk],), W=(ZCSb,))
            tile_ctr = 0
            for Gt in range(8):
                wt, wb = wload(dft_in[Gt], SLOT)
                wv = wt[:].rearrange("p (k s c) -> p k s c", k=KC, s=2)
                for g in range(4):
                    bk = tile_ctr % 4
                    tile_ctr += 1
                    for k in range(KC):
                        mm(psum[bk][:, 0:256], ZCS[:, k, g, 0:128], wv[:, k, 0, :], k == 0, False,
                           R=(ZCSb, wb), W=(PS[bk],))
                        mm(psum[bk][:, 0:256], ZCS[:, k, g, 128:256], wv[:, k, 1, :], False, k == KC - 1,
                           R=(ZCSb, wb), W=(PS[bk],))
                    act(YT[:, g, Gt * 256:(Gt + 1) * 256], psum[bk][:, 0:256], AF.Copy, R=(PS[bk],), W=(YTb,))
            if not last:
                for g in range(4):
                    bk = tile_ctr % 4
                    tile_ctr += 1
                    for k in range(2):
                        mm(psum[bk][:, 0:256], ZCS[:, 16 + k, g, 0:128], DFC[:, k, 0, :], k == 0, False,
                           R=(ZCSb, DFCb), W=(PS[bk],))
                        mm(psum[bk][:, 0:256], ZCS[:, 16 + k, g, 128:256], DFC[:, k, 1, :], False, k == 1,
                           R=(ZCSb, DFCb), W=(PS[bk],))
                    act(YT[:, g, T:NT], psum[bk][:, 0:256], AF.Copy, R=(PS[bk],), W=(YTb,))
            wt, wb = wload(w_four[l], 2048)
            wv = wt[:, 0:2048].rearrange("p (k c) -> p k c", k=4)
            for mi in range(4):
                oc = OUTC[mi % 2]
                ocb = OUTCb[mi % 2]
                for (t0, w) in tgs_q:
                    bk = tile_ctr % 4
                    tile_ctr += 1
                    for k in range(4):
                        mm(psum[bk][:, 0:w], wv[:, k, mi * 128:(mi + 1) * 128], YT[:, k, t0:t0 + w], k == 0, k == 3,
                           R=(wb, YTb), W=(PS[bk],))
                    act(oc[:, t0:t0 + w], psum[bk][:, 0:w], AF.Copy, R=(PS[bk],), W=(ocb,))
                store(MIX[mi][:, 0:wf], oc[:, 0:wf], ocb)
            end_phase()
            if check_stop(l, b, "P4"):
                break

            xdst = XB
            MX = [A.alloc([KC, 512], BF16) for _ in range(2)]
            Zt = [A.alloc([KC, 512], F32) for _ in range(2)]
            ZB2 = [A.alloc([512], BF16) for _ in range(4)]
            SQ2 = [A.alloc([512], BF16) for _ in range(4)]
            HO = A.alloc([KC, 512], BF16)
            STAT = A.alloc([2, 512], F32)
            MXb = [dbuf(f"mx{i}") for i in range(2)]
            Zb = [dbuf(f"z{i}") for i in range(2)]
            ZK = [[Buf(f"zk{i}_{k}") for k in range(KC)] for i in range(2)]
            ZB2b = [Buf(f"zb2{i}") for i in range(4)]
            SQ2b = [Buf(f"sq2{i}") for i in range(4)]
            HOb = dbuf("ho")
            STb = Buf("stat")
            tile_ctr = 0
            DQ = Deferred()
            ntg = len(tgs_q)

            def p5_load_mx(ti_):
                t0_, w_ = tgs_q[ti_]
                load(MX[ti_ % 2][:, :, 0:w_], MIX[:, :, t0_:t0_ + w_].rearrange("k p t -> p k t"), MXb[ti_ % 2])

            def p5_load_z(ti_, src_):
                t0_, w_ = tgs_q[ti_]
                load(Zt[ti_ % 2][:, :, 0:w_], src_[:, :, t0_:t0_ + w_].rearrange("k p t -> p k t"), Zb[ti_ % 2])

            p5_load_mx(0)
            p5_load_z(0, xsrc)
            if ntg > 1:
                p5_load_z(1, xsrc)
            st_ctr = 0
            for ti, (t0, w) in enumerate(tgs_q):
                i2 = ti % 2
                r = b if t0 < T else 2
                sp_ = 4 + 2 * (ti % 2)
                pm, pq = psum[sp_], psum[sp_ + 1]
                pmb, pqb = PS[sp_], PS[sp_ + 1]
                if ti + 1 < ntg:
                    p5_load_mx(ti + 1)
                for G in range(4):
                    wt, wb = wload(WO16[l, G], SLOT, extra_R=(PRE[l],))
                    wv = wt[:].rearrange("p (k c) -> p k c", k=KC)
                    for mi in range(4):
                        m = G * 4 + mi
                        bk = tile_ctr % 4
                        tile_ctr += 1
                        for k in range(KC):
                            mm(psum[bk][:, 0:w], wv[:, k, mi * 128:(mi + 1) * 128], MX[i2][:, k, 0:w],
                               k == 0, k == KC - 1, R=(wb, MXb[i2]), W=(PS[bk],))
                        P.op("dve", ("stt", Zt[i2][:, m, 0:w], psum[bk][:, 0:w], mod_ap(l, 2, m, r),
                                     Zt[i2][:, m, 0:w], ALU.mult, ALU.add),
                             R=(PS[bk], Zb[i2], ZK[i2][m], GLOB), W=(ZK[i2][m],))
                        si = st_ctr % 4
                        st_ctr += 1
                        act(ZB2[si][:, 0:w], Zt[i2][:, m, 0:w], AF.Copy, R=(ZK[i2][m],), W=(ZB2b[si],))
                        act(SQ2[si][:, 0:w], Zt[i2][:, m, 0:w], AF.Square, R=(ZK[i2][m],), W=(SQ2b[si],))

                        def stat_mm(m=m, si=si, w=w, pm=pm, pq=pq, pmb=pmb, pqb=pqb):
                            mm(pm[:, 0:w], ONESD, ZB2[si][:, 0:w], m == 0, m == KC - 1, R=(ZB2b[si], CONSTB), W=(pmb,))
                            mm(pq[:, 0:w], ONESD, SQ2[si][:, 0:w], m == 0, m == KC - 1, R=(SQ2b[si], CONSTB), W=(pqb,))

                        DQ.tick()
                        DQ.add(2, stat_mm)

                def ln_tail(ti=ti, t0=t0, w=w, i2=i2, r=r, pm=pm, pq=pq, pmb=pmb, pqb=pqb, src_=xsrc):
                    Z = Zt[i2]
                    zk = ZK[i2]
                    MEAN = STAT[:, 0, 0:w]
                    RSTD = STAT[:, 1, 0:w]

                    def d0():
                        act(MEAN, pm[:, 0:w], AF.Copy, R=(pmb,), W=(STb,))
                        P.op("dve", ("tt", RSTD, MEAN, MEAN, ALU.mult), R=(STb,), W=(STb,))
                        P.op("dve", ("tt", pq[:, 0:w], pq[:, 0:w], RSTD, ALU.subtract), R=(STb, pqb), W=(pqb,))
                        act(pq[:, 0:w], pq[:, 0:w], AF.Sqrt, R=(pqb, GLOB), W=(pqb,), bias=EPS[:, 0:1])
                        P.op("dve", ("recip", pq[:, 0:w], pq[:, 0:w]), R=(pqb,), W=(pqb,))

                    def dk(k0, k1):
                        for k in range(k0, k1):
                            P.op("dve", ("tt", Z[:, k, 0:w], Z[:, k, 0:w], pm[:, 0:w], ALU.subtract), R=(zk[k], pmb), W=(zk[k],))
                            P.op("dve", ("tt", Z[:, k, 0:w], Z[:, k, 0:w], pq[:, 0:w], ALU.mult), R=(zk[k], pqb), W=(zk[k],))
                            act(HO[:, k, 0:w], Z[:, k, 0:w], AF.Identity, R=(zk[k], GLOB), W=(HOb,),
                                bias=FB[:, l, 0, k, r:r + 1], scale=FS[:, l, 0, k, r:r + 1])
                            act(Z[:, k, 0:w], Z[:, k, 0:w], AF.Identity, R=(zk[k], CONSTB), W=(zk[k],),
                                bias=LNP[:, l, 0, 1, k:k + 1], scale=LNP[:, l, 0, 0, k:k + 1])
                        if k1 == KC // 2:
                            P.op("sp", ("dma", xdst[0:8, :, t0:t0 + w].rearrange("k p t -> p k t"), Z[:, 0:8, 0:w]),
                                 R=[Zb[i2]] + zk[0:8], W=(), dma=Zb[i2])
                        if k1 == KC:
                            P.op("sp", ("dma", xdst[8:16, :, t0:t0 + w].rearrange("k p t -> p k t"), Z[:, 8:16, 0:w]),
                                 R=[Zb[i2]] + zk[8:16], W=(), dma=Zb[i2])
                            store(H2[:, :, t0:t0 + w].rearrange("k p t -> p k t"), HO[:, :, 0:w], HOb)
                            if ti + 2 < ntg:
                                p5_load_z(ti + 2, src_)

                    DQ.add(3, d0)
                    for i_, k0 in enumerate(range(0, KC, 2)):
                        DQ.add(4 + i_, lambda k0=k0: dk(k0, k0 + 2))

                ln_tail()
            DQ.flush()
            end_phase()
            xsrc = xdst
            if check_stop(l, b, "P5"):
                break

            CW = 2308
            LAT0, CTX0 = 1, 2051
            HT = A.alloc([KC, NT], BF16)
            CAt = [A.alloc([CW], F32) for _ in range(2)]
            CGt = [A.alloc([CW], F32) for _ in range(2)]
            RO = [A.alloc([NT], BF16) for _ in range(2)]
            HTb = [dbuf(f"ht{i}") for i in range(5)]
            ROb = [dbuf(f"ro{i}") for i in range(2)]
            CAb = [[Buf(f"ca{i}_{n}") for n in range(5)] for i in range(2)]
            CGb = [[Buf(f"cg{i}_{n}") for n in range(5)] for i in range(2)]
            wq = T if last else NT
            for n_, (t0_, w_) in enumerate(tgs_q):
                for kh in range(2):
                    load(HT[:, kh * 8:kh * 8 + 8, t0_:t0_ + w_],
                         H2[kh * 8:kh * 8 + 8, :, t0_:t0_ + w_].rearrange("k p t -> p k t"), HTb[n_])

            def ccol(t0):
                return (LAT0 + t0) if t0 < T else (CTX0 + t0 - T)

            tile_ctr = 0
            if b == 0 and l + 1 < nlayers:
                pre_pending[l + 1] = precast_steps(l + 1)
            for G in range(22):
                if b == 0 and pre_pending.get(l + 1):
                    pre_pending[l + 1].pop(0)()
                wt, wb = wload(w_up[l, G], SLOT)
                wv = wt[:].rearrange("p (j a k c) -> p j a k c", j=2, a=2, k=KC)
                for jj in range(2):
                    j = 2 * G + jj
                    i2 = j % 2
                    streams = ((0, CAt[i2], CAb[i2], j, 0), (1, CGt[i2], CGb[i2], NJ + j, 4))
                    pend = None

                    def taps(pd):
                        (n_, t0_, w_, bka_) = pd
                        for (ag, Ct, Cb, ch, bo) in streams:
                            c0 = ccol(t0_)
                            nb_ = [Cb[n_]]
                            if t0_ < T:
                                if n_ > 0:
                                    nb_.append(Cb[n_ - 1])
                                if n_ < 3:
                                    nb_.append(Cb[n_ + 1])
                            pt_ = psum[bo + bka_]
                            P.op("dve", ("stt", Ct[:, c0 + 1:c0 + 1 + w_], pt_[:, 0:w_], CONV[:, l, 0, ch:ch + 1],
                                         Ct[:, c0 + 1:c0 + 1 + w_], ALU.mult, ALU.add),
                                 R=[PS[bo + bka_], CONSTB] + nb_, W=nb_)
                            P.op("dve", ("stt", Ct[:, c0 - 1:c0 - 1 + w_], pt_[:, 0:w_], CONV[:, l, 2, ch:ch + 1],
                                         Ct[:, c0 - 1:c0 - 1 + w_], ALU.mult, ALU.add),
                                 R=[PS[bo + bka_], CONSTB] + nb_, W=nb_)

                    for n, (t0, w) in enumerate(tgs_q):
                        bka = tile_ctr % 4
                        tile_ctr += 1
                        for (ag, Ct, Cb, ch, bo) in streams:
                            for k in range(KC):
                                mm(psum[bo + bka][:, 0:w], wv[:, jj, ag, k, :], HT[:, k, t0:t0 + w], k == 0, k == KC - 1,
                                   R=(wb, HTb[n]), W=(PS[bo + bka],))
                        for (ag, Ct, Cb, ch, bo) in streams:
                            c0 = ccol(t0)
                            act(Ct[:, c0:c0 + w], psum[bo + bka][:, 0:w], AF.Identity, R=(PS[bo + bka], CONSTB), W=(Cb[n],),
                                bias=CONV[:, l, 3, ch:ch + 1], scale=CONV[:, l, 1, ch:ch + 1])
                        if pend is not None and t0 < T:
                            taps(pend)
                            pend = None
                        if pend is not None:
                            taps(pend)
                            pend = None
                        pend = (n, t0, w, bka)
                    taps(pend)
                    pend = None
                    allb_a = CAb[i2][:len(tgs_q)]
                    allb_g = CGb[i2][:len(tgs_q)]
                    segs = [(LAT0, 0, T)] + ([] if last else [(CTX0, T, CL)])
                    for (c0, o0, w) in segs:
                        act(CGt[i2][:, c0:c0 + w], CGt[i2][:, c0:c0 + w], AF.Silu, R=allb_g, W=allb_g)
                    for (c0, o0, w) in segs:
                        P.op("dve", ("tt", RO[i2][:, o0:o0 + w], CGt[i2][:, c0:c0 + w], CAt[i2][:, c0:c0 + w], ALU.mult),
                             R=allb_g + allb_a, W=(ROb[i2],))
                    store(RR[j][:, 0:wq], RO[i2][:, 0:wq], ROb[i2])
            end_phase()
            if check_stop(l, b, "P6"):
                break

            xdst = XA
            RTh = [A.alloc([22, 512], BF16) for _ in range(2)]
            Zt = [A.alloc([KC, 512], F32) for _ in range(2)]
            ZB2 = [A.alloc([512], BF16) for _ in range(2)]
            SQ2 = [A.alloc([512], BF16) for _ in range(2)]
            HO = A.alloc([KC, 512], BF16)
            STAT = A.alloc([2, 512], F32)
            RTb = [dbuf(f"rt{i}") for i in range(2)]
            Zb = [dbuf(f"z{i}") for i in range(2)]
            ZK = [[Buf(f"zk{i}_{k}") for k in range(KC)] for i in range(2)]
            ZB2b = [Buf(f"zb2{i}") for i in range(2)]
            SQ2b = [Buf(f"sq2{i}") for i in range(2)]
            HOb = dbuf("ho")
            STb = Buf("stat")
            tile_ctr = 0
            st_ctr = 0
            DQ = Deferred()
            ntg = len(tgs_q)
            halves = [(ti, jh) for ti in range(ntg) for jh in range(2)]
            want_h7 = (l + 1 < nlayers)

            def p7_load_half(idx_):
                ti_, jh_ = halves[idx_]
                t0_, w_ = tgs_q[ti_]
                bi = idx_ % 2
                for jq in range(2):
                    j0 = jh_ * 22 + jq * 11
                    load(RTh[bi][:, jq * 11:(jq + 1) * 11, 0:w_],
                         RR[j0:j0 + 11, :, t0_:t0_ + w_].rearrange("j p t -> p j t"), RTb[bi])

            def p7_load_z(ti_, src_):
                t0_, w_ = tgs_q[ti_]
                load(Zt[ti_ % 2][:, :, 0:w_], src_[:, :, t0_:t0_ + w_].rearrange("k p t -> p k t"), Zb[ti_ % 2])

            p7_load_half(0)
            p7_load_z(0, xsrc)
            if ntg > 1:
                p7_load_z(1, xsrc)
            for idx, (ti, jh) in enumerate(halves):
                t0, w = tgs_q[ti]
                i2 = ti % 2
                r = b if t0 < T else 2
                sp_ = 4 + 2 * (ti % 2)
                pm, pq = psum[sp_], psum[sp_ + 1]
                pmb, pqb = PS[sp_], PS[sp_ + 1]
                if idx + 1 < len(halves):
                    p7_load_half(idx + 1)
                for mp in range(8):
                    wt, wb = wload(WD16[l, jh, mp], 5632, extra_R=(PRE[l],))
                    wv = wt[:, 0:5632].rearrange("p (m j c) -> p m j c", m=2, j=22)
                    for mi in range(2):
                        m = mp * 2 + mi
                        bk = tile_ctr % 4
                        tile_ctr += 1
                        for j in range(22):
                            mm(psum[bk][:, 0:w], wv[:, mi, j, :], RTh[idx % 2][:, j, 0:w], j == 0, j == 21,
                               R=(wb, RTb[idx % 2]), W=(PS[bk],))
                        P.op("dve", ("stt", Zt[i2][:, m, 0:w], psum[bk][:, 0:w], mod_ap(l, 5, m, r),
                                     Zt[i2][:, m, 0:w], ALU.mult, ALU.add),
                             R=(PS[bk], Zb[i2], ZK[i2][m], GLOB), W=(ZK[i2][m],))
                        if jh == 1:
                            si = st_ctr % 2
                            st_ctr += 1
                            act(ZB2[si][:, 0:w], Zt[i2][:, m, 0:w], AF.Copy, R=(ZK[i2][m],), W=(ZB2b[si],))
                            act(SQ2[si][:, 0:w], Zt[i2][:, m, 0:w], AF.Square, R=(ZK[i2][m],), W=(SQ2b[si],))

                            def stat_mm(m=m, si=si, w=w, pm=pm, pq=pq, pmb=pmb, pqb=pqb):
                                mm(pm[:, 0:w], ONESD, ZB2[si][:, 0:w], m == 0, m == KC - 1, R=(ZB2b[si], CONSTB), W=(pmb,))
                                mm(pq[:, 0:w], ONESD, SQ2[si][:, 0:w], m == 0, m == KC - 1, R=(SQ2b[si], CONSTB), W=(pqb,))

                            DQ.tick()
                            DQ.add(1, stat_mm)
                        else:
                            DQ.tick()
                if jh == 1:
                    def ln_tail(ti=ti, t0=t0, w=w, i2=i2, r=r, pm=pm, pq=pq, pmb=pmb, pqb=pqb, src_=xsrc):
                        Z = Zt[i2]
                        zk = ZK[i2]
                        MEAN = STAT[:, 0, 0:w]
                        RSTD = STAT[:, 1, 0:w]

                        def d0():
                            act(MEAN, pm[:, 0:w], AF.Copy, R=(pmb,), W=(STb,))
                            P.op("dve", ("tt", RSTD, MEAN, MEAN, ALU.mult), R=(STb,), W=(STb,))
                            P.op("dve", ("tt", pq[:, 0:w], pq[:, 0:w], RSTD, ALU.subtract), R=(STb, pqb), W=(pqb,))
                            act(pq[:, 0:w], pq[:, 0:w], AF.Sqrt, R=(pqb, GLOB), W=(pqb,), bias=EPS[:, 0:1])
                            P.op("dve", ("recip", pq[:, 0:w], pq[:, 0:w]), R=(pqb,), W=(pqb,))

                        def dk(k):
                            P.op("dve", ("tt", Z[:, k, 0:w], Z[:, k, 0:w], pm[:, 0:w], ALU.subtract), R=(zk[k], pmb), W=(zk[k],))
                            P.op("dve", ("tt", Z[:, k, 0:w], Z[:, k, 0:w], pq[:, 0:w], ALU.mult), R=(zk[k], pqb), W=(zk[k],))
                            if want_h7:
                                act(HO[:, k, 0:w], Z[:, k, 0:w], AF.Identity, R=(zk[k], GLOB), W=(HOb,),
                                    bias=FB[:, l, 1, k, r:r + 1], scale=FS[:, l, 1, k, r:r + 1])
                            act(Z[:, k, 0:w], Z[:, k, 0:w], AF.Identity, R=(zk[k], CONSTB), W=(zk[k],),
                                bias=LNP[:, l, 1, 1, k:k + 1], scale=LNP[:, l, 1, 0, k:k + 1])
                            if k == KC // 2 - 1:
                                dst_ = outT[b] if last else xdst
                                P.op("sp", ("dma", dst_[0:8, :, t0:t0 + w].rearrange("k p t -> p k t"), Z[:, 0:8, 0:w]),
                                     R=[Zb[i2]] + zk[0:8], W=(), dma=Zb[i2])
                            if k == KC - 1:
                                dst_ = outT[b] if last else xdst
                                P.op("sp", ("dma", dst_[8:16, :, t0:t0 + w].rearrange("k p t -> p k t"), Z[:, 8:16, 0:w]),
                                     R=[Zb[i2]] + zk[8:16], W=(), dma=Zb[i2])
                                if (not last) and want_h7:
                                    store(H1[:, :, t0:t0 + w].rearrange("k p t -> p k t"), HO[:, :, 0:w], HOb)
                                if ti + 2 < ntg:
                                    p7_load_z(ti + 2, src_)

                        DQ.add(3, d0)
                        for k in range(KC):
                            DQ.add(4 + k, lambda k=k: dk(k))

                    ln_tail()
            DQ.flush()
            end_phase()
            xsrc = xdst
            if check_stop(l, b, "P7"):
                break

    P.op("sp", ("nop",))
    P.op("act", ("nop",))

    P.finalize(nc, engsem)
    assert _deadlock_check(P), 'deadlock in sync plan'
    with sem_stack, nc.Block() as block:
        @block.tensor
        def _(e):
            P.emit("pe", e, engsem)

        @block.scalar
        def _(e):
            P.emit("act", e, engsem)

        @block.vector
        def _(e):
            P.emit("dve", e, engsem)

        @block.gpsimd
        def _(e):
            P.emit("pool", e, engsem)

        @block.sync
        def _(e):
            P.emit("sp", e, engsem)
    return nc, len(P.ops)


def _const_tables():
    bf = ml_dtypes.bfloat16
    cb = np.zeros((128, 2176), np.float32)
    cb[:, 0:128] = np.eye(128)
    cb[:, 128:256] = 1.0
    cb[:, 256:384] = 1.0 / D
    ps = np.zeros((128, 128), np.float32)
    for i in range(32):
        ps[32 + i, i] = -1.0
        ps[i, 32 + i] = 1.0
        ps[96 + i, 64 + i] = -1.0
        ps[64 + i, 96 + i] = 1.0
    cb[:, 384:512] = ps
    kk = np.arange(128)[:, None]
    qq = np.arange(128)[None, :]
    lo = np.where(kk >= qq, 0.0, NEG).astype(np.float32)
    up = np.where(kk <= qq, 0.0, NEG).astype(np.float32)
    cb[:, 512:896] = np.tile(lo, (1, 3))
    cb[:, 896:1280] = np.tile(up, (1, 3))
    c = np.arange(128, dtype=np.float64)
    ang = 2 * np.pi * np.outer(c, c) / 128.0
    cb[:, 1280:1408] = np.cos(ang) / np.sqrt(128.0)
    cb[:, 1408:1536] = -np.sin(ang) / np.sqrt(128.0)
    cbf = cb.astype(bf)
    inv = (10000.0 ** (-np.arange(0, 64, 2, dtype=np.float32) / np.float32(64))).astype(np.float32)
    row = np.repeat(np.arange(T // 64, dtype=np.float32), 64)
    col = np.tile(np.arange(64, dtype=np.float32), T // 64)
    ar = (row[None, :] * inv[:, None]).astype(np.float32)
    ac = (col[None, :] * inv[:, None]).astype(np.float32)
    angp = np.concatenate([ar, ar, ac, ac], axis=0)
    rope = np.stack([np.cos(angp), np.sin(angp)], axis=1).astype(np.float32)
    t = np.arange(T, dtype=np.float64)
    full = 2 * np.pi * ((np.outer(t, t)) % T) / T
    Cm = (np.cos(full) / np.sqrt(T)).astype(np.float32)
    Sm = (np.sin(full) / np.sqrt(T)).astype(np.float32)
    dft = np.zeros((8, 128, KC, 2, 256), np.float32)
    for G in range(8):
        cs = Cm[:, G * 256:(G + 1) * 256].reshape(KC, 128, 256).transpose(1, 0, 2)
        ss = Sm[:, G * 256:(G + 1) * 256].reshape(KC, 128, 256).transpose(1, 0, 2)
        dft[G, :, :, 0, :] = cs
        dft[G, :, :, 1, :] = ss
    dft = dft.reshape(8, 128, SLOT).astype(bf)
    tc = np.arange(CL, dtype=np.float64)
    fc = 2 * np.pi * (np.outer(tc, tc) % CL) / CL
    Cc = (np.cos(fc) / np.sqrt(CL)).astype(np.float32).reshape(2, 128, 256).transpose(1, 0, 2)
    Sc = (np.sin(fc) / np.sqrt(CL)).astype(np.float32).reshape(2, 128, 256).transpose(1, 0, 2)
    dftc = np.stack([Cc, Sc], axis=2).reshape(128, 1024).astype(bf)
    return cbf, rope, dft, dftc


def _layout_weights(w_ada, b_ada, w_in, sink, w_four, w_out, ln_g, ln_b, w_up, conv_w, conv_b, w_down, nl):
    f = np.float32
    res = {}
    wa = np.asarray(w_ada[:nl], f).reshape(nl, KC, 128, 24, 4, 128)
    res["w_ada_t"] = np.ascontiguousarray(wa.transpose(0, 3, 2, 4, 1, 5)).reshape(nl, 24, 128, SLOT)
    res["b_ada_t"] = np.ascontiguousarray(np.asarray(b_ada[:nl], f).reshape(nl, 96, 128).transpose(2, 0, 1))
    wi = np.asarray(w_in[:nl], f)
    cols = []
    cols.append(np.arange(2560, 3072))
    cols.append(np.arange(0, 512))
    for h in range(NKV):
        cc = [np.arange(2048 + h * 128, 2048 + (h + 1) * 128)]
        for g in range(3):
            hq = 3 * h + g
            cc.append(np.arange(512 + hq * 128, 512 + (hq + 1) * 128))
        cols.append(np.concatenate(cc))
    wit = np.empty((nl, 6, 128, KC, 512), f)
    for G in range(6):
        wit[:, G] = wi[:, :, cols[G]].reshape(nl, KC, 128, 512).transpose(0, 2, 1, 3)
    res["w_in_t"] = wit.reshape(nl, 6, 128, SLOT)
    res["w_four_t"] = np.ascontiguousarray(
        np.asarray(w_four[:nl], f).reshape(nl, 4, 128, 512).transpose(0, 2, 1, 3)).reshape(nl, 128, 2048)
    wo = np.asarray(w_out[:nl], f).reshape(nl, KC, 128, 4, 512)
    res["w_out_t"] = np.ascontiguousarray(wo.transpose(0, 3, 2, 1, 4)).reshape(nl, 4, 128, SLOT)
    wu = np.asarray(w_up[:nl], f).reshape(nl, KC, 128, 2, 22, 2, 128)
    res["w_up_t"] = np.ascontiguousarray(wu.transpose(0, 4, 2, 5, 3, 1, 6)).reshape(nl, 22, 128, SLOT)
    wd = np.asarray(w_down[:nl], f).reshape(nl, NJ, 128, KC, 128)
    res["w_down_t"] = np.ascontiguousarray(wd.transpose(0, 3, 2, 1, 4)).reshape(nl, KC, 128, NJ * 128)
    cv = np.concatenate([np.asarray(conv_w[:nl], f), np.asarray(conv_b[:nl], f)[:, None, :]], axis=1)
    res["conv_t"] = np.ascontiguousarray(cv.reshape(nl, 4, 88, 128).transpose(3, 0, 1, 2))
    lg = np.stack([np.asarray(ln_g[:nl], f), np.asarray(ln_b[:nl], f)], axis=2)
    res["ln_t"] = np.ascontiguousarray(lg.reshape(nl, 2, 2, KC, 128).transpose(4, 0, 1, 2, 3))
    res["sink_t"] = np.ascontiguousarray(np.broadcast_to(np.asarray(sink[:nl], f).reshape(1, nl * NQ), (128, nl * NQ)))
    return res


def _core_inputs(x, c, ctx, c_ctx, b0, nbc):
    xs = np.asarray(x[b0:b0 + nbc], np.float32)
    cs = np.asarray(ctx[b0:b0 + nbc], np.float32)
    xa = np.concatenate([xs, cs], axis=1)
    x0T = np.ascontiguousarray(xa.reshape(nbc, NT, KC, 128).transpose(0, 2, 3, 1))
    rows = [np.asarray(c[b0 + i], np.float32) for i in range(nbc)]
    while len(rows) < 2:
        rows.append(rows[-1])
    rows.append(np.asarray(c_ctx, np.float32))
    cc = np.stack(rows, axis=0)
    cT = np.ascontiguousarray(cc.reshape(3, KC, 128).transpose(2, 1, 0))
    return x0T, cT


_CACHE = {}


def kernel(x, c, ctx, c_ctx, w_ada, b_ada, w_in, sink, w_four, w_out, ln_g, ln_b,
           w_up, conv_w, conv_b, w_down):
    ncores = 8
    nbc = 2
    if "nc" not in _CACHE:
        _CACHE["nc"] = build_nc(L_ALL, nbc)[0]
        _CACHE["consts"] = _const_tables()
    nc = _CACHE["nc"]
    cbf, rope, dft, dftc = _CACHE["consts"]
    wl = _layout_weights(w_ada, b_ada, w_in, sink, w_four, w_out, ln_g, ln_b, w_up, conv_w, conv_b, w_down, L_ALL)
    in_maps = []
    for ci in range(ncores):
        x0T, cT = _core_inputs(x, c, ctx, c_ctx, ci * nbc, nbc)
        m = dict(wl)
        m.update({"x0T": x0T, "cT": cT, "rope_t": rope, "cbf_t": cbf, "dft_t": dft, "dftc_t": dftc})
        in_maps.append(m)
    res = run_bass_kernel_spmd(nc, in_maps, core_ids=list(range(ncores)))
    out = np.empty((ncores * nbc, T, D), np.float32)
    for ci in range(ncores):
        o = res.results[ci]["outT"]
        out[ci * nbc:(ci + 1) * nbc] = o.transpose(0, 3, 1, 2).reshape(nbc, T, D)
    return out


def _deadlock_check(P):
    engs = {}
    for o in P.ops:
        engs.setdefault(o.eng, []).append(o)
    pos = {e: 0 for e in engs}
    sem = {}
    dsem = {}
    progress = True
    while progress:
        progress = False
        for e, lst in engs.items():
            while pos[e] < len(lst):
                o = lst[pos[e]]
                ok = all(sem.get(p.eng, 0) >= p.ms for p in o.cdeps) and \
                    all(dsem.get(ds.id, 0) >= val for (ds, val) in o.ddeps)
                if not ok:
                    break
                if o.ds is not None:
                    dsem[o.ds.id] = dsem.get(o.ds.id, 0) + 16
                elif o.need:
                    sem[e] = sem.get(e, 0) + 1
                    assert sem[e] == o.ms, (e, sem[e], o.ms)
                pos[e] += 1
                progress = True
    stuck = {e: (pos[e], len(lst)) for e, lst in engs.items() if pos[e] < len(lst)}
    for e, (p_, n_) in stuck.items():
        o = engs[e][p_]
        print("STUCK", e, p_, n_, o.ins[0], [(p.eng, p.ms, sem.get(p.eng, 0)) for p in o.cdeps],
              [(ds.id, val, dsem.get(ds.id, 0)) for (ds, val) in o.ddeps])
    return not stuck
```
